# Optimizing a Trainium2 kernel written in Bass

```python
import jax, jax.numpy as jnp
from jax import lax
import numpy as np

D_MODEL = 1024
BATCH = 8
SEQ = 2048
DEPTH = 1
DEC_BATCH = 128
DEC_SEQ = 4
PAST_LEN = 16384
PAGE_SIZE = 128

N_META = 16
MIX_W = D_MODEL
POOL_W = MIX_W // 2
CONV_W = MIX_W - POOL_W
POOL_WINDOWS = (2, 4, 8, 16)
N_POOL_GROUPS = len(POOL_WINDOWS)
POOL_GC = POOL_W // N_POOL_GROUPS
POOL_BUF = max(POOL_WINDOWS) - 1
CONV_K = 3
D_FF = 2816
EPS = 1e-6

kernel_name = "hybrid_pool_shortconv_convffn_step"


def _rmsnorm(x, g):
    xf = x.astype(jnp.float32)
    y = xf * lax.rsqrt(jnp.mean(xf * xf, axis=-1, keepdims=True) + EPS)
    return (y * g.astype(jnp.float32)).astype(x.dtype)


def _causal_dwconv(z, prev, w):
    T = z.shape[1]
    ext = jnp.concatenate([prev.astype(z.dtype), z], axis=1)
    y = ext[:, 0:T] * w[0]
    for k in range(1, CONV_K):
        y = y + ext[:, k:k + T] * w[k]
    return y, ext[:, -(CONV_K - 1):]


def _multi_scale_pool(u, prev, start_pos):
    T = u.shape[1]
    ext = jnp.concatenate([prev.astype(u.dtype), u], axis=1)
    extf = ext.astype(jnp.float32)
    cs = jnp.concatenate([jnp.zeros_like(extf[:, :1]), jnp.cumsum(extf, axis=1)], axis=1)
    end = cs[:, POOL_BUF + 1:POOL_BUF + 1 + T]
    pos = start_pos + jnp.arange(T)
    outs = []
    for g, w in enumerate(POOL_WINDOWS):
        sl = slice(g * POOL_GC, (g + 1) * POOL_GC)
        begin = cs[:, POOL_BUF + 1 - w:POOL_BUF + 1 - w + T, sl]
        cnt = jnp.minimum(pos + 1, w).astype(jnp.float32)[None, :, None]
        outs.append((end[..., sl] - begin) / cnt)
    pooled = jnp.concatenate(outs, axis=-1) - u.astype(jnp.float32)
    return pooled.astype(u.dtype), ext[:, -POOL_BUF:]


def _layer(x, prev_pool, prev_conv, prev_ffn, start_pos,
           norm_mix, w_in, w_pool, pool_scale, conv_w, w_out,
           norm_ffn, w_up, ffn_conv_w, ffn_conv_b, w_down):
    Bn, T, _ = x.shape
    h = _rmsnorm(x, norm_mix)
    proj = h @ w_in
    u = proj[..., :POOL_W]
    gb = proj[..., POOL_W:POOL_W + CONV_W]
    gc = proj[..., POOL_W + CONV_W:POOL_W + 2 * CONV_W]
    v = proj[..., POOL_W + 2 * CONV_W:]
    d, new_pool = _multi_scale_pool(u, prev_pool, start_pos)
    d = d.reshape(Bn, T, N_POOL_GROUPS, POOL_GC)
    a = jnp.einsum('btgc,gcd->btgd', d, w_pool).reshape(Bn, T, POOL_W) * pool_scale
    conv, new_conv = _causal_dwconv(gc * v, prev_conv, conv_w)
    b = gb * conv
    x = x + jnp.concatenate([a, b], axis=-1) @ w_out
    h = _rmsnorm(x, norm_ffn)
    up = h @ w_up
    upc, new_ffn = _causal_dwconv(up, prev_ffn, ffn_conv_w)
    upc = upc + ffn_conv_b
    val, gate = upc[..., :D_FF], upc[..., D_FF:]
    x = x + (jax.nn.silu(gate) * val) @ w_down
    return x, new_pool, new_conv, new_ffn


def setup_inputs(seed: int = 0) -> dict:
    key = jax.random.key(seed)
    ks = jax.random.split(key, 20)
    n = jax.random.normal
    f32 = jnp.float32
    return {
        "x_prompt": n(ks[0], (BATCH, SEQ, D_MODEL), f32),
        "x_sample": n(ks[1], (DEC_BATCH, DEC_SEQ, D_MODEL), f32),
        "state_pool": n(ks[2], (DEPTH, DEC_BATCH, POOL_BUF, POOL_W), f32),
        "state_conv": n(ks[3], (DEPTH, DEC_BATCH, CONV_K - 1, CONV_W), f32),
        "state_ffn": n(ks[4], (DEPTH, DEC_BATCH, CONV_K - 1, 2 * D_FF), f32),
        "meta_tokens": n(ks[5], (N_META, D_MODEL), f32),
        "norm_mix": 1.0 + 0.05 * n(ks[6], (DEPTH, D_MODEL), f32),
        "w_in": n(ks[7], (DEPTH, D_MODEL, POOL_W + 3 * CONV_W), f32) * D_MODEL ** -0.5,
        "w_pool": n(ks[8], (DEPTH, N_POOL_GROUPS, POOL_GC, POOL_GC), f32) * POOL_GC ** -0.5,
        "pool_scale": 1.0 + 0.1 * n(ks[9], (DEPTH, POOL_W), f32),
        "conv_w": n(ks[10], (DEPTH, CONV_K, CONV_W), f32) * CONV_K ** -0.5,
        "w_out": n(ks[11], (DEPTH, MIX_W, D_MODEL), f32) * MIX_W ** -0.5,
        "norm_ffn": 1.0 + 0.05 * n(ks[12], (DEPTH, D_MODEL), f32),
        "w_up": n(ks[13], (DEPTH, D_MODEL, 2 * D_FF), f32) * D_MODEL ** -0.5,
        "ffn_conv_w": n(ks[14], (DEPTH, CONV_K, 2 * D_FF), f32) * CONV_K ** -0.5,
        "ffn_conv_b": 0.02 * n(ks[15], (DEPTH, 2 * D_FF), f32),
        "w_down": n(ks[16], (DEPTH, D_FF, D_MODEL), f32) * D_FF ** -0.5,
        "norm_final": 1.0 + 0.05 * n(ks[17], (D_MODEL,), f32),
    }


def reference(x_prompt, x_sample, state_pool, state_conv, state_ffn, meta_tokens,
              norm_mix, w_in, w_pool, pool_scale, conv_w, w_out,
              norm_ffn, w_up, ffn_conv_w, ffn_conv_b, w_down, norm_final):
    Bp = x_prompt.shape[0]
    meta = jnp.broadcast_to(meta_tokens.astype(x_prompt.dtype)[None], (Bp, N_META, D_MODEL))
    xp = jnp.concatenate([meta, x_prompt], axis=1)
    xs = x_sample
    zp_pool = jnp.zeros((Bp, POOL_BUF, POOL_W), xp.dtype)
    zp_conv = jnp.zeros((Bp, CONV_K - 1, CONV_W), xp.dtype)
    zp_ffn = jnp.zeros((Bp, CONV_K - 1, 2 * D_FF), xp.dtype)
    pp, pc, pf, sp, sc, sf = [], [], [], [], [], []
    for l in range(DEPTH):
        params = (norm_mix[l], w_in[l], w_pool[l], pool_scale[l], conv_w[l], w_out[l],
                  norm_ffn[l], w_up[l], ffn_conv_w[l], ffn_conv_b[l], w_down[l])
        xp, a1, a2, a3 = _layer(xp, zp_pool, zp_conv, zp_ffn, 0, *params)
        xs, b1, b2, b3 = _layer(xs, state_pool[l], state_conv[l], state_ffn[l], PAST_LEN, *params)
        pp.append(a1); pc.append(a2); pf.append(a3)
        sp.append(b1); sc.append(b2); sf.append(b3)
    y_prompt = _rmsnorm(xp[:, N_META:], norm_final)
    y_sample = _rmsnorm(xs, norm_final)
    return (y_prompt, y_sample,
            jnp.stack(pp), jnp.stack(pc), jnp.stack(pf),
            jnp.stack(sp), jnp.stack(sc), jnp.stack(sf))
```

```python
from contextlib import ExitStack

import numpy as np
import concourse.bass as bass
import concourse.mybir as mybir
from concourse.bass_utils import run_bass_kernel_spmd

F32 = mybir.dt.float32
BF16 = mybir.dt.bfloat16
AF = mybir.ActivationFunctionType
ALU = mybir.AluOpType

D = 1024
NCORES = 8
SEQ = 2048
NMETA = 16
NSEQ = 16
DSEQ = 4
NPAD = 4
SOFF = NMETA + NPAD
NSM = SOFF + NSEQ * DSEQ
NQ = NSM // 4
DFF = 2816
NFF = 22
EPS = 1e-6
NST = 4
TW = 512
WINDOWS = (2, 4, 8, 16)
NW = 4
DEBUG = False
WARM_S2, WARM_S3A, WARM_S3B, WARM_S1 = 6, 12, 6, 6

C_G1, C_G2, C_PS, C_CW, C_FW, C_FB, NCST = 0, 8, 16, 20, 32, 164, 208

ENGS = ("pe", "act", "dve", "pool", "sp")
SAME_ENG_DIST = 1 << 30


class Op:
    __slots__ = ("eng", "fn", "deps", "dsem", "idx", "pos", "sig", "signal")

    def __init__(self, eng, fn, deps, dsem, idx):
        self.eng = eng
        self.fn = fn
        self.deps = deps
        self.dsem = dsem
        self.idx = idx
        self.pos = None
        self.sig = None
        self.signal = False


class Prog:
    def __init__(self):
        self.ops = []
        self.last_writer = {}
        self.readers = {}

    def op(self, eng, fn, reads=(), writes=(), dsem=None):
        deps = set()
        for k in reads:
            w = self.last_writer.get(k)
            if w is not None:
                deps.add(w)
        for k in writes:
            w = self.last_writer.get(k)
            if w is not None:
                deps.add(w)
            deps.update(self.readers.get(k, ()))
        idx = len(self.ops)
        self.ops.append(Op(eng, fn, deps, dsem, idx))
        for k in reads:
            self.readers.setdefault(k, []).append(idx)
        for k in writes:
            self.last_writer[k] = idx
            self.readers[k] = []
        return idx

    def emit(self, nc, final_wait_ops=()):
        ops = self.ops
        streams = {e: [] for e in ENGS}
        for o in ops:
            o.pos = len(streams[o.eng])
            streams[o.eng].append(o)
        for o in ops:
            for d in o.deps:
                p = ops[d]
                if p.dsem is not None:
                    p.signal = True
                elif p.eng != o.eng:
                    p.signal = True
                elif o.eng != "pe" and o.pos - p.pos <= SAME_ENG_DIST:
                    p.signal = True
        for d in final_wait_ops:
            ops[d].signal = True
        cnt = {}
        dsems = []
        for o in ops:
            if o.dsem is not None:
                if o.dsem not in cnt:
                    dsems.append(o.dsem)
                cnt[o.dsem] = cnt.get(o.dsem, 0) + 16
                o.sig = (o.dsem, cnt[o.dsem])
            elif o.signal:
                k = "E_" + o.eng
                cnt[k] = cnt.get(k, 0) + 1
                o.sig = (k, cnt[k])
        with ExitStack() as es:
            sems = {}
            for k in ["E_" + e for e in ENGS] + dsems:
                sems[k] = es.enter_context(nc.semaphore(k))
            block = es.enter_context(nc.Block())

            def run_stream(ename, eng):
                waited = {}
                for o in streams[ename]:
                    need = {}
                    for d in o.deps:
                        p = ops[d]
                        if p.sig is None:
                            continue
                        if p.dsem is None and p.eng == o.eng and (o.eng == "pe" or o.pos - p.pos > SAME_ENG_DIST):
                            continue
                        s, v = p.sig
                        if need.get(s, 0) < v:
                            need[s] = v
                    for s, v in need.items():
                        if waited.get(s, 0) < v:
                            eng.wait_ge(sems[s], v)
                            waited[s] = v
                    ins = o.fn(eng)
                    if o.sig is not None:
                        ins.then_inc(sems[o.sig[0]], 16 if o.dsem is not None else 1)
                if ename == "sp":
                    for d in final_wait_ops:
                        s, v = ops[d].sig
                        if waited.get(s, 0) < v:
                            eng.wait_ge(sems[s], v)
                            waited[s] = v

            @block.tensor
            def _(e):
                run_stream("pe", e)

            @block.scalar
            def _(e):
                run_stream("act", e)

            @block.vector
            def _(e):
                run_stream("dve", e)

            @block.gpsimd
            def _(e):
                run_stream("pool", e)

            @block.sync
            def _(e):
                run_stream("sp", e)


def build_nc():
    nc = bass.Bass("TRN2", target_bir_lowering=False)

    def din(name, shape):
        return nc.dram_tensor(name, shape, F32, kind="ExternalInput").ap()

    def dout(name, shape):
        return nc.dram_tensor(name, shape, F32, kind="ExternalOutput").ap()

    xp = din("xp", [SEQ, D])
    xsm = din("xsm", [NSM, D])
    w_in = din("w_in", [D, 2048])
    w_up = din("w_up", [D, 2 * DFF])
    w_out = din("w_out", [D, D])
    w_down = din("w_down", [DFF, D])
    w_pool = din("w_pool", [512, 128])
    cst = din("cst", [128, NCST])
    gfin = din("gfin", [128, D])
    st_pool = din("st_pool", [128, 4 * NSEQ * 15])
    st_conv = din("st_conv", [128, 4 * NSEQ * 2])
    st_ffn = din("st_ffn", [128, 44 * NSEQ * 2])

    yp = dout("yp", [SEQ, D])
    ys = dout("ys", [NSEQ * DSEQ, D])
    o_pool_p = dout("o_pool_p", [128, 4 * 15])
    o_conv_p = dout("o_conv_p", [128, 4 * 2])
    o_ffn_p = dout("o_ffn_p", [128, 44 * 2])
    o_pool_s = dout("o_pool_s", [128, 4 * NSEQ * 15])
    o_conv_s = dout("o_conv_s", [128, 4 * NSEQ * 2])
    o_ffn_s = dout("o_ffn_s", [128, 44 * NSEQ * 2])

    P = Prog()
    finals = []
    if DEBUG:
        dbg_aT = nc.dram_tensor("dbg_aT", [128, 8 * TW], BF16, kind="ExternalOutput").ap()
        dbg_h2T = nc.dram_tensor("dbg_h2T", [128, 8 * TW], BF16, kind="ExternalOutput").ap()
        dbg_hT = nc.dram_tensor("dbg_hT", [128, 8 * TW], BF16, kind="ExternalOutput").ap()
        dbg_x1 = nc.dram_tensor("dbg_x1", [128, 4 * D], F32, kind="ExternalOutput").ap()
        dbg_actT = nc.dram_tensor("dbg_actT", [128, NFF * TW], BF16, kind="ExternalOutput").ap()

    with ExitStack() as es:
        def sb(name, shape, dt=F32):
            return es.enter_context(nc.sbuf_tensor(name, shape, dt))

        def psum(name, shape, dt=F32):
            return es.enter_context(nc.psum_tensor(name, shape, dt))

        wslot = [sb(f"wslot{i}", [128, 8, 256], BF16) for i in range(NW)]
        w_out_sb = sb("w_out_sb", [128, 8, D], BF16)
        w_down_sb = sb("w_down_sb", [128, NFF, D], BF16)
        w_pool_sb = sb("w_pool_sb", [128, 4, 128], BF16)
        cst_sb = sb("cst_sb", [128, NCST])
        gfin_sb = sb("gfin_sb", [128, D])
        ident = sb("ident", [128, 128], BF16)
        epst = sb("epst", [128, 1])
        invcnt = sb("invcnt", [128, 4, 16])
        x1 = sb("x1", [128, 4, D])
        xrot = [sb(f"xrot{i}", [128, D]) for i in range(2)]
        hbuf = [sb(f"hbuf{i}", [128, D], BF16) for i in range(2)]
        hb0 = sb("hb0", [128, D], BF16)
        hT = sb("hT", [128, 8, TW], BF16)
        aT = sb("aT", [128, 8, TW], BF16)
        actT = sb("actT", [128, NFF, TW], BF16)
        xs_sb = sb("xs_sb", [128, D])
        hs = sb("hs", [128, D], BF16)
        hTs = sb("hTs", [128, 8, NSM], BF16)
        aTs = sb("aTs", [128, 8, NSM], BF16)
        actTs = sb("actTs", [128, NFF, NSM], BF16)
        ss_all = sb("ss_all", [128, 64])
        rs_all = sb("rs_all", [128, 64])
        Dsm = [sb(f"Dsm{i}", [128, NSM], BF16) for i in range(4)]
        Usave = sb("Usave", [128, 4, 15])
        CVsave = sb("CVsave", [128, 4, 2])
        Esave = sb("Esave", [128, 44, 2])
        OFp = sb("OFp", [128, 44, 2])
        stp_sb = sb("stp_sb", [128, 4, NSEQ, 15])
        stc_sb = sb("stc_sb", [128, 4, NSEQ, 2])
        stf_sb = sb("stf_sb", [128, 44, NSEQ, 2])
        Um = [sb(f"Um{i}", [128, 31]) for i in range(2)]
        Us = [sb(f"Us{i}", [128, NSEQ, 19]) for i in range(2)]
        Am = [sb(f"Am{i}", [128, 31]) for i in range(2)]
        As = [sb(f"As{i}", [128, NSEQ, 19]) for i in range(2)]
        GCs = sb("GCs", [128, NSM])
        Xcv = [sb(f"Xcv{i}", [128, NQ, 6]) for i in range(2)]
        Tcs = [sb(f"Tcs{i}", [128, NQ, 4]) for i in range(2)]
        X3p = [sb(f"X3p{i}", [128, 2, NQ, 6]) for i in range(2)]
        T3p = [sb(f"T3p{i}", [128, 2, NQ, 4]) for i in range(2)]
        stfV = stf_sb[:].rearrange("p (h c) b t -> p h c b t", h=2)
        ARENA = 2 * 527 + 2 * 527 + 512 + 2 * 514 + 2 * 512 + 4 * 256
        arena = sb("arena", [128, ARENA])
        off = 0

        def carve(n):
            nonlocal off
            v = arena[:, off:off + n]
            off += n
            return v
        U = [carve(527) for _ in range(2)]
        AB = [carve(527) for _ in range(2)]
        GC = [carve(512)]
        Dbuf = [carve(256).bitcast(BF16) for _ in range(4)]
        CV = [carve(514) for _ in range(2)]
        TT = [carve(512) for _ in range(2)]
        assert off == ARENA
        NT1P = 5
        T1p = [arena[:, i * 1028:(i + 1) * 1028].rearrange("p (h n) -> p h n", h=2) for i in range(NT1P)]
        assert NT1P * 1028 <= ARENA
        EsV = Esave[:].rearrange("p (h c) t -> p h c t", h=2)

        G = [psum(f"G{i}", [128, TW]) for i in range(6)]
        TR = [psum(f"TR{i}", [128, 8, 128], BF16) for i in range(2)]
        NG = len(G)
        gctr = [0]
        trctr = [0]
        nctr = [0]

        gfree = list(range(NG))
        gref = {}

        def galloc(nref=1):
            assert gfree, "out of PSUM banks (program-order allocation)"
            i = gfree.pop(0)
            gref[i] = nref
            return i

        def gdone(i):
            gref[i] -= 1
            assert gref[i] >= 0
            if gref[i] == 0:
                gfree.append(i)

        def gk(g):
            return ("G", g)

        def gt(g):
            return ("Gt", g)

        def sub(g, k):
            return G[g][:, k * NSM:(k + 1) * NSM]

        S1T = ["arS1_dve", "arS1_pe", "arS1_pool"]
        S3T = ["arS3_act", "arS3_pool"]

        def cc(col):
            return cst_sb[:, col:col + 1]

        P.op("sp", lambda e: e.dma_start(out=cst_sb[:], in_=cst[:, :]), writes=["cst"], dsem="D_cst")
        P.op("sp", lambda e: e.dma_start(out=xs_sb[0:NSM, :], in_=xsm[:, :]), writes=["xs"], dsem="D_xs")

        P.op("pool", lambda e: e.memset(x1[:, 0, 0:128], 0.0), writes=[("x1", 0)])
        P.op("pool", lambda e: e.affine_select(out=x1[:, 0, 0:128], in_=x1[:, 0, 0:128], pattern=[[-1, 128]],
                                               compare_op=ALU.not_equal, fill=1.0, base=0, channel_multiplier=1),
             reads=[("x1", 0)], writes=[("x1", 0)])
        P.op("pool", lambda e: e.tensor_copy(out=ident[:], in_=x1[:, 0, 0:128]), reads=[("x1", 0)], writes=["ident"])

        def mk_consts(e):
            e.memset(epst[:], EPS)
            for g, w in enumerate(WINDOWS):
                e.memset(invcnt[:, g, w - 1:16], 1.0 / w)
                for t in range(w - 1):
                    e.memset(invcnt[:, g, t:t + 1], 1.0 / (t + 1))
            for i in range(2):
                e.memset(Um[i][:, 0:15], 0.0)
                e.memset(Xcv[i][:, 0, 0:2], 0.0)
            for i in range(4):
                e.memset(Dsm[i][:, NMETA:SOFF], 0.0)
            for i in range(2):
                ins = e.memset(X3p[i][:, :, 0, 0:2], 0.0)
            return ins
        P.op("pool", mk_consts, writes=["epst", "invcnt", "zeros"])
        P.op("sp", lambda e: e.dma_start(out=stp_sb[:].rearrange("p a b c -> p (a b c)"), in_=st_pool[:, :]),
             writes=[("stp", j) for j in range(4)], dsem="D_stp")
        P.op("sp", lambda e: e.dma_start(out=stc_sb[:].rearrange("p a b c -> p (a b c)"), in_=st_conv[:, :]),
             writes=[("stc", j) for j in range(4)], dsem="D_stc")
        P.op("sp", lambda e: e.dma_start(out=stf_sb[:].rearrange("p a b c -> p (a b c)"), in_=st_ffn[:, :]),
             writes=[("stf", c) for c in range(44)], dsem="D_stf")
        P.op("sp", lambda e: e.dma_start(out=gfin_sb[:], in_=gfin[:, :]), writes=["gfin"], dsem="D_gfin")

        S1_COL = {"u": 0, "gb": 512, "gc": 1024, "v": 1536}
        S1_PAIRS_G = ((("gc", 0), ("v", 0)), (("gb", 0), ("gc", 1)), (("u", 3), ("u", 2)), (("v", 1), ("gb", 1)),
                      (("gc", 2), ("v", 2)), (("u", 1), ("u", 0)), (("gb", 2), ("gc", 3)), (("v", 3), ("gb", 3)))
        wplan = []
        for s in range(NST):
            for pair in S1_PAIRS_G:
                (a, b) = [S1_COL[kind] + 128 * j for (kind, j) in pair]
                wplan.append([(0, w_in[:, a:a + 128]), (128, w_in[:, b:b + 128])])
            for j in range(NFF):
                wplan.append([(0, w_up[:, j * 128:(j + 1) * 128]), (128, w_up[:, DFF + j * 128:DFF + (j + 1) * 128])])
        wissued = [0]
        resident = []

        def res_dma(dst, src, key, name):
            resident.append((dst, src, key, name))
        res_dma(w_pool_sb[:], w_pool.rearrange("(g c) d -> c g d", c=128), "wpool", "D_wpool")
        for q in range(4):
            res_dma(w_out_sb[:, :, q * 256:(q + 1) * 256],
                    w_out[:, q * 256:(q + 1) * 256].rearrange("(k p) n -> p k n", p=128), ("wout", q), f"D_wout{q}")
        for q in range(11):
            res_dma(w_down_sb[:, 2 * q:2 * q + 2, :],
                    w_down[q * 256:(q + 1) * 256, :].rearrange("(k p) n -> p k n", p=128), ("wdown", q), f"D_wdown{q}")
        res_i = [0]

        def issue_resident(n=1):
            for _ in range(n):
                if res_i[0] < len(resident):
                    dst, src, key, name = resident[res_i[0]]
                    res_i[0] += 1
                    P.op("pool", lambda e, dst=dst, src=src: e.dma_start(out=dst, in_=src), writes=[key], dsem=name)

        def prefetch(upto):
            while wissued[0] <= min(upto, len(wplan) - 1):
                wi = wissued[0]
                slot = wi % NW
                for h, (c0, src) in enumerate(wplan[wi]):
                    P.op("pool", lambda e, slot=slot, c0=c0, src=src: e.dma_start(
                        out=wslot[slot][:, :, c0:c0 + 128], in_=src.rearrange("(k p) n -> p k n", p=128)),
                        writes=[("W", slot, h)], dsem=f"D_w{slot}_{h}")
                wissued[0] += 1
                if wi >= 1:
                    issue_resident(1)

        wuse = [0]

        def next_weights(ahead=0):
            wi = wuse[0]
            wuse[0] += 1
            prefetch(wi + NW - 1 - ahead)
            return wi % NW

        def norm_stats(src, nrows, junk, junkkey, srckeys):
            n = nctr[0]
            nctr[0] += 1
            P.op("act", lambda e: e.activation(out=junk, in_=src, func=AF.Square, accum_out=ss_all[0:nrows, n:n + 1]),
                 reads=srckeys, writes=[("ss", n), junkkey])
            P.op("act", lambda e: e.activation(out=rs_all[0:nrows, n:n + 1], in_=ss_all[0:nrows, n:n + 1], func=AF.Sqrt,
                                               scale=1.0 / D, bias=epst[0:nrows, :]),
                 reads=[("ss", n), "epst"], writes=[("rs", n)])
            P.op("dve", lambda e: e.reciprocal(out=rs_all[0:nrows, n:n + 1], in_=rs_all[0:nrows, n:n + 1]),
                 reads=[("rs", n)], writes=[("rs", n)])
            return n

        def norm_transpose(src, nrows, srckeys, hb, hk, dstT, col0, gcol, dstkeys):
            hbv = hb[0:nrows, :]
            n = norm_stats(src, nrows, hbv, hk, srckeys)
            P.op("dve", lambda e: e.tensor_scalar(out=hbv, in0=src, scalar1=rs_all[0:nrows, n:n + 1], scalar2=None,
                                                  op0=ALU.mult),
                 reads=list(srckeys) + [("rs", n)], writes=[hk])

            def part_b():
                tr = TR[trctr[0] % 2]
                trk = ("TR", trctr[0] % 2)
                trctr[0] += 1

                def do_tr(e):
                    for k in range(8):
                        ins = e.transpose(out=tr[:, k, 0:nrows], in_=hb[0:nrows, k * 128:(k + 1) * 128],
                                          identity=ident[0:nrows, 0:nrows])
                    return ins
                P.op("pe", do_tr, reads=[hk, "ident"], writes=[trk])
                P.op("dve", lambda e: e.tensor_tensor(
                    out=dstT[:, :, col0:col0 + nrows], in0=tr[:, :, 0:nrows],
                    in1=cst_sb[:, gcol:gcol + 8].unsqueeze(2).to_broadcast([128, 8, nrows]), op=ALU.mult),
                    reads=[trk, "cst"], writes=dstkeys)
            return part_b

        def v3(ap, b=4):
            return ap.rearrange("p (a b) -> p a b", b=b)

        def stage0_first():
            hbs = [(hb0, "hb0"), (hbuf[0], ("hb", 0)), (hbuf[1], ("hb", 1))]
            pbs = []
            for i in range(4):
                hb, hk = hbs[i % 3]
                pbs.append(norm_transpose(x1[:, i, :], 128, [("x1", i)], hb, hk, hT, i * 128, C_G1, [("hT", i)]))
                if i >= 1:
                    pbs.pop(0)()
            for pb in pbs:
                pb()

        def stage0_small():
            norm_transpose(xs_sb[0:NSM, :], NSM, ["xs"], hs, "hs", hTs, 0, C_G1, ["hTs"])()

        def x1_reload(s):
            for i in range(4):
                r0 = (s * 4 + i) * 128
                P.op("sp", lambda e, i=i, r0=r0: e.dma_start(out=x1[:, i, :], in_=xp[r0:r0 + 128, :]),
                     writes=[("x1", i)], dsem=f"D_x1_{i}")

        def mm_feat(slot, half, rhsT, rhs_keys, ncols, out_ap, g, nk=8):
            def f(e):
                for k in range(nk):
                    ins = e.matmul(out_ap, lhsT=wslot[slot][:, k, half * 128:(half + 1) * 128],
                                   rhs=rhsT[:, k, 0:ncols], start=(k == 0), stop=(k == nk - 1))
                return ins
            P.op("pe", f, reads=[("W", slot, half)] + rhs_keys, writes=[gk(g), gt(g)])

        S1_PAIRS = S1_PAIRS_G

        def stage1(s):
            small = (s == 0)
            hkeys = [("hT", i) for i in range(4)]
            deferred = []
            slots = {}
            for pi, pair in enumerate(S1_PAIRS):
                slot = next_weights()
                gsm = galloc(2) if small else None
                nd = []
                for (rdy, fn) in deferred:
                    if pi >= rdy:
                        fn()
                    else:
                        nd.append((rdy, fn))
                deferred[:] = nd
                for half, (kind, j) in enumerate(pair):
                    g = galloc()
                    mm_feat(slot, half, hT, hkeys, TW, G[g][:, :], g)
                    slots[(kind, j)] = (g, gsm, half)
                if small:
                    for half in range(2):
                        mm_feat(slot, half, hTs, ["hTs"], NSM, sub(gsm, half), gsm)
                for half, (kind, j) in enumerate(pair):
                    g = slots[(kind, j)][0]
                    if kind == "u":
                        rest = small_u(j, gsm, half) if small else None
                        rdy = 99 if small else pi + 3
                        deferred.append((rdy, main_u(s, j, g)))
                        if rest is not None:
                            deferred.append((rdy, rest()))
                    elif kind == "v":
                        g_gc, gsm_gc, half_gc = slots[("gc", j)]
                        rest = small_cv(j, gsm_gc, half_gc, gsm, half) if small else None
                        main_cv(s, j, g_gc, g)
                        if rest is not None:
                            rest()
                    elif kind == "gb":
                        bj = j % 2
                        P.op("dve", lambda e, g=g, j=j, bj=bj: e.tensor_tensor(
                            out=aT[:, 4 + j, :], in0=G[g][:, :], in1=TT[bj], op=ALU.mult),
                            reads=[gk(g), ("TT", bj)], writes=[("aT", 4 + j, i) for i in range(4)] + [gt(g), "arS1_dve"])
                        gdone(g)
                        if small:
                            P.op("dve", lambda e, gsm=gsm, half=half, j=j, bj=bj: e.tensor_tensor(
                                out=v3(aTs[:, 4 + j, :]), in0=v3(sub(gsm, half)), in1=Tcs[bj][:], op=ALU.mult),
                                reads=[gk(gsm), ("Tcs", bj)], writes=[("aTs", 4 + j), gt(gsm)])
                            gdone(gsm)
            for (rdy, fn) in deferred:
                fn()

        def small_u(j, gsm, half):
            W = WINDOWS[j]
            b = j % 2
            um, us = Um[b], Us[b]
            sp_ap = sub(gsm, half)
            P.op("act", lambda e: e.activation(out=um[:, 15:31], in_=sp_ap[:, 0:16], func=AF.Copy),
                 reads=[gk(gsm), "zeros"], writes=[("Um", b), gt(gsm)])
            P.op("act", lambda e: e.activation(out=us[:, :, 15:19], in_=v3(sp_ap[:, SOFF:NSM]), func=AF.Copy),
                 reads=[gk(gsm)], writes=[("UsN", b), gt(gsm)])
            gdone(gsm)
            P.op("act", lambda e: e.activation(out=Usave[:, j, :], in_=um[:, 16:31], func=AF.Copy),
                 reads=[("Um", b)], writes=[("Usave", j)])

            def rest():
                P.op("pool", lambda e: e.tensor_copy(out=us[:, :, 0:15], in_=stp_sb[:, j, :, :]),
                     reads=[("stp", j)], writes=[("UsH", b)])
                P.op("pool", lambda e: e.tensor_copy(out=stp_sb[:, j, :, :], in_=us[:, :, 4:19]),
                     reads=[("UsH", b), ("UsN", b)], writes=[("stp", j)])
                cur_m, cur_s = um, us
                m = 1
                si = 0
                while m < W:
                    lo = 15 - (W - 2 * m)
                    dst_m, dst_s = Am[si % 2], As[si % 2]

                    def f(e, cur_m=cur_m, cur_s=cur_s, dst_m=dst_m, dst_s=dst_s, m=m, lo=lo):
                        e.tensor_tensor(out=dst_m[:, lo:31], in0=cur_m[:, lo:31], in1=cur_m[:, lo - m:31 - m], op=ALU.add)
                        return e.tensor_tensor(out=dst_s[:, :, lo:19], in0=cur_s[:, :, lo:19],
                                               in1=cur_s[:, :, lo - m:19 - m], op=ALU.add)
                    P.op("dve", f, reads=[("Um", b), ("UsH", b), ("UsN", b), "smscr"], writes=["smscr"])
                    cur_m, cur_s = dst_m, dst_s
                    m *= 2
                    si += 1
                P.op("dve", lambda e, cur_m=cur_m: e.tensor_tensor(out=cur_m[:, 15:31], in0=cur_m[:, 15:31],
                                                                 in1=invcnt[:, j, :], op=ALU.mult),
                     reads=["smscr", "invcnt"], writes=["smscr"])

                def comb(e, cur_m=cur_m, cur_s=cur_s):
                    e.tensor_tensor(out=Dsm[j][:, 0:16], in0=cur_m[:, 15:31], in1=um[:, 15:31], op=ALU.subtract)
                    return e.scalar_tensor_tensor(out=v3(Dsm[j][:, SOFF:NSM]), in0=cur_s[:, :, 15:19], scalar=1.0 / W,
                                                  in1=us[:, :, 15:19], op0=ALU.mult, op1=ALU.subtract)
                P.op("dve", comb, reads=["smscr", ("Um", b), ("UsN", b), "invcnt"], writes=["smscr", ("Dsm", j)])

                def later():
                    g2 = galloc()
                    so = sub(g2, 0)
                    P.op("pe", lambda e: e.matmul(so, lhsT=w_pool_sb[:, j, :], rhs=Dsm[j][:, :], start=True, stop=True),
                         reads=[("Dsm", j), "wpool"], writes=[gk(g2), gt(g2)])
                    P.op("act", lambda e: e.activation(out=aTs[:, j, :], in_=so, func=AF.Copy, scale=cc(C_PS + j)),
                         reads=[gk(g2), "cst"], writes=[("aTs", j), gt(g2)])
                    gdone(g2)
                return later
            return rest

        def main_u(s, j, g):
            W = WINDOWS[j]
            b = j % 2
            u = U[b]
            P.op("act", lambda e: e.activation(out=u[:, 0:15], in_=Usave[:, j, :], func=AF.Copy),
                 reads=[("Usave", j)] + S3T, writes=[("UH", b)])
            P.op("act", lambda e: e.activation(out=u[:, 15:527], in_=G[g][:, :], func=AF.Copy),
                 reads=[gk(g)] + S3T, writes=[("UN", b), gt(g)])
            gdone(g)
            P.op("act", lambda e: e.activation(out=Usave[:, j, :], in_=u[:, 512:527], func=AF.Copy),
                 reads=[("UN", b)], writes=[("Usave", j), "arS1_pool"])
            cur = u
            m = 1
            si = 0
            while m < W:
                lo = 15 - (W - 2 * m)
                dst = AB[si % 2]
                P.op("dve", lambda e, cur=cur, dst=dst, m=m, lo=lo: e.tensor_tensor(
                    out=dst[:, lo:527], in0=cur[:, lo:527], in1=cur[:, lo - m:527 - m], op=ALU.add),
                    reads=[("UH", b), ("UN", b), "AB"] + S3T, writes=["AB", "arS1_dve"])
                cur = dst
                m *= 2
                si += 1
            db = Dbuf[j]
            P.op("dve", lambda e, cur=cur: e.scalar_tensor_tensor(
                out=db[:, :], in0=cur[:, 15:527], scalar=1.0 / W, in1=u[:, 15:527], op0=ALU.mult, op1=ALU.subtract),
                reads=["AB", ("UN", b)] + S3T, writes=[("D", j), "arS1_dve"])

            def later():
                g2 = galloc()
                P.op("pe", lambda e: e.matmul(G[g2][:, :], lhsT=w_pool_sb[:, j, :], rhs=db[:, :], start=True, stop=True),
                     reads=[("D", j), "wpool"], writes=[gk(g2), gt(g2), "arS1_pe"])
                P.op("act", lambda e: e.activation(out=aT[:, j, :], in_=G[g2][:, :], func=AF.Copy, scale=cc(C_PS + j)),
                     reads=[gk(g2), "cst"], writes=[("aT", j, i) for i in range(4)] + [gt(g2)])
                gdone(g2)
            return later

        def small_cv(j, g_gc, h_gc, g_v, h_v):
            b = j % 2
            x = Xcv[b]
            t = Tcs[b]
            gc_ap = sub(g_gc, h_gc)
            v_ap = sub(g_v, h_v)
            P.op("act", lambda e: e.activation(out=GCs[:, :], in_=gc_ap, func=AF.Copy),
                 reads=[gk(g_gc)], writes=["GCs", gt(g_gc)])
            gdone(g_gc)
            P.op("dve", lambda e: e.tensor_tensor(out=x[:, :, 2:6], in0=v3(v_ap), in1=v3(GCs[:, :]), op=ALU.mult),
                 reads=[gk(g_v), "GCs"], writes=[("XcvN", b), gt(g_v)])
            gdone(g_v)
            P.op("act", lambda e: e.activation(out=CVsave[:, j, :], in_=x[:, 3, 4:6], func=AF.Copy),
                 reads=[("XcvN", b)], writes=[("CVsave", j)])

            def rest():
                def hist(e):
                    e.tensor_copy(out=x[:, 1:5, 0:2], in_=x[:, 0:4, 4:6])
                    return e.tensor_copy(out=x[:, 5:NQ, 0:2], in_=stc_sb[:, j, :, :])
                P.op("pool", hist, reads=[("XcvN", b), ("stc", j), "zeros"], writes=[("XcvH", b)])
                P.op("pool", lambda e: e.tensor_copy(out=stc_sb[:, j, :, :], in_=x[:, 5:NQ, 4:6]),
                     reads=[("XcvN", b), ("XcvH", b)], writes=[("stc", j)])
                P.op("act", lambda e: e.activation(out=t[:], in_=x[:, :, 0:4], func=AF.Copy, scale=cc(C_CW + 3 * j)),
                     reads=[("XcvN", b), ("XcvH", b), "cst"], writes=[("Tcs", b)])
                P.op("dve", lambda e: e.scalar_tensor_tensor(out=t[:], in0=x[:, :, 1:5], scalar=cc(C_CW + 3 * j + 1),
                                                              in1=t[:], op0=ALU.mult, op1=ALU.add),
                     reads=[("XcvN", b), ("XcvH", b), ("Tcs", b)], writes=[("Tcs", b)])
                P.op("dve", lambda e: e.scalar_tensor_tensor(out=t[:], in0=x[:, :, 2:6], scalar=cc(C_CW + 3 * j + 2),
                                                              in1=t[:], op0=ALU.mult, op1=ALU.add),
                     reads=[("XcvN", b), ("Tcs", b)], writes=[("Tcs", b)])
            return rest

        def main_cv(s, j, g_gc, g_v):
            b = j % 2
            cv = CV[b]
            t = TT[b]
            gcb = GC[0]
            P.op("act", lambda e: e.activation(out=gcb, in_=G[g_gc][:, :], func=AF.Copy),
                 reads=[gk(g_gc)] + S3T, writes=["GC", gt(g_gc)])
            gdone(g_gc)
            P.op("act", lambda e: e.activation(out=cv[:, 0:2], in_=CVsave[:, j, :], func=AF.Copy),
                 reads=[("CVsave", j)] + S3T, writes=[("CVH", b)])
            P.op("dve", lambda e: e.tensor_tensor(out=cv[:, 2:514], in0=G[g_v][:, :], in1=gcb, op=ALU.mult),
                 reads=[gk(g_v), "GC"] + S3T, writes=[("CVN", b), gt(g_v), "arS1_dve"])
            gdone(g_v)
            P.op("act", lambda e: e.activation(out=CVsave[:, j, :], in_=cv[:, 512:514], func=AF.Copy),
                 reads=[("CVN", b)], writes=[("CVsave", j), "arS1_pool"])
            P.op("act", lambda e: e.activation(out=t, in_=cv[:, 0:512], func=AF.Copy, scale=cc(C_CW + 3 * j)),
                 reads=[("CVN", b), ("CVH", b), "cst"] + S3T, writes=[("TT", b)])
            P.op("dve", lambda e: e.scalar_tensor_tensor(out=t, in0=cv[:, 1:513], scalar=cc(C_CW + 3 * j + 1), in1=t,
                                                          op0=ALU.mult, op1=ALU.add),
                 reads=[("CVN", b), ("CVH", b), ("TT", b)], writes=[("TT", b)])
            P.op("dve", lambda e: e.scalar_tensor_tensor(out=t, in0=cv[:, 2:514], scalar=cc(C_CW + 3 * j + 2), in1=t,
                                                          op0=ALU.mult, op1=ALU.add),
                 reads=[("CVN", b), ("TT", b)], writes=[("TT", b), "arS1_dve"])

        def mm_tok(lhsT_of_k, lkey_of_k, nrows, wsb, wkeys, korder, ngrp, ga, gb):
            nk = len(korder)
            per = (nk + ngrp - 1) // ngrp
            for gi in range(0, nk, per):
                ks = korder[gi:gi + per]

                def f(e, ks=ks, gi=gi):
                    for n, k in enumerate(ks):
                        first = (gi + n == 0)
                        lastk = (gi + n == nk - 1)
                        e.matmul(G[ga][0:nrows, :], lhsT=lhsT_of_k(k), rhs=wsb[:, k, 0:512], start=first, stop=lastk)
                        ins = e.matmul(G[gb][0:nrows, :], lhsT=lhsT_of_k(k), rhs=wsb[:, k, 512:1024],
                                       start=first, stop=lastk)
                    return ins
                P.op("pe", f, reads=[lkey_of_k(k) for k in ks] + list(wkeys), writes=[gk(ga), gk(gb), gt(ga), gt(gb)])

        def resid(dst, nrows, dkey, ga, gb):
            def f(e):
                e.tensor_tensor(out=dst[0:nrows, 0:512], in0=G[ga][0:nrows, :], in1=dst[0:nrows, 0:512], op=ALU.add)
                return e.tensor_tensor(out=dst[0:nrows, 512:1024], in0=G[gb][0:nrows, :], in1=dst[0:nrows, 512:1024],
                                       op=ALU.add)
            P.op("dve", f, reads=[gk(ga), gk(gb), dkey], writes=[dkey, gt(ga), gt(gb)])
            gdone(ga)
            gdone(gb)

        S2_KORDER = [4, 5, 6, 0, 1, 2, 3, 7]
        WOUT_KEYS = [("wout", q) for q in range(4)]
        WDOWN_KEYS = [("wdown", q) for q in range(11)]

        def warm(n):
            if n <= 0:
                return
            g = galloc()

            def f(e):
                for _ in range(n):
                    ins = e.matmul(G[g][:, :], lhsT=w_out_sb[:, 0, 0:128], rhs=w_out_sb[:, 0, 0:512],
                                   start=True, stop=True)
                return ins
            P.op("pe", f, reads=[("wout", 0), ("wout", 1)], writes=[gk(g), gt(g)])
            gdone(g)

        def stage2(s):
            small = (s == 0)
            pend = []
            warm(WARM_S2)

            def do_small():
                ga, gb = galloc(), galloc()
                mm_tok(lambda k: aTs[:, k, 0:NSM], lambda k: ("aTs", k), NSM, w_out_sb, WOUT_KEYS, S2_KORDER, 4, ga, gb)
                resid(xs_sb, NSM, "xs", ga, gb)
                pbs = norm_transpose(xs_sb[0:NSM, :], NSM, ["xs"], hs, "hs", aTs, 0, C_G2,
                                     [("aTs", k) for k in range(8)])

                def pbs2(pbs=pbs):
                    pbs()
                    P.op("pool", lambda e: e.memset(aTs[:, :, NMETA:SOFF], 0.0), writes=[("aTs", k) for k in range(8)])
                return pbs2
            for i in range(4):
                ga, gb = galloc(), galloc()
                mm_tok(lambda k, i=i: aT[:, k, i * 128:(i + 1) * 128], lambda k, i=i: ("aT", k, i), 128,
                       w_out_sb, WOUT_KEYS, S2_KORDER, 4 if i == 0 else 1, ga, gb)
                resid(x1[:, i, :], 128, ("x1", i), ga, gb)
                while pend:
                    pend.pop(0)()
                pend.append(norm_transpose(x1[:, i, :], 128, [("x1", i)], hbuf[i % 2], ("hb", i % 2), aT, i * 128, C_G2,
                                           [("aT", k, i) for k in range(8)]))
                if small and i == 1:
                    pend.append(do_small())
            return pend

        t1ctr = [0]
        x3ctr = [0]

        def stage3(s):
            small = (s == 0)
            last = (s == NST - 1)
            akeys = [("aT", k, i) for k in range(8) for i in range(4)]
            askeys = [("aTs", k) for k in range(8)]
            prev_tail = None
            slots3 = {}
            LA = 2

            def get_slot(j, ahead=0):
                if j not in slots3:
                    slots3[j] = next_weights(ahead)
                return slots3[j]

            def small_mm(j):
                slot = get_slot(j, LA - 1 if j >= LA else j)
                gsm = galloc()
                for half in range(2):
                    mm_feat(slot, half, aTs, askeys, NSM, sub(gsm, half), gsm)
                return gsm
            if small:
                for jj in range(LA):
                    small_A(jj, small_mm(jj))
                small_B(0)
            for j in range(NFF):
                slot = get_slot(j)
                gs = []
                for half in range(2):
                    g = galloc()
                    gs.append(g)
                    mm_feat(slot, half, aT, akeys, TW, G[g][:, :], g)
                gsm_next = small_mm(j + LA) if (small and j + LA < NFF) else None
                if small and j + 1 < NFF:
                    small_B(j + 1)
                ti = t1ctr[0] % NT1P
                t1ctr[0] += 1
                tp = T1p[ti]
                kh = [("T1h", ti, h) for h in range(2)]
                kb = [("T1b", ti, h) for h in range(2)]
                kt = [("T1t", ti, h) for h in range(2)]
                cs = (j, NFF + j)
                ek = [("Esave", c) for c in cs]
                P.op("act", lambda e, tp=tp, j=j: e.activation(out=tp[:, :, 0:2], in_=EsV[:, :, j, :], func=AF.Copy),
                     reads=ek + S1T, writes=kh)
                for h in range(2):
                    c, g = cs[h], gs[h]
                    P.op("act", lambda e, tp=tp, g=g, c=c, h=h: e.activation(
                        out=tp[:, h, 2:514], in_=G[g][:, :], func=AF.Identity, scale=cc(C_FW + 3 * c), bias=cc(C_FB + c)),
                        reads=[gk(g), "cst"] + S1T, writes=[kb[h], kt[h], gt(g)])
                if prev_tail is not None:
                    prev_tail()
                if gsm_next is not None:
                    small_A(j + LA, gsm_next)
                for h in range(2):
                    c, g = cs[h], gs[h]
                    P.op("dve", lambda e, tp=tp, g=g, c=c, h=h: e.scalar_tensor_tensor(
                        out=tp[:, h, 1:513], in0=G[g][:, :], scalar=cc(C_FW + 3 * c + 1), in1=tp[:, h, 1:513],
                        op0=ALU.mult, op1=ALU.add),
                        reads=[gk(g), kh[h], kb[h], kt[h], "cst"], writes=[kh[h], kb[h], kt[h], gt(g)])
                for h in range(2):
                    c, g = cs[h], gs[h]
                    if last:
                        P.op("dve", lambda e, g=g, c=c: e.tensor_copy(out=OFp[:, c, :], in_=G[g][:, 510:512]),
                             reads=[gk(g)], writes=[("OFp", c), gt(g)])
                    P.op("dve", lambda e, tp=tp, g=g, c=c, h=h: e.scalar_tensor_tensor(
                        out=tp[:, h, 0:512], in0=G[g][:, :], scalar=cc(C_FW + 3 * c + 2), in1=tp[:, h, 0:512],
                        op0=ALU.mult, op1=ALU.add),
                        reads=[gk(g), kh[h], kb[h], "cst"], writes=[kh[h], kb[h], gt(g)])
                    gdone(g)

                def tail(tp=tp, j=j, kh=kh, kb=kb, kt=kt, ek=ek):
                    P.op("act", lambda e: e.activation(out=EsV[:, :, j, :], in_=tp[:, :, 512:514], func=AF.Copy),
                         reads=kt, writes=ek + ["arS3_act"])
                    P.op("act", lambda e: e.activation(out=tp[:, 1, 0:512], in_=tp[:, 1, 0:512], func=AF.Silu),
                         reads=[kh[1], kb[1]], writes=[kh[1], kb[1]])
                    P.op("pool", lambda e: e.tensor_tensor(out=actT[:, j, :], in0=tp[:, 1, 0:512], in1=tp[:, 0, 0:512],
                                                           op=ALU.mult),
                         reads=kh + kb, writes=[("actT", j, i) for i in range(4)] + ["arS3_pool"])
                prev_tail = tail
                if small:
                    small_C(j)
            prev_tail()

        def small_bufs(j):
            xi = j % 2
            return X3p[xi], T3p[xi], ("X3N", xi), ("X3H", xi), ("T3", xi), (j, NFF + j)

        def small_A(j, gsm):
            x, t, kx, kxh, ktt, cs = small_bufs(j)
            P.op("act", lambda e: e.activation(out=x[:, :, :, 2:6],
                                               in_=G[gsm][:, 0:2 * NSM].rearrange("p (h q t) -> p h q t", h=2, t=4),
                                               func=AF.Copy),
                 reads=[gk(gsm)], writes=[kx, gt(gsm)])
            gdone(gsm)

            def hist(e):
                e.tensor_copy(out=x[:, :, 1:5, 0:2], in_=x[:, :, 0:4, 4:6])
                return e.tensor_copy(out=x[:, :, 5:NQ, 0:2], in_=stfV[:, :, j, :, :])
            P.op("pool", hist, reads=[kx, ("stf", cs[0]), ("stf", cs[1]), "zeros"], writes=[kxh])
            P.op("pool", lambda e: e.tensor_copy(out=stfV[:, :, j, :, :], in_=x[:, :, 5:NQ, 4:6]),
                 reads=[kx, kxh], writes=[("stf", cs[0]), ("stf", cs[1])])

        def small_B(j):
            x, t, kx, kxh, ktt, cs = small_bufs(j)
            for h in range(2):
                c = cs[h]
                P.op("act", lambda e, h=h, c=c: e.activation(out=t[:, h], in_=x[:, h, :, 0:4], func=AF.Identity,
                                                             scale=cc(C_FW + 3 * c), bias=cc(C_FB + c)),
                     reads=[kx, kxh, "cst"], writes=[(ktt, h)])
            for tap in (1, 2):
                for h in range(2):
                    c = cs[h]
                    P.op("dve", lambda e, h=h, c=c, tap=tap: e.scalar_tensor_tensor(
                        out=t[:, h], in0=x[:, h, :, tap:tap + 4], scalar=cc(C_FW + 3 * c + tap), in1=t[:, h],
                        op0=ALU.mult, op1=ALU.add),
                        reads=[kx, kxh, (ktt, h), "cst"], writes=[(ktt, h)])
            P.op("pool", lambda e: e.tensor_copy(out=EsV[:, :, j, :], in_=t[:, :, NMETA // 4, 0:2]),
                 reads=[(ktt, 0), (ktt, 1)], writes=[("Esave", cs[0]), ("Esave", cs[1])])

        def small_C(j):
            x, t, kx, kxh, ktt, cs = small_bufs(j)
            P.op("act", lambda e: e.activation(out=t[:, 1], in_=t[:, 1], func=AF.Silu),
                 reads=[(ktt, 1)], writes=[(ktt, 1)])
            P.op("dve", lambda e: e.tensor_tensor(out=v3(actTs[:, j, :]), in0=t[:, 1], in1=t[:, 0], op=ALU.mult),
                 reads=[(ktt, 0), (ktt, 1)], writes=[("actTs", j)])

        def final_norm_store(dst, nrows, dkey, junk, junkkey, out_ap, src_rows, dsem):
            n = norm_stats(dst[0:nrows, :], nrows, junk, junkkey, [dkey])
            P.op("dve", lambda e: e.scalar_tensor_tensor(
                out=dst[0:nrows, :], in0=dst[0:nrows, :], scalar=rs_all[0:nrows, n:n + 1], in1=gfin_sb[0:nrows, :],
                op0=ALU.mult, op1=ALU.mult), reads=[dkey, ("rs", n), "gfin"], writes=[dkey])
            lo, hi = src_rows
            finals.append(P.op("sp", lambda e: e.dma_start(out=out_ap, in_=dst[lo:hi, :]), reads=[dkey], dsem=dsem))

        S4_KORDER = list(range(NFF))

        def stage0_tile(sn, i):
            gti = sn * 4 + i
            xr = xrot[gti % 2]
            xk = ("xrot", gti % 2)
            r0 = gti * 128
            P.op("sp", lambda e: e.dma_start(out=xr[:], in_=xp[r0:r0 + 128, :]), writes=[xk], dsem=f"D_xrot{gti % 2}")
            return norm_transpose(xr[:], 128, [xk], hb0, "hb0", hT, i * 128, C_G1, [("hT", i)])

        def stage4(s):
            small = (s == 0)
            nxt = s + 1 if s + 1 < NST else None
            def do_small4():
                ga, gb = galloc(), galloc()
                mm_tok(lambda k: actTs[:, k, 0:NSM], lambda k: ("actTs", k), NSM, w_down_sb, WDOWN_KEYS,
                       S4_KORDER, 3, ga, gb)
                resid(xs_sb, NSM, "xs", ga, gb)
                final_norm_store(xs_sb, NSM, "xs", hs[0:NSM, :], "hs", ys[:, :], (SOFF, NSM), "D_ys")
            pb = stage0_tile(nxt, 0) if nxt is not None else None
            for i in range(4):
                ga, gb = galloc(), galloc()
                mm_tok(lambda k, i=i: actT[:, k, i * 128:(i + 1) * 128], lambda k, i=i: ("actT", k, i), 128,
                       w_down_sb, WDOWN_KEYS, S4_KORDER, 4 if i == 0 else 1, ga, gb)
                if pb is not None:
                    pb()
                    pb = stage0_tile(nxt, i + 1) if i < 3 else None
                    if i == 3:
                        warm(WARM_S1)
                resid(x1[:, i, :], 128, ("x1", i), ga, gb)
                r0 = (s * 4 + i) * 128
                final_norm_store(x1[:, i, :], 128, ("x1", i), hbuf[i % 2][:, :], ("hb", i % 2), yp[r0:r0 + 128, :],
                                 (0, 128), f"D_x1_{i}")
                if small and i == 1:
                    do_small4()

        prefetch(NW - 2)
        x1_reload(0)
        stage0_small()
        stage0_first()
        for s in range(NST):
            if s > 0:
                x1_reload(s)
            if DEBUG and s == 0:
                finals.append(P.op("sp", lambda e: e.dma_start(out=dbg_hT[:, :], in_=hT[:].rearrange("p a b -> p (a b)")),
                                   reads=[("hT", i) for i in range(4)], dsem="D_dbg0"))
            stage1(s)
            if DEBUG and s == 0:
                finals.append(P.op("sp", lambda e: e.dma_start(out=dbg_aT[:, :], in_=aT[:].rearrange("p a b -> p (a b)")),
                                   reads=[("aT", k, i) for k in range(8) for i in range(4)], dsem="D_dbg1"))
            pend = stage2(s)
            warm(WARM_S3A)
            for fn in pend:
                fn()
            warm(WARM_S3B)
            if DEBUG and s == 0:
                finals.append(P.op("sp", lambda e: e.dma_start(out=dbg_h2T[:, :], in_=aT[:].rearrange("p a b -> p (a b)")),
                                   reads=[("aT", k, i) for k in range(8) for i in range(4)], dsem="D_dbg2"))
                finals.append(P.op("sp", lambda e: e.dma_start(out=dbg_x1[:, :], in_=x1[:].rearrange("p a b -> p (a b)")),
                                   reads=[("x1", i) for i in range(4)], dsem="D_dbg3"))
            stage3(s)
            if DEBUG and s == 0:
                finals.append(P.op("sp", lambda e: e.dma_start(out=dbg_actT[:, :], in_=actT[:].rearrange("p a b -> p (a b)")),
                                   reads=[("actT", j, i) for j in range(NFF) for i in range(4)], dsem="D_dbg4"))
            stage4(s)
        while res_i[0] < len(resident):
            issue_resident(1)
        finals.append(P.op("sp", lambda e: e.dma_start(out=o_pool_p[:, :], in_=Usave[:].rearrange("p a b -> p (a b)")),
                           reads=[("Usave", j) for j in range(4)], dsem="D_o1"))
        finals.append(P.op("sp", lambda e: e.dma_start(out=o_conv_p[:, :], in_=CVsave[:].rearrange("p a b -> p (a b)")),
                           reads=[("CVsave", j) for j in range(4)], dsem="D_o2"))
        finals.append(P.op("sp", lambda e: e.dma_start(out=o_ffn_p[:, :], in_=OFp[:].rearrange("p a b -> p (a b)")),
                           reads=[("OFp", c) for c in range(44)], dsem="D_o3"))
        finals.append(P.op("sp", lambda e: e.dma_start(out=o_pool_s[:, :], in_=stp_sb[:].rearrange("p a b c -> p (a b c)")),
                           reads=[("stp", j) for j in range(4)], dsem="D_o4"))
        finals.append(P.op("sp", lambda e: e.dma_start(out=o_conv_s[:, :], in_=stc_sb[:].rearrange("p a b c -> p (a b c)")),
                           reads=[("stc", j) for j in range(4)], dsem="D_o5"))
        finals.append(P.op("sp", lambda e: e.dma_start(out=o_ffn_s[:, :], in_=stf_sb[:].rearrange("p a b c -> p (a b c)")),
                           reads=[("stf", c) for c in range(44)], dsem="D_o6"))
        P.emit(nc, final_wait_ops=finals)
    return nc


def _col(v, n):
    return np.ascontiguousarray(np.asarray(v, np.float32).reshape(n, 128).T)


_NC_CACHE = {}


def kernel(x_prompt, x_sample, state_pool, state_conv, state_ffn, meta_tokens,
           norm_mix, w_in, w_pool, pool_scale, conv_w, w_out,
           norm_ffn, w_up, ffn_conv_w, ffn_conv_b, w_down, norm_final):
    f = lambda a: np.ascontiguousarray(np.asarray(a, dtype=np.float32))
    x_prompt, x_sample = f(x_prompt), f(x_sample)
    state_pool, state_conv, state_ffn = f(state_pool), f(state_conv), f(state_ffn)
    meta_tokens = f(meta_tokens)
    cst = np.zeros((128, NCST), np.float32)
    cst[:, C_G1:C_G1 + 8] = _col(norm_mix[0], 8)
    cst[:, C_G2:C_G2 + 8] = _col(norm_ffn[0], 8)
    cst[:, C_PS:C_PS + 4] = _col(pool_scale[0], 4)
    cw = f(conv_w)[0]
    cst[:, C_CW:C_CW + 12] = cw.reshape(3, 4, 128).transpose(2, 1, 0).reshape(128, 12)
    fw = f(ffn_conv_w)[0]
    cst[:, C_FW:C_FW + 132] = fw.reshape(3, 44, 128).transpose(2, 1, 0).reshape(128, 132)
    cst[:, C_FB:C_FB + 44] = _col(f(ffn_conv_b)[0], 44)
    gfin = np.ascontiguousarray(np.broadcast_to(f(norm_final)[None, :], (128, D)))
    shared = {
        "w_in": f(w_in)[0], "w_up": f(w_up)[0], "w_out": f(w_out)[0], "w_down": f(w_down)[0],
        "w_pool": f(w_pool)[0].reshape(512, 128), "cst": cst, "gfin": gfin,
    }
    in_maps = []
    for c in range(NCORES):
        sl = slice(c * NSEQ, (c + 1) * NSEQ)
        m = dict(shared)
        m["xp"] = x_prompt[c]
        m["xsm"] = np.ascontiguousarray(np.concatenate(
            [meta_tokens, np.zeros((NPAD, D), np.float32), x_sample[sl].reshape(NSEQ * DSEQ, D)], axis=0))
        m["st_pool"] = np.ascontiguousarray(
            state_pool[0, sl].reshape(NSEQ, 15, 4, 128).transpose(3, 2, 0, 1).reshape(128, -1))
        m["st_conv"] = np.ascontiguousarray(
            state_conv[0, sl].reshape(NSEQ, 2, 4, 128).transpose(3, 2, 0, 1).reshape(128, -1))
        m["st_ffn"] = np.ascontiguousarray(
            state_ffn[0, sl].reshape(NSEQ, 2, 44, 128).transpose(3, 2, 0, 1).reshape(128, -1))
        in_maps.append(m)
    if "nc" not in _NC_CACHE:
        _NC_CACHE["nc"] = build_nc()
    nc = _NC_CACHE["nc"]
    res = run_bass_kernel_spmd(nc, in_maps, core_ids=list(range(NCORES)))
    R = res.results
    if DEBUG:
        _NC_CACHE["dbg"] = R[0]
    y_prompt = np.stack([R[c]["yp"] for c in range(NCORES)], axis=0)
    y_sample = np.concatenate([R[c]["ys"].reshape(NSEQ, DSEQ, D) for c in range(NCORES)], axis=0)

    def unp(name, nch, rows):
        return np.stack([R[c][name].reshape(128, nch, rows).transpose(2, 1, 0).reshape(rows, nch * 128)
                         for c in range(NCORES)], axis=0)[None]

    def uns(name, nch, rows):
        return np.concatenate([R[c][name].reshape(128, nch, NSEQ, rows).transpose(2, 3, 1, 0).reshape(NSEQ, rows, nch * 128)
                               for c in range(NCORES)], axis=0)[None]
    outs = (y_prompt, y_sample,
            unp("o_pool_p", 4, 15), unp("o_conv_p", 4, 2), unp("o_ffn_p", 44, 2),
            uns("o_pool_s", 4, 15), uns("o_conv_s", 4, 2), uns("o_ffn_s", 44, 2))
    return tuple(np.ascontiguousarray(o.astype(np.float32)) for o in outs)
```

```python
from contextlib import ExitStack

import numpy as np
import concourse.bass as bass
import concourse.mybir as mybir
from concourse.bass_utils import run_bass_kernel_spmd

F32 = mybir.dt.float32
BF16 = mybir.dt.bfloat16
AF = mybir.ActivationFunctionType
ALU = mybir.AluOpType

D = 1024
NCORES = 8
SEQ = 2048
NMETA = 16
NSEQ = 16
DSEQ = 4
NPAD = 4
SOFF = NMETA + NPAD
NSM = SOFF + NSEQ * DSEQ
NQ = NSM // 4
DFF = 2816
NFF = 22
EPS = 1e-6
NST = 4
TW = 512
WINDOWS = (2, 4, 8, 16)
NW = 4
DEBUG = False
WARM_S2, WARM_S3A, WARM_S3B, WARM_S1 = 6, 12, 6, 6

C_G1, C_G2, C_PS, C_CW, C_FW, C_FB, NCST = 0, 8, 16, 20, 32, 164, 208

ENGS = ("pe", "act", "dve", "pool", "sp")
SAME_ENG_DIST = 1 << 30


class Op:
    __slots__ = ("eng", "fn", "deps", "dsem", "idx", "pos", "sig", "signal")

    def __init__(self, eng, fn, deps, dsem, idx):
        self.eng = eng
        self.fn = fn
        self.deps = deps
        self.dsem = dsem
        self.idx = idx
        self.pos = None
        self.sig = None
        self.signal = False


class Prog:
    def __init__(self):
        self.ops = []
        self.last_writer = {}
        self.readers = {}

    def op(self, eng, fn, reads=(), writes=(), dsem=None):
        deps = set()
        for k in reads:
            w = self.last_writer.get(k)
            if w is not None:
                deps.add(w)
        for k in writes:
            w = self.last_writer.get(k)
            if w is not None:
                deps.add(w)
            deps.update(self.readers.get(k, ()))
        idx = len(self.ops)
        self.ops.append(Op(eng, fn, deps, dsem, idx))
        for k in reads:
            self.readers.setdefault(k, []).append(idx)
        for k in writes:
            self.last_writer[k] = idx
            self.readers[k] = []
        return idx

    def emit(self, nc, final_wait_ops=()):
        ops = self.ops
        streams = {e: [] for e in ENGS}
        for o in ops:
            o.pos = len(streams[o.eng])
            streams[o.eng].append(o)
        for o in ops:
            for d in o.deps:
                p = ops[d]
                if p.dsem is not None:
                    p.signal = True
                elif p.eng != o.eng:
                    p.signal = True
                elif o.eng != "pe" and o.pos - p.pos <= SAME_ENG_DIST:
                    p.signal = True
        for d in final_wait_ops:
            ops[d].signal = True
        cnt = {}
        dsems = []
        for o in ops:
            if o.dsem is not None:
                if o.dsem not in cnt:
                    dsems.append(o.dsem)
                cnt[o.dsem] = cnt.get(o.dsem, 0) + 16
                o.sig = (o.dsem, cnt[o.dsem])
            elif o.signal:
                k = "E_" + o.eng
                cnt[k] = cnt.get(k, 0) + 1
                o.sig = (k, cnt[k])
        with ExitStack() as es:
            sems = {}
            for k in ["E_" + e for e in ENGS] + dsems:
                sems[k] = es.enter_context(nc.semaphore(k))
            block = es.enter_context(nc.Block())

            def run_stream(ename, eng):
                waited = {}
                for o in streams[ename]:
                    need = {}
                    for d in o.deps:
                        p = ops[d]
                        if p.sig is None:
                            continue
                        if p.dsem is None and p.eng == o.eng and (o.eng == "pe" or o.pos - p.pos > SAME_ENG_DIST):
                            continue
                        s, v = p.sig
                        if need.get(s, 0) < v:
                            need[s] = v
                    for s, v in need.items():
                        if waited.get(s, 0) < v:
                            eng.wait_ge(sems[s], v)
                            waited[s] = v
                    ins = o.fn(eng)
                    if o.sig is not None:
                        ins.then_inc(sems[o.sig[0]], 16 if o.dsem is not None else 1)
                if ename == "sp":
                    for d in final_wait_ops:
                        s, v = ops[d].sig
                        if waited.get(s, 0) < v:
                            eng.wait_ge(sems[s], v)
                            waited[s] = v

            @block.tensor
            def _(e):
                run_stream("pe", e)

            @block.scalar
            def _(e):
                run_stream("act", e)

            @block.vector
            def _(e):
                run_stream("dve", e)

            @block.gpsimd
            def _(e):
                run_stream("pool", e)

            @block.sync
            def _(e):
                run_stream("sp", e)


def build_nc():
    nc = bass.Bass("TRN2", target_bir_lowering=False)

    def din(name, shape):
        return nc.dram_tensor(name, shape, F32, kind="ExternalInput").ap()

    def dout(name, shape):
        return nc.dram_tensor(name, shape, F32, kind="ExternalOutput").ap()

    xp = din("xp", [SEQ, D])
    xsm = din("xsm", [NSM, D])
    w_in = din("w_in", [D, 2048])
    w_up = din("w_up", [D, 2 * DFF])
    w_out = din("w_out", [D, D])
    w_down = din("w_down", [DFF, D])
    w_pool = din("w_pool", [512, 128])
    cst = din("cst", [128, NCST])
    gfin = din("gfin", [128, D])
    st_pool = din("st_pool", [128, 4 * NSEQ * 15])
    st_conv = din("st_conv", [128, 4 * NSEQ * 2])
    st_ffn = din("st_ffn", [128, 44 * NSEQ * 2])

    yp = dout("yp", [SEQ, D])
    ys = dout("ys", [NSEQ * DSEQ, D])
    o_pool_p = dout("o_pool_p", [128, 4 * 15])
    o_conv_p = dout("o_conv_p", [128, 4 * 2])
    o_ffn_p = dout("o_ffn_p", [128, 44 * 2])
    o_pool_s = dout("o_pool_s", [128, 4 * NSEQ * 15])
    o_conv_s = dout("o_conv_s", [128, 4 * NSEQ * 2])
    o_ffn_s = dout("o_ffn_s", [128, 44 * NSEQ * 2])

    P = Prog()
    finals = []
    if DEBUG:
        dbg_aT = nc.dram_tensor("dbg_aT", [128, 8 * TW], BF16, kind="ExternalOutput").ap()
        dbg_h2T = nc.dram_tensor("dbg_h2T", [128, 8 * TW], BF16, kind="ExternalOutput").ap()
        dbg_hT = nc.dram_tensor("dbg_hT", [128, 8 * TW], BF16, kind="ExternalOutput").ap()
        dbg_x1 = nc.dram_tensor("dbg_x1", [128, 4 * D], F32, kind="ExternalOutput").ap()
        dbg_actT = nc.dram_tensor("dbg_actT", [128, NFF * TW], BF16, kind="ExternalOutput").ap()

    with ExitStack() as es:
        def sb(name, shape, dt=F32):
            return es.enter_context(nc.sbuf_tensor(name, shape, dt))

        def psum(name, shape, dt=F32):
            return es.enter_context(nc.psum_tensor(name, shape, dt))

        wslot = [sb(f"wslot{i}", [128, 8, 256], BF16) for i in range(NW)]
        w_out_sb = sb("w_out_sb", [128, 8, D], BF16)
        w_down_sb = sb("w_down_sb", [128, NFF, D], BF16)
        w_pool_sb = sb("w_pool_sb", [128, 4, 128], BF16)
        cst_sb = sb("cst_sb", [128, NCST])
        gfin_sb = sb("gfin_sb", [128, D])
        ident = sb("ident", [128, 128], BF16)
        epst = sb("epst", [128, 1])
        invcnt = sb("invcnt", [128, 4, 16])
        x1 = sb("x1", [128, 4, D])
        xrot = [sb(f"xrot{i}", [128, D]) for i in range(2)]
        hbuf = [sb(f"hbuf{i}", [128, D], BF16) for i in range(2)]
        hb0 = sb("hb0", [128, D], BF16)
        hT = sb("hT", [128, 8, TW], BF16)
        aT = sb("aT", [128, 8, TW], BF16)
        actT = sb("actT", [128, NFF, TW], BF16)
        xs_sb = sb("xs_sb", [128, D])
        hs = sb("hs", [128, D], BF16)
        hTs = sb("hTs", [128, 8, NSM], BF16)
        aTs = sb("aTs", [128, 8, NSM], BF16)
        actTs = sb("actTs", [128, NFF, NSM], BF16)
        ss_all = sb("ss_all", [128, 64])
        rs_all = sb("rs_all", [128, 64])
        Dsm = [sb(f"Dsm{i}", [128, NSM], BF16) for i in range(4)]
        Usave = sb("Usave", [128, 4, 15])
        CVsave = sb("CVsave", [128, 4, 2])
        Esave = sb("Esave", [128, 44, 2])
        OFp = sb("OFp", [128, 44, 2])
        stp_sb = sb("stp_sb", [128, 4, NSEQ, 15])
        stc_sb = sb("stc_sb", [128, 4, NSEQ, 2])
        stf_sb = sb("stf_sb", [128, 44, NSEQ, 2])
        Um = [sb(f"Um{i}", [128, 31]) for i in range(2)]
        Us = [sb(f"Us{i}", [128, NSEQ, 19]) for i in range(2)]
        Am = [sb(f"Am{i}", [128, 31]) for i in range(2)]
        As = [sb(f"As{i}", [128, NSEQ, 19]) for i in range(2)]
        GCs = sb("GCs", [128, NSM])
        Xcv = [sb(f"Xcv{i}", [128, NQ, 6]) for i in range(2)]
        Tcs = [sb(f"Tcs{i}", [128, NQ, 4]) for i in range(2)]
        X3p = [sb(f"X3p{i}", [128, 2, NQ, 6]) for i in range(2)]
        T3p = [sb(f"T3p{i}", [128, 2, NQ, 4]) for i in range(2)]
        stfV = stf_sb[:].rearrange("p (h c) b t -> p h c b t", h=2)
        ARENA = 2 * 527 + 2 * 527 + 512 + 2 * 514 + 2 * 512 + 4 * 256
        arena = sb("arena", [128, ARENA])
        off = 0

        def carve(n):
            nonlocal off
            v = arena[:, off:off + n]
            off += n
            return v
        U = [carve(527) for _ in range(2)]
        AB = [carve(527) for _ in range(2)]
        GC = [carve(512)]
        Dbuf = [carve(256).bitcast(BF16) for _ in range(4)]
        CV = [carve(514) for _ in range(2)]
        TT = [carve(512) for _ in range(2)]
        assert off == ARENA
        NT1P = 5
        T1p = [arena[:, i * 1028:(i + 1) * 1028].rearrange("p (h n) -> p h n", h=2) for i in range(NT1P)]
        assert NT1P * 1028 <= ARENA
        EsV = Esave[:].rearrange("p (h c) t -> p h c t", h=2)

        G = [psum(f"G{i}", [128, TW]) for i in range(6)]
        TR = [psum(f"TR{i}", [128, 8, 128], BF16) for i in range(2)]
        NG = len(G)
        gctr = [0]
        trctr = [0]
        nctr = [0]

        gfree = list(range(NG))
        gref = {}

        def galloc(nref=1):
            assert gfree, "out of PSUM banks (program-order allocation)"
            i = gfree.pop(0)
            gref[i] = nref
            return i

        def gdone(i):
            gref[i] -= 1
            assert gref[i] >= 0
            if gref[i] == 0:
                gfree.append(i)

        def gk(g):
            return ("G", g)

        def gt(g):
            return ("Gt", g)

        def sub(g, k):
            return G[g][:, k * NSM:(k + 1) * NSM]

        S1T = ["arS1_dve", "arS1_pe", "arS1_pool"]
        S3T = ["arS3_act", "arS3_pool"]

        def cc(col):
            return cst_sb[:, col:col + 1]

        P.op("sp", lambda e: e.dma_start(out=cst_sb[:], in_=cst[:, :]), writes=["cst"], dsem="D_cst")
        P.op("sp", lambda e: e.dma_start(out=xs_sb[0:NSM, :], in_=xsm[:, :]), writes=["xs"], dsem="D_xs")

        P.op("pool", lambda e: e.memset(x1[:, 0, 0:128], 0.0), writes=[("x1", 0)])
        P.op("pool", lambda e: e.affine_select(out=x1[:, 0, 0:128], in_=x1[:, 0, 0:128], pattern=[[-1, 128]],
                                               compare_op=ALU.not_equal, fill=1.0, base=0, channel_multiplier=1),
             reads=[("x1", 0)], writes=[("x1", 0)])
        P.op("pool", lambda e: e.tensor_copy(out=ident[:], in_=x1[:, 0, 0:128]), reads=[("x1", 0)], writes=["ident"])

        def mk_consts(e):
            e.memset(epst[:], EPS)
            for g, w in enumerate(WINDOWS):
                e.memset(invcnt[:, g, w - 1:16], 1.0 / w)
                for t in range(w - 1):
                    e.memset(invcnt[:, g, t:t + 1], 1.0 / (t + 1))
            for i in range(2):
                e.memset(Um[i][:, 0:15], 0.0)
                e.memset(Xcv[i][:, 0, 0:2], 0.0)
            for i in range(4):
                e.memset(Dsm[i][:, NMETA:SOFF], 0.0)
            for i in range(2):
                ins = e.memset(X3p[i][:, :, 0, 0:2], 0.0)
            return ins
        P.op("pool", mk_consts, writes=["epst", "invcnt", "zeros"])
        P.op("sp", lambda e: e.dma_start(out=stp_sb[:].rearrange("p a b c -> p (a b c)"), in_=st_pool[:, :]),
             writes=[("stp", j) for j in range(4)], dsem="D_stp")
        P.op("sp", lambda e: e.dma_start(out=stc_sb[:].rearrange("p a b c -> p (a b c)"), in_=st_conv[:, :]),
             writes=[("stc", j) for j in range(4)], dsem="D_stc")
        P.op("sp", lambda e: e.dma_start(out=stf_sb[:].rearrange("p a b c -> p (a b c)"), in_=st_ffn[:, :]),
             writes=[("stf", c) for c in range(44)], dsem="D_stf")
        P.op("sp", lambda e: e.dma_start(out=gfin_sb[:], in_=gfin[:, :]), writes=["gfin"], dsem="D_gfin")

        S1_COL = {"u": 0, "gb": 512, "gc": 1024, "v": 1536}
        S1_PAIRS_G = ((("gc", 0), ("v", 0)), (("gb", 0), ("gc", 1)), (("u", 3), ("u", 2)), (("v", 1), ("gb", 1)),
                      (("gc", 2), ("v", 2)), (("u", 1), ("u", 0)), (("gb", 2), ("gc", 3)), (("v", 3), ("gb", 3)))
        wplan = []
        for s in range(NST):
            for pair in S1_PAIRS_G:
                (a, b) = [S1_COL[kind] + 128 * j for (kind, j) in pair]
                wplan.append([(0, w_in[:, a:a + 128]), (128, w_in[:, b:b + 128])])
            for j in range(NFF):
                wplan.append([(0, w_up[:, j * 128:(j + 1) * 128]), (128, w_up[:, DFF + j * 128:DFF + (j + 1) * 128])])
        wissued = [0]
        resident = []

        def res_dma(dst, src, key, name):
            resident.append((dst, src, key, name))
        res_dma(w_pool_sb[:], w_pool.rearrange("(g c) d -> c g d", c=128), "wpool", "D_wpool")
        for q in range(4):
            res_dma(w_out_sb[:, :, q * 256:(q + 1) * 256],
                    w_out[:, q * 256:(q + 1) * 256].rearrange("(k p) n -> p k n", p=128), ("wout", q), f"D_wout{q}")
        for q in range(11):
            res_dma(w_down_sb[:, 2 * q:2 * q + 2, :],
                    w_down[q * 256:(q + 1) * 256, :].rearrange("(k p) n -> p k n", p=128), ("wdown", q), f"D_wdown{q}")
        res_i = [0]

        def issue_resident(n=1):
            for _ in range(n):
                if res_i[0] < len(resident):
                    dst, src, key, name = resident[res_i[0]]
                    res_i[0] += 1
                    P.op("pool", lambda e, dst=dst, src=src: e.dma_start(out=dst, in_=src), writes=[key], dsem=name)

        def prefetch(upto):
            while wissued[0] <= min(upto, len(wplan) - 1):
                wi = wissued[0]
                slot = wi % NW
                for h, (c0, src) in enumerate(wplan[wi]):
                    P.op("pool", lambda e, slot=slot, c0=c0, src=src: e.dma_start(
                        out=wslot[slot][:, :, c0:c0 + 128], in_=src.rearrange("(k p) n -> p k n", p=128)),
                        writes=[("W", slot, h)], dsem=f"D_w{slot}_{h}")
                wissued[0] += 1
                if wi >= 1:
                    issue_resident(1)

        wuse = [0]

        def next_weights(ahead=0):
            wi = wuse[0]
            wuse[0] += 1
            prefetch(wi + NW - 1 - ahead)
            return wi % NW

        def norm_stats(src, nrows, junk, junkkey, srckeys):
            n = nctr[0]
            nctr[0] += 1
            P.op("act", lambda e: e.activation(out=junk, in_=src, func=AF.Square, accum_out=ss_all[0:nrows, n:n + 1]),
                 reads=srckeys, writes=[("ss", n), junkkey])
            P.op("act", lambda e: e.activation(out=rs_all[0:nrows, n:n + 1], in_=ss_all[0:nrows, n:n + 1], func=AF.Sqrt,
                                               scale=1.0 / D, bias=epst[0:nrows, :]),
                 reads=[("ss", n), "epst"], writes=[("rs", n)])
            P.op("dve", lambda e: e.reciprocal(out=rs_all[0:nrows, n:n + 1], in_=rs_all[0:nrows, n:n + 1]),
                 reads=[("rs", n)], writes=[("rs", n)])
            return n

        def norm_transpose(src, nrows, srckeys, hb, hk, dstT, col0, gcol, dstkeys):
            hbv = hb[0:nrows, :]
            n = norm_stats(src, nrows, hbv, hk, srckeys)
            P.op("dve", lambda e: e.tensor_scalar(out=hbv, in0=src, scalar1=rs_all[0:nrows, n:n + 1], scalar2=None,
                                                  op0=ALU.mult),
                 reads=list(srckeys) + [("rs", n)], writes=[hk])

            def part_b():
                tr = TR[trctr[0] % 2]
                trk = ("TR", trctr[0] % 2)
                trctr[0] += 1

                def do_tr(e):
                    for k in range(8):
                        ins = e.transpose(out=tr[:, k, 0:nrows], in_=hb[0:nrows, k * 128:(k + 1) * 128],
                                          identity=ident[0:nrows, 0:nrows])
                    return ins
                P.op("pe", do_tr, reads=[hk, "ident"], writes=[trk])
                P.op("dve", lambda e: e.tensor_tensor(
                    out=dstT[:, :, col0:col0 + nrows], in0=tr[:, :, 0:nrows],
                    in1=cst_sb[:, gcol:gcol + 8].unsqueeze(2).to_broadcast([128, 8, nrows]), op=ALU.mult),
                    reads=[trk, "cst"], writes=dstkeys)
            return part_b

        def v3(ap, b=4):
            return ap.rearrange("p (a b) -> p a b", b=b)

        def stage0_first():
            hbs = [(hb0, "hb0"), (hbuf[0], ("hb", 0)), (hbuf[1], ("hb", 1))]
            pbs = []
            for i in range(4):
                hb, hk = hbs[i % 3]
                pbs.append(norm_transpose(x1[:, i, :], 128, [("x1", i)], hb, hk, hT, i * 128, C_G1, [("hT", i)]))
                if i >= 1:
                    pbs.pop(0)()
            for pb in pbs:
                pb()

        def stage0_small():
            norm_transpose(xs_sb[0:NSM, :], NSM, ["xs"], hs, "hs", hTs, 0, C_G1, ["hTs"])()

        def x1_reload(s):
            for i in range(4):
                r0 = (s * 4 + i) * 128
                P.op("sp", lambda e, i=i, r0=r0: e.dma_start(out=x1[:, i, :], in_=xp[r0:r0 + 128, :]),
                     writes=[("x1", i)], dsem=f"D_x1_{i}")

        def mm_feat(slot, half, rhsT, rhs_keys, ncols, out_ap, g, nk=8):
            def f(e):
                for k in range(nk):
                    ins = e.matmul(out_ap, lhsT=wslot[slot][:, k, half * 128:(half + 1) * 128],
                                   rhs=rhsT[:, k, 0:ncols], start=(k == 0), stop=(k == nk - 1))
                return ins
            P.op("pe", f, reads=[("W", slot, half)] + rhs_keys, writes=[gk(g), gt(g)])

        S1_PAIRS = S1_PAIRS_G

        def stage1(s):
            small = (s == 0)
            hkeys = [("hT", i) for i in range(4)]
            deferred = []
            slots = {}
            for pi, pair in enumerate(S1_PAIRS):
                slot = next_weights()
                gsm = galloc(2) if small else None
                nd = []
                for (rdy, fn) in deferred:
                    if pi >= rdy:
                        fn()
                    else:
                        nd.append((rdy, fn))
                deferred[:] = nd
                for half, (kind, j) in enumerate(pair):
                    g = galloc()
                    mm_feat(slot, half, hT, hkeys, TW, G[g][:, :], g)
                    slots[(kind, j)] = (g, gsm, half)
                if small:
                    for half in range(2):
                        mm_feat(slot, half, hTs, ["hTs"], NSM, sub(gsm, half), gsm)
                for half, (kind, j) in enumerate(pair):
                    g = slots[(kind, j)][0]
                    if kind == "u":
                        rest = small_u(j, gsm, half) if small else None
                        rdy = (pi + 4 if pi < 4 else 99) if small else pi + 3
                        deferred.append((rdy, main_u(s, j, g)))
                        if rest is not None:
                            deferred.append((rdy, rest()))
                    elif kind == "v":
                        g_gc, gsm_gc, half_gc = slots[("gc", j)]
                        rest = small_cv(j, gsm_gc, half_gc, gsm, half) if small else None
                        main_cv(s, j, g_gc, g)
                        if rest is not None:
                            rest()
                    elif kind == "gb":
                        bj = j % 2
                        P.op("dve", lambda e, g=g, j=j, bj=bj: e.tensor_tensor(
                            out=aT[:, 4 + j, :], in0=G[g][:, :], in1=TT[bj], op=ALU.mult),
                            reads=[gk(g), ("TT", bj)], writes=[("aT", 4 + j, i) for i in range(4)] + [gt(g), "arS1_dve"])
                        gdone(g)
                        if small:
                            P.op("dve", lambda e, gsm=gsm, half=half, j=j, bj=bj: e.tensor_tensor(
                                out=v3(aTs[:, 4 + j, :]), in0=v3(sub(gsm, half)), in1=Tcs[bj][:], op=ALU.mult),
                                reads=[gk(gsm), ("Tcs", bj)], writes=[("aTs", 4 + j), gt(gsm)])
                            gdone(gsm)
            for (rdy, fn) in deferred:
                fn()

        def small_u(j, gsm, half):
            W = WINDOWS[j]
            b = j % 2
            um, us = Um[b], Us[b]
            sp_ap = sub(gsm, half)
            P.op("act", lambda e: e.activation(out=um[:, 15:31], in_=sp_ap[:, 0:16], func=AF.Copy),
                 reads=[gk(gsm), "zeros"], writes=[("Um", b), gt(gsm)])
            P.op("act", lambda e: e.activation(out=us[:, :, 15:19], in_=v3(sp_ap[:, SOFF:NSM]), func=AF.Copy),
                 reads=[gk(gsm)], writes=[("UsN", b), gt(gsm)])
            gdone(gsm)
            P.op("act", lambda e: e.activation(out=Usave[:, j, :], in_=um[:, 16:31], func=AF.Copy),
                 reads=[("Um", b)], writes=[("Usave", j)])

            def rest():
                P.op("pool", lambda e: e.tensor_copy(out=us[:, :, 0:15], in_=stp_sb[:, j, :, :]),
                     reads=[("stp", j)], writes=[("UsH", b)])
                P.op("pool", lambda e: e.tensor_copy(out=stp_sb[:, j, :, :], in_=us[:, :, 4:19]),
                     reads=[("UsH", b), ("UsN", b)], writes=[("stp", j)])
                cur_m, cur_s = um, us
                m = 1
                si = 0
                while m < W:
                    lo = 15 - (W - 2 * m)
                    dst_m, dst_s = Am[si % 2], As[si % 2]

                    def f(e, cur_m=cur_m, cur_s=cur_s, dst_m=dst_m, dst_s=dst_s, m=m, lo=lo):
                        e.tensor_tensor(out=dst_m[:, lo:31], in0=cur_m[:, lo:31], in1=cur_m[:, lo - m:31 - m], op=ALU.add)
                        return e.tensor_tensor(out=dst_s[:, :, lo:19], in0=cur_s[:, :, lo:19],
                                               in1=cur_s[:, :, lo - m:19 - m], op=ALU.add)
                    P.op("dve", f, reads=[("Um", b), ("UsH", b), ("UsN", b), "smscr"], writes=["smscr"])
                    cur_m, cur_s = dst_m, dst_s
                    m *= 2
                    si += 1
                P.op("dve", lambda e, cur_m=cur_m: e.tensor_tensor(out=cur_m[:, 15:31], in0=cur_m[:, 15:31],
                                                                 in1=invcnt[:, j, :], op=ALU.mult),
                     reads=["smscr", "invcnt"], writes=["smscr"])

                def comb(e, cur_m=cur_m, cur_s=cur_s):
                    e.tensor_tensor(out=Dsm[j][:, 0:16], in0=cur_m[:, 15:31], in1=um[:, 15:31], op=ALU.subtract)
                    return e.scalar_tensor_tensor(out=v3(Dsm[j][:, SOFF:NSM]), in0=cur_s[:, :, 15:19], scalar=1.0 / W,
                                                  in1=us[:, :, 15:19], op0=ALU.mult, op1=ALU.subtract)
                P.op("dve", comb, reads=["smscr", ("Um", b), ("UsN", b), "invcnt"], writes=["smscr", ("Dsm", j)])

                def later():
                    g2 = galloc()
                    so = sub(g2, 0)
                    P.op("pe", lambda e: e.matmul(so, lhsT=w_pool_sb[:, j, :], rhs=Dsm[j][:, :], start=True, stop=True),
                         reads=[("Dsm", j), "wpool"], writes=[gk(g2), gt(g2)])
                    P.op("act", lambda e: e.activation(out=aTs[:, j, :], in_=so, func=AF.Copy, scale=cc(C_PS + j)),
                         reads=[gk(g2), "cst"], writes=[("aTs", j), gt(g2)])
                    gdone(g2)
                return later
            return rest

        def main_u(s, j, g):
            W = WINDOWS[j]
            b = j % 2
            u = U[b]
            P.op("act", lambda e: e.activation(out=u[:, 0:15], in_=Usave[:, j, :], func=AF.Copy),
                 reads=[("Usave", j)] + S3T, writes=[("UH", b)])
            P.op("act", lambda e: e.activation(out=u[:, 15:527], in_=G[g][:, :], func=AF.Copy),
                 reads=[gk(g)] + S3T, writes=[("UN", b), gt(g)])
            gdone(g)
            P.op("act", lambda e: e.activation(out=Usave[:, j, :], in_=u[:, 512:527], func=AF.Copy),
                 reads=[("UN", b)], writes=[("Usave", j), "arS1_pool"])
            cur = u
            m = 1
            si = 0
            while m < W:
                lo = 15 - (W - 2 * m)
                dst = AB[si % 2]
                P.op("dve", lambda e, cur=cur, dst=dst, m=m, lo=lo: e.tensor_tensor(
                    out=dst[:, lo:527], in0=cur[:, lo:527], in1=cur[:, lo - m:527 - m], op=ALU.add),
                    reads=[("UH", b), ("UN", b), "AB"] + S3T, writes=["AB", "arS1_dve"])
                cur = dst
                m *= 2
                si += 1
            db = Dbuf[j]
            P.op("dve", lambda e, cur=cur: e.scalar_tensor_tensor(
                out=db[:, :], in0=cur[:, 15:527], scalar=1.0 / W, in1=u[:, 15:527], op0=ALU.mult, op1=ALU.subtract),
                reads=["AB", ("UN", b)] + S3T, writes=[("D", j), "arS1_dve"])

            def later():
                g2 = galloc()
                P.op("pe", lambda e: e.matmul(G[g2][:, :], lhsT=w_pool_sb[:, j, :], rhs=db[:, :], start=True, stop=True),
                     reads=[("D", j), "wpool"], writes=[gk(g2), gt(g2), "arS1_pe"])
                P.op("act", lambda e: e.activation(out=aT[:, j, :], in_=G[g2][:, :], func=AF.Copy, scale=cc(C_PS + j)),
                     reads=[gk(g2), "cst"], writes=[("aT", j, i) for i in range(4)] + [gt(g2)])
                gdone(g2)
            return later

        def small_cv(j, g_gc, h_gc, g_v, h_v):
            b = j % 2
            x = Xcv[b]
            t = Tcs[b]
            gc_ap = sub(g_gc, h_gc)
            v_ap = sub(g_v, h_v)
            P.op("act", lambda e: e.activation(out=GCs[:, :], in_=gc_ap, func=AF.Copy),
                 reads=[gk(g_gc)], writes=["GCs", gt(g_gc)])
            gdone(g_gc)
            P.op("dve", lambda e: e.tensor_tensor(out=x[:, :, 2:6], in0=v3(v_ap), in1=v3(GCs[:, :]), op=ALU.mult),
                 reads=[gk(g_v), "GCs"], writes=[("XcvN", b), gt(g_v)])
            gdone(g_v)
            P.op("act", lambda e: e.activation(out=CVsave[:, j, :], in_=x[:, 3, 4:6], func=AF.Copy),
                 reads=[("XcvN", b)], writes=[("CVsave", j)])

            def rest():
                def hist(e):
                    e.tensor_copy(out=x[:, 1:5, 0:2], in_=x[:, 0:4, 4:6])
                    return e.tensor_copy(out=x[:, 5:NQ, 0:2], in_=stc_sb[:, j, :, :])
                P.op("pool", hist, reads=[("XcvN", b), ("stc", j), "zeros"], writes=[("XcvH", b)])
                P.op("pool", lambda e: e.tensor_copy(out=stc_sb[:, j, :, :], in_=x[:, 5:NQ, 4:6]),
                     reads=[("XcvN", b), ("XcvH", b)], writes=[("stc", j)])
                P.op("act", lambda e: e.activation(out=t[:], in_=x[:, :, 0:4], func=AF.Copy, scale=cc(C_CW + 3 * j)),
                     reads=[("XcvN", b), ("XcvH", b), "cst"], writes=[("Tcs", b)])
                P.op("dve", lambda e: e.scalar_tensor_tensor(out=t[:], in0=x[:, :, 1:5], scalar=cc(C_CW + 3 * j + 1),
                                                              in1=t[:], op0=ALU.mult, op1=ALU.add),
                     reads=[("XcvN", b), ("XcvH", b), ("Tcs", b)], writes=[("Tcs", b)])
                P.op("dve", lambda e: e.scalar_tensor_tensor(out=t[:], in0=x[:, :, 2:6], scalar=cc(C_CW + 3 * j + 2),
                                                              in1=t[:], op0=ALU.mult, op1=ALU.add),
                     reads=[("XcvN", b), ("Tcs", b)], writes=[("Tcs", b)])
            return rest

        def main_cv(s, j, g_gc, g_v):
            b = j % 2
            cv = CV[b]
            t = TT[b]
            gcb = GC[0]
            P.op("act", lambda e: e.activation(out=gcb, in_=G[g_gc][:, :], func=AF.Copy),
                 reads=[gk(g_gc)] + S3T, writes=["GC", gt(g_gc)])
            gdone(g_gc)
            P.op("act", lambda e: e.activation(out=cv[:, 0:2], in_=CVsave[:, j, :], func=AF.Copy),
                 reads=[("CVsave", j)] + S3T, writes=[("CVH", b)])
            P.op("dve", lambda e: e.tensor_tensor(out=cv[:, 2:514], in0=G[g_v][:, :], in1=gcb, op=ALU.mult),
                 reads=[gk(g_v), "GC"] + S3T, writes=[("CVN", b), gt(g_v), "arS1_dve"])
            gdone(g_v)
            P.op("act", lambda e: e.activation(out=CVsave[:, j, :], in_=cv[:, 512:514], func=AF.Copy),
                 reads=[("CVN", b)], writes=[("CVsave", j), "arS1_pool"])
            P.op("act", lambda e: e.activation(out=t, in_=cv[:, 0:512], func=AF.Copy, scale=cc(C_CW + 3 * j)),
                 reads=[("CVN", b), ("CVH", b), "cst"] + S3T, writes=[("TT", b)])
            P.op("dve", lambda e: e.scalar_tensor_tensor(out=t, in0=cv[:, 1:513], scalar=cc(C_CW + 3 * j + 1), in1=t,
                                                          op0=ALU.mult, op1=ALU.add),
                 reads=[("CVN", b), ("CVH", b), ("TT", b)], writes=[("TT", b)])
            P.op("dve", lambda e: e.scalar_tensor_tensor(out=t, in0=cv[:, 2:514], scalar=cc(C_CW + 3 * j + 2), in1=t,
                                                          op0=ALU.mult, op1=ALU.add),
                 reads=[("CVN", b), ("TT", b)], writes=[("TT", b), "arS1_dve"])

        def mm_tok(lhsT_of_k, lkey_of_k, nrows, wsb, wkeys, korder, ngrp, ga, gb):
            nk = len(korder)
            per = (nk + ngrp - 1) // ngrp
            for gi in range(0, nk, per):
                ks = korder[gi:gi + per]

                def f(e, ks=ks, gi=gi):
                    for n, k in enumerate(ks):
                        first = (gi + n == 0)
                        lastk = (gi + n == nk - 1)
                        e.matmul(G[ga][0:nrows, :], lhsT=lhsT_of_k(k), rhs=wsb[:, k, 0:512], start=first, stop=lastk)
                        ins = e.matmul(G[gb][0:nrows, :], lhsT=lhsT_of_k(k), rhs=wsb[:, k, 512:1024],
                                       start=first, stop=lastk)
                    return ins
                P.op("pe", f, reads=[lkey_of_k(k) for k in ks] + list(wkeys), writes=[gk(ga), gk(gb), gt(ga), gt(gb)])

        def resid(dst, nrows, dkey, ga, gb):
            def f(e):
                e.tensor_tensor(out=dst[0:nrows, 0:512], in0=G[ga][0:nrows, :], in1=dst[0:nrows, 0:512], op=ALU.add)
                return e.tensor_tensor(out=dst[0:nrows, 512:1024], in0=G[gb][0:nrows, :], in1=dst[0:nrows, 512:1024],
                                       op=ALU.add)
            P.op("dve", f, reads=[gk(ga), gk(gb), dkey], writes=[dkey, gt(ga), gt(gb)])
            gdone(ga)
            gdone(gb)

        S2_KORDER = [4, 5, 6, 0, 1, 2, 3, 7]
        WOUT_KEYS = [("wout", q) for q in range(4)]
        WDOWN_KEYS = [("wdown", q) for q in range(11)]

        def warm(n):
            if n <= 0:
                return
            g = galloc()

            def f(e):
                for _ in range(n):
                    ins = e.matmul(G[g][:, :], lhsT=w_out_sb[:, 0, 0:128], rhs=w_out_sb[:, 0, 0:512],
                                   start=True, stop=True)
                return ins
            P.op("pe", f, reads=[("wout", 0), ("wout", 1)], writes=[gk(g), gt(g)])
            gdone(g)

        def stage2(s):
            small = (s == 0)
            pend = []
            warm(WARM_S2)

            def do_small():
                ga, gb = galloc(), galloc()
                mm_tok(lambda k: aTs[:, k, 0:NSM], lambda k: ("aTs", k), NSM, w_out_sb, WOUT_KEYS, S2_KORDER, 4, ga, gb)
                resid(xs_sb, NSM, "xs", ga, gb)
                pbs = norm_transpose(xs_sb[0:NSM, :], NSM, ["xs"], hs, "hs", aTs, 0, C_G2,
                                     [("aTs", k) for k in range(8)])

                def pbs2(pbs=pbs):
                    pbs()
                    P.op("pool", lambda e: e.memset(aTs[:, :, NMETA:SOFF], 0.0), writes=[("aTs", k) for k in range(8)])
                return pbs2
            for i in range(4):
                ga, gb = galloc(), galloc()
                mm_tok(lambda k, i=i: aT[:, k, i * 128:(i + 1) * 128], lambda k, i=i: ("aT", k, i), 128,
                       w_out_sb, WOUT_KEYS, S2_KORDER, 4 if i == 0 else 1, ga, gb)
                resid(x1[:, i, :], 128, ("x1", i), ga, gb)
                while pend:
                    pend.pop(0)()
                pend.append(norm_transpose(x1[:, i, :], 128, [("x1", i)], hbuf[i % 2], ("hb", i % 2), aT, i * 128, C_G2,
                                           [("aT", k, i) for k in range(8)]))
                if small and i == 1:
                    pend.append(do_small())
            return pend

        t1ctr = [0]
        x3ctr = [0]

        def stage3(s):
            small = (s == 0)
            last = (s == NST - 1)
            akeys = [("aT", k, i) for k in range(8) for i in range(4)]
            askeys = [("aTs", k) for k in range(8)]
            prev_tail = None
            slots3 = {}
            LA = 2

            def get_slot(j, ahead=0):
                if j not in slots3:
                    slots3[j] = next_weights(ahead)
                return slots3[j]

            def small_mm(j):
                slot = get_slot(j, LA - 1 if j >= LA else j)
                gsm = galloc()
                for half in range(2):
                    mm_feat(slot, half, aTs, askeys, NSM, sub(gsm, half), gsm)
                return gsm
            if small:
                for jj in range(LA):
                    small_A(jj, small_mm(jj))
                small_B(0)
            for j in range(NFF):
                slot = get_slot(j)
                gs = []
                for half in range(2):
                    g = galloc()
                    gs.append(g)
                    mm_feat(slot, half, aT, akeys, TW, G[g][:, :], g)
                gsm_next = small_mm(j + LA) if (small and j + LA < NFF) else None
                if small and j + 1 < NFF:
                    small_B(j + 1)
                ti = t1ctr[0] % NT1P
                t1ctr[0] += 1
                tp = T1p[ti]
                kh = [("T1h", ti, h) for h in range(2)]
                kb = [("T1b", ti, h) for h in range(2)]
                kt = [("T1t", ti, h) for h in range(2)]
                cs = (j, NFF + j)
                ek = [("Esave", c) for c in cs]
                P.op("act", lambda e, tp=tp, j=j: e.activation(out=tp[:, :, 0:2], in_=EsV[:, :, j, :], func=AF.Copy),
                     reads=ek + S1T, writes=kh)
                for h in range(2):
                    c, g = cs[h], gs[h]
                    P.op("act", lambda e, tp=tp, g=g, c=c, h=h: e.activation(
                        out=tp[:, h, 2:514], in_=G[g][:, :], func=AF.Identity, scale=cc(C_FW + 3 * c), bias=cc(C_FB + c)),
                        reads=[gk(g), "cst"] + S1T, writes=[kb[h], kt[h], gt(g)])
                if prev_tail is not None:
                    prev_tail()
                if gsm_next is not None:
                    small_A(j + LA, gsm_next)
                for h in range(2):
                    c, g = cs[h], gs[h]
                    P.op("dve", lambda e, tp=tp, g=g, c=c, h=h: e.scalar_tensor_tensor(
                        out=tp[:, h, 1:513], in0=G[g][:, :], scalar=cc(C_FW + 3 * c + 1), in1=tp[:, h, 1:513],
                        op0=ALU.mult, op1=ALU.add),
                        reads=[gk(g), kh[h], kb[h], kt[h], "cst"], writes=[kh[h], kb[h], kt[h], gt(g)])
                for h in range(2):
                    c, g = cs[h], gs[h]
                    if last:
                        P.op("dve", lambda e, g=g, c=c: e.tensor_copy(out=OFp[:, c, :], in_=G[g][:, 510:512]),
                             reads=[gk(g)], writes=[("OFp", c), gt(g)])
                    P.op("dve", lambda e, tp=tp, g=g, c=c, h=h: e.scalar_tensor_tensor(
                        out=tp[:, h, 0:512], in0=G[g][:, :], scalar=cc(C_FW + 3 * c + 2), in1=tp[:, h, 0:512],
                        op0=ALU.mult, op1=ALU.add),
                        reads=[gk(g), kh[h], kb[h], "cst"], writes=[kh[h], kb[h], gt(g)])
                    gdone(g)

                def tail(tp=tp, j=j, kh=kh, kb=kb, kt=kt, ek=ek):
                    P.op("act", lambda e: e.activation(out=EsV[:, :, j, :], in_=tp[:, :, 512:514], func=AF.Copy),
                         reads=kt, writes=ek + ["arS3_act"])
                    P.op("act", lambda e: e.activation(out=tp[:, 1, 0:512], in_=tp[:, 1, 0:512], func=AF.Silu),
                         reads=[kh[1], kb[1]], writes=[kh[1], kb[1]])
                    P.op("pool", lambda e: e.tensor_tensor(out=actT[:, j, :], in0=tp[:, 1, 0:512], in1=tp[:, 0, 0:512],
                                                           op=ALU.mult),
                         reads=kh + kb, writes=[("actT", j, i) for i in range(4)] + ["arS3_pool"])
                prev_tail = tail
                if small:
                    small_C(j)
            prev_tail()

        def small_bufs(j):
            xi = j % 2
            return X3p[xi], T3p[xi], ("X3N", xi), ("X3H", xi), ("T3", xi), (j, NFF + j)

        def small_A(j, gsm):
            x, t, kx, kxh, ktt, cs = small_bufs(j)
            P.op("act", lambda e: e.activation(out=x[:, :, :, 2:6],
                                               in_=G[gsm][:, 0:2 * NSM].rearrange("p (h q t) -> p h q t", h=2, t=4),
                                               func=AF.Copy),
                 reads=[gk(gsm)], writes=[kx, gt(gsm)])
            gdone(gsm)

            def hist(e):
                e.tensor_copy(out=x[:, :, 1:5, 0:2], in_=x[:, :, 0:4, 4:6])
                return e.tensor_copy(out=x[:, :, 5:NQ, 0:2], in_=stfV[:, :, j, :, :])
            P.op("pool", hist, reads=[kx, ("stf", cs[0]), ("stf", cs[1]), "zeros"], writes=[kxh])
            P.op("pool", lambda e: e.tensor_copy(out=stfV[:, :, j, :, :], in_=x[:, :, 5:NQ, 4:6]),
                 reads=[kx, kxh], writes=[("stf", cs[0]), ("stf", cs[1])])

        def small_B(j):
            x, t, kx, kxh, ktt, cs = small_bufs(j)
            for h in range(2):
                c = cs[h]
                P.op("act", lambda e, h=h, c=c: e.activation(out=t[:, h], in_=x[:, h, :, 0:4], func=AF.Identity,
                                                             scale=cc(C_FW + 3 * c), bias=cc(C_FB + c)),
                     reads=[kx, kxh, "cst"], writes=[(ktt, h)])
            for tap in (1, 2):
                for h in range(2):
                    c = cs[h]
                    P.op("dve", lambda e, h=h, c=c, tap=tap: e.scalar_tensor_tensor(
                        out=t[:, h], in0=x[:, h, :, tap:tap + 4], scalar=cc(C_FW + 3 * c + tap), in1=t[:, h],
                        op0=ALU.mult, op1=ALU.add),
                        reads=[kx, kxh, (ktt, h), "cst"], writes=[(ktt, h)])
            P.op("pool", lambda e: e.tensor_copy(out=EsV[:, :, j, :], in_=t[:, :, NMETA // 4, 0:2]),
                 reads=[(ktt, 0), (ktt, 1)], writes=[("Esave", cs[0]), ("Esave", cs[1])])

        def small_C(j):
            x, t, kx, kxh, ktt, cs = small_bufs(j)
            P.op("act", lambda e: e.activation(out=t[:, 1], in_=t[:, 1], func=AF.Silu),
                 reads=[(ktt, 1)], writes=[(ktt, 1)])
            P.op("dve", lambda e: e.tensor_tensor(out=v3(actTs[:, j, :]), in0=t[:, 1], in1=t[:, 0], op=ALU.mult),
                 reads=[(ktt, 0), (ktt, 1)], writes=[("actTs", j)])

        def final_norm_store(dst, nrows, dkey, junk, junkkey, out_ap, src_rows, dsem):
            n = norm_stats(dst[0:nrows, :], nrows, junk, junkkey, [dkey])
            P.op("dve", lambda e: e.scalar_tensor_tensor(
                out=dst[0:nrows, :], in0=dst[0:nrows, :], scalar=rs_all[0:nrows, n:n + 1], in1=gfin_sb[0:nrows, :],
                op0=ALU.mult, op1=ALU.mult), reads=[dkey, ("rs", n), "gfin"], writes=[dkey])
            lo, hi = src_rows
            finals.append(P.op("sp", lambda e: e.dma_start(out=out_ap, in_=dst[lo:hi, :]), reads=[dkey], dsem=dsem))

        S4_KORDER = list(range(NFF))

        def stage0_tile(sn, i):
            gti = sn * 4 + i
            xr = xrot[gti % 2]
            xk = ("xrot", gti % 2)
            r0 = gti * 128
            P.op("sp", lambda e: e.dma_start(out=xr[:], in_=xp[r0:r0 + 128, :]), writes=[xk], dsem=f"D_xrot{gti % 2}")
            return norm_transpose(xr[:], 128, [xk], hb0, "hb0", hT, i * 128, C_G1, [("hT", i)])

        def stage4(s):
            small = (s == 0)
            nxt = s + 1 if s + 1 < NST else None
            def do_small4():
                ga, gb = galloc(), galloc()
                mm_tok(lambda k: actTs[:, k, 0:NSM], lambda k: ("actTs", k), NSM, w_down_sb, WDOWN_KEYS,
                       S4_KORDER, 3, ga, gb)
                resid(xs_sb, NSM, "xs", ga, gb)
                final_norm_store(xs_sb, NSM, "xs", hs[0:NSM, :], "hs", ys[:, :], (SOFF, NSM), "D_ys")
            pb = stage0_tile(nxt, 0) if nxt is not None else None
            for i in range(4):
                ga, gb = galloc(), galloc()
                mm_tok(lambda k, i=i: actT[:, k, i * 128:(i + 1) * 128], lambda k, i=i: ("actT", k, i), 128,
                       w_down_sb, WDOWN_KEYS, S4_KORDER, 4 if i == 0 else 1, ga, gb)
                if pb is not None:
                    pb()
                    pb = stage0_tile(nxt, i + 1) if i < 3 else None
                    if i == 3:
                        warm(WARM_S1)
                resid(x1[:, i, :], 128, ("x1", i), ga, gb)
                r0 = (s * 4 + i) * 128
                final_norm_store(x1[:, i, :], 128, ("x1", i), hbuf[i % 2][:, :], ("hb", i % 2), yp[r0:r0 + 128, :],
                                 (0, 128), f"D_x1_{i}")
                if small and i == 1:
                    do_small4()

        prefetch(NW - 2)
        x1_reload(0)
        stage0_small()
        stage0_first()
        for s in range(NST):
            if s > 0:
                x1_reload(s)
            if DEBUG and s == 0:
                finals.append(P.op("sp", lambda e: e.dma_start(out=dbg_hT[:, :], in_=hT[:].rearrange("p a b -> p (a b)")),
                                   reads=[("hT", i) for i in range(4)], dsem="D_dbg0"))
            stage1(s)
            if DEBUG and s == 0:
                finals.append(P.op("sp", lambda e: e.dma_start(out=dbg_aT[:, :], in_=aT[:].rearrange("p a b -> p (a b)")),
                                   reads=[("aT", k, i) for k in range(8) for i in range(4)], dsem="D_dbg1"))
            pend = stage2(s)
            warm(WARM_S3A)
            for fn in pend:
                fn()
            warm(WARM_S3B)
            if DEBUG and s == 0:
                finals.append(P.op("sp", lambda e: e.dma_start(out=dbg_h2T[:, :], in_=aT[:].rearrange("p a b -> p (a b)")),
                                   reads=[("aT", k, i) for k in range(8) for i in range(4)], dsem="D_dbg2"))
                finals.append(P.op("sp", lambda e: e.dma_start(out=dbg_x1[:, :], in_=x1[:].rearrange("p a b -> p (a b)")),
                                   reads=[("x1", i) for i in range(4)], dsem="D_dbg3"))
            stage3(s)
            if DEBUG and s == 0:
                finals.append(P.op("sp", lambda e: e.dma_start(out=dbg_actT[:, :], in_=actT[:].rearrange("p a b -> p (a b)")),
                                   reads=[("actT", j, i) for j in range(NFF) for i in range(4)], dsem="D_dbg4"))
            stage4(s)
        while res_i[0] < len(resident):
            issue_resident(1)
        finals.append(P.op("sp", lambda e: e.dma_start(out=o_pool_p[:, :], in_=Usave[:].rearrange("p a b -> p (a b)")),
                           reads=[("Usave", j) for j in range(4)], dsem="D_o1"))
        finals.append(P.op("sp", lambda e: e.dma_start(out=o_conv_p[:, :], in_=CVsave[:].rearrange("p a b -> p (a b)")),
                           reads=[("CVsave", j) for j in range(4)], dsem="D_o2"))
        finals.append(P.op("sp", lambda e: e.dma_start(out=o_ffn_p[:, :], in_=OFp[:].rearrange("p a b -> p (a b)")),
                           reads=[("OFp", c) for c in range(44)], dsem="D_o3"))
        finals.append(P.op("sp", lambda e: e.dma_start(out=o_pool_s[:, :], in_=stp_sb[:].rearrange("p a b c -> p (a b c)")),
                           reads=[("stp", j) for j in range(4)], dsem="D_o4"))
        finals.append(P.op("sp", lambda e: e.dma_start(out=o_conv_s[:, :], in_=stc_sb[:].rearrange("p a b c -> p (a b c)")),
                           reads=[("stc", j) for j in range(4)], dsem="D_o5"))
        finals.append(P.op("sp", lambda e: e.dma_start(out=o_ffn_s[:, :], in_=stf_sb[:].rearrange("p a b c -> p (a b c)")),
                           reads=[("stf", c) for c in range(44)], dsem="D_o6"))
        P.emit(nc, final_wait_ops=finals)
    return nc


def _col(v, n):
    return np.ascontiguousarray(np.asarray(v, np.float32).reshape(n, 128).T)


_NC_CACHE = {}


def kernel(x_prompt, x_sample, state_pool, state_conv, state_ffn, meta_tokens,
           norm_mix, w_in, w_pool, pool_scale, conv_w, w_out,
           norm_ffn, w_up, ffn_conv_w, ffn_conv_b, w_down, norm_final):
    f = lambda a: np.ascontiguousarray(np.asarray(a, dtype=np.float32))
    x_prompt, x_sample = f(x_prompt), f(x_sample)
    state_pool, state_conv, state_ffn = f(state_pool), f(state_conv), f(state_ffn)
    meta_tokens = f(meta_tokens)
    cst = np.zeros((128, NCST), np.float32)
    cst[:, C_G1:C_G1 + 8] = _col(norm_mix[0], 8)
    cst[:, C_G2:C_G2 + 8] = _col(norm_ffn[0], 8)
    cst[:, C_PS:C_PS + 4] = _col(pool_scale[0], 4)
    cw = f(conv_w)[0]
    cst[:, C_CW:C_CW + 12] = cw.reshape(3, 4, 128).transpose(2, 1, 0).reshape(128, 12)
    fw = f(ffn_conv_w)[0]
    cst[:, C_FW:C_FW + 132] = fw.reshape(3, 44, 128).transpose(2, 1, 0).reshape(128, 132)
    cst[:, C_FB:C_FB + 44] = _col(f(ffn_conv_b)[0], 44)
    gfin = np.ascontiguousarray(np.broadcast_to(f(norm_final)[None, :], (128, D)))
    shared = {
        "w_in": f(w_in)[0], "w_up": f(w_up)[0], "w_out": f(w_out)[0], "w_down": f(w_down)[0],
        "w_pool": f(w_pool)[0].reshape(512, 128), "cst": cst, "gfin": gfin,
    }
    in_maps = []
    for c in range(NCORES):
        sl = slice(c * NSEQ, (c + 1) * NSEQ)
        m = dict(shared)
        m["xp"] = x_prompt[c]
        m["xsm"] = np.ascontiguousarray(np.concatenate(
            [meta_tokens, np.zeros((NPAD, D), np.float32), x_sample[sl].reshape(NSEQ * DSEQ, D)], axis=0))
        m["st_pool"] = np.ascontiguousarray(
            state_pool[0, sl].reshape(NSEQ, 15, 4, 128).transpose(3, 2, 0, 1).reshape(128, -1))
        m["st_conv"] = np.ascontiguousarray(
            state_conv[0, sl].reshape(NSEQ, 2, 4, 128).transpose(3, 2, 0, 1).reshape(128, -1))
        m["st_ffn"] = np.ascontiguousarray(
            state_ffn[0, sl].reshape(NSEQ, 2, 44, 128).transpose(3, 2, 0, 1).reshape(128, -1))
        in_maps.append(m)
    if "nc" not in _NC_CACHE:
        _NC_CACHE["nc"] = build_nc()
    nc = _NC_CACHE["nc"]
    res = run_bass_kernel_spmd(nc, in_maps, core_ids=list(range(NCORES)))
    R = res.results
    if DEBUG:
        _NC_CACHE["dbg"] = R[0]
    y_prompt = np.stack([R[c]["yp"] for c in range(NCORES)], axis=0)
    y_sample = np.concatenate([R[c]["ys"].reshape(NSEQ, DSEQ, D) for c in range(NCORES)], axis=0)

    def unp(name, nch, rows):
        return np.stack([R[c][name].reshape(128, nch, rows).transpose(2, 1, 0).reshape(rows, nch * 128)
                         for c in range(NCORES)], axis=0)[None]

    def uns(name, nch, rows):
        return np.concatenate([R[c][name].reshape(128, nch, NSEQ, rows).transpose(2, 3, 1, 0).reshape(NSEQ, rows, nch * 128)
                               for c in range(NCORES)], axis=0)[None]
    outs = (y_prompt, y_sample,
            unp("o_pool_p", 4, 15), unp("o_conv_p", 4, 2), unp("o_ffn_p", 44, 2),
            uns("o_pool_s", 4, 15), uns("o_conv_s", 4, 2), uns("o_ffn_s", 44, 2))
    return tuple(np.ascontiguousarray(o.astype(np.float32)) for o in outs)
```

```python
from contextlib import ExitStack

import numpy as np
import concourse.bass as bass
import concourse.mybir as mybir
from concourse.bass_utils import run_bass_kernel_spmd

F32 = mybir.dt.float32
BF16 = mybir.dt.bfloat16
AF = mybir.ActivationFunctionType
ALU = mybir.AluOpType

D = 1024
NCORES = 8
SEQ = 2048
NMETA = 16
NSEQ = 16
DSEQ = 4
NPAD = 4
SOFF = NMETA + NPAD
NSM = SOFF + NSEQ * DSEQ
NQ = NSM // 4
DFF = 2816
NFF = 22
EPS = 1e-6
NST = 4
TW = 512
WINDOWS = (2, 4, 8, 16)
NW = 4
DEBUG = False
WARM_S2, WARM_S3A, WARM_S3B, WARM_S1 = 6, 12, 6, 6

C_G1, C_G2, C_PS, C_CW, C_FW, C_FB, NCST = 0, 8, 16, 20, 32, 164, 208

ENGS = ("pe", "act", "dve", "pool", "sp")
SAME_ENG_DIST = 1 << 30


class Op:
    __slots__ = ("eng", "fn", "deps", "dsem", "idx", "pos", "sig", "signal")

    def __init__(self, eng, fn, deps, dsem, idx):
        self.eng = eng
        self.fn = fn
        self.deps = deps
        self.dsem = dsem
        self.idx = idx
        self.pos = None
        self.sig = None
        self.signal = False


class Prog:
    def __init__(self):
        self.ops = []
        self.last_writer = {}
        self.readers = {}

    def op(self, eng, fn, reads=(), writes=(), dsem=None):
        deps = set()
        for k in reads:
            w = self.last_writer.get(k)
            if w is not None:
                deps.add(w)
        for k in writes:
            w = self.last_writer.get(k)
            if w is not None:
                deps.add(w)
            deps.update(self.readers.get(k, ()))
        idx = len(self.ops)
        self.ops.append(Op(eng, fn, deps, dsem, idx))
        for k in reads:
            self.readers.setdefault(k, []).append(idx)
        for k in writes:
            self.last_writer[k] = idx
            self.readers[k] = []
        return idx

    def emit(self, nc, final_wait_ops=()):
        ops = self.ops
        streams = {e: [] for e in ENGS}
        for o in ops:
            o.pos = len(streams[o.eng])
            streams[o.eng].append(o)
        for o in ops:
            for d in o.deps:
                p = ops[d]
                if p.dsem is not None:
                    p.signal = True
                elif p.eng != o.eng:
                    p.signal = True
                elif o.eng != "pe" and o.pos - p.pos <= SAME_ENG_DIST:
                    p.signal = True
        for d in final_wait_ops:
            ops[d].signal = True
        cnt = {}
        dsems = []
        for o in ops:
            if o.dsem is not None:
                if o.dsem not in cnt:
                    dsems.append(o.dsem)
                cnt[o.dsem] = cnt.get(o.dsem, 0) + 16
                o.sig = (o.dsem, cnt[o.dsem])
            elif o.signal:
                k = "E_" + o.eng
                cnt[k] = cnt.get(k, 0) + 1
                o.sig = (k, cnt[k])
        with ExitStack() as es:
            sems = {}
            for k in ["E_" + e for e in ENGS] + dsems:
                sems[k] = es.enter_context(nc.semaphore(k))
            block = es.enter_context(nc.Block())

            def run_stream(ename, eng):
                waited = {}
                for o in streams[ename]:
                    need = {}
                    for d in o.deps:
                        p = ops[d]
                        if p.sig is None:
                            continue
                        if p.dsem is None and p.eng == o.eng and (o.eng == "pe" or o.pos - p.pos > SAME_ENG_DIST):
                            continue
                        s, v = p.sig
                        if need.get(s, 0) < v:
                            need[s] = v
                    for s, v in need.items():
                        if waited.get(s, 0) < v:
                            eng.wait_ge(sems[s], v)
                            waited[s] = v
                    ins = o.fn(eng)
                    if o.sig is not None:
                        ins.then_inc(sems[o.sig[0]], 16 if o.dsem is not None else 1)
                if ename == "sp":
                    for d in final_wait_ops:
                        s, v = ops[d].sig
                        if waited.get(s, 0) < v:
                            eng.wait_ge(sems[s], v)
                            waited[s] = v

            @block.tensor
            def _(e):
                run_stream("pe", e)

            @block.scalar
            def _(e):
                run_stream("act", e)

            @block.vector
            def _(e):
                run_stream("dve", e)

            @block.gpsimd
            def _(e):
                run_stream("pool", e)

            @block.sync
            def _(e):
                run_stream("sp", e)


def build_nc():
    nc = bass.Bass("TRN2", target_bir_lowering=False)

    def din(name, shape):
        return nc.dram_tensor(name, shape, F32, kind="ExternalInput").ap()

    def dout(name, shape):
        return nc.dram_tensor(name, shape, F32, kind="ExternalOutput").ap()

    xp = din("xp", [SEQ, D])
    xsm = din("xsm", [NSM, D])
    w_in = din("w_in", [D, 2048])
    w_up = din("w_up", [D, 2 * DFF])
    w_out = din("w_out", [D, D])
    w_down = din("w_down", [DFF, D])
    w_pool = din("w_pool", [512, 128])
    cst = din("cst", [128, NCST])
    gfin = din("gfin", [128, D])
    st_pool = din("st_pool", [128, 4 * NSEQ * 15])
    st_conv = din("st_conv", [128, 4 * NSEQ * 2])
    st_ffn = din("st_ffn", [128, 44 * NSEQ * 2])

    yp = dout("yp", [SEQ, D])
    ys = dout("ys", [NSEQ * DSEQ, D])
    o_pool_p = dout("o_pool_p", [128, 4 * 15])
    o_conv_p = dout("o_conv_p", [128, 4 * 2])
    o_ffn_p = dout("o_ffn_p", [128, 44 * 2])
    o_pool_s = dout("o_pool_s", [128, 4 * NSEQ * 15])
    o_conv_s = dout("o_conv_s", [128, 4 * NSEQ * 2])
    o_ffn_s = dout("o_ffn_s", [128, 44 * NSEQ * 2])

    P = Prog()
    finals = []
    if DEBUG:
        dbg_aT = nc.dram_tensor("dbg_aT", [128, 8 * TW], BF16, kind="ExternalOutput").ap()
        dbg_h2T = nc.dram_tensor("dbg_h2T", [128, 8 * TW], BF16, kind="ExternalOutput").ap()
        dbg_hT = nc.dram_tensor("dbg_hT", [128, 8 * TW], BF16, kind="ExternalOutput").ap()
        dbg_x1 = nc.dram_tensor("dbg_x1", [128, 4 * D], F32, kind="ExternalOutput").ap()
        dbg_actT = nc.dram_tensor("dbg_actT", [128, NFF * TW], BF16, kind="ExternalOutput").ap()

    with ExitStack() as es:
        def sb(name, shape, dt=F32):
            return es.enter_context(nc.sbuf_tensor(name, shape, dt))

        def psum(name, shape, dt=F32):
            return es.enter_context(nc.psum_tensor(name, shape, dt))

        wslot = [sb(f"wslot{i}", [128, 8, 256], BF16) for i in range(NW)]
        w_out_sb = sb("w_out_sb", [128, 8, D], BF16)
        w_down_sb = sb("w_down_sb", [128, NFF, D], BF16)
        w_pool_sb = sb("w_pool_sb", [128, 4, 128], BF16)
        cst_sb = sb("cst_sb", [128, NCST])
        gfin_sb = sb("gfin_sb", [128, D])
        ident = sb("ident", [128, 128], BF16)
        epst = sb("epst", [128, 1])
        invcnt = sb("invcnt", [128, 4, 16])
        x1 = sb("x1", [128, 4, D])
        xrot = [sb(f"xrot{i}", [128, D]) for i in range(2)]
        hbuf = [sb(f"hbuf{i}", [128, D], BF16) for i in range(2)]
        hb0 = sb("hb0", [128, D], BF16)
        hT = sb("hT", [128, 8, TW], BF16)
        aT = sb("aT", [128, 8, TW], BF16)
        actT = sb("actT", [128, NFF, TW], BF16)
        xs_sb = sb("xs_sb", [128, D])
        hs = sb("hs", [128, D], BF16)
        hTs = sb("hTs", [128, 8, NSM], BF16)
        aTs = sb("aTs", [128, 8, NSM], BF16)
        actTs = sb("actTs", [128, NFF, NSM], BF16)
        ss_all = sb("ss_all", [128, 64])
        rs_all = sb("rs_all", [128, 64])
        Dsm = [sb(f"Dsm{i}", [128, NSM], BF16) for i in range(4)]
        Usave = sb("Usave", [128, 4, 15])
        CVsave = sb("CVsave", [128, 4, 2])
        Esave = sb("Esave", [128, 44, 2])
        OFp = sb("OFp", [128, 44, 2])
        stp_sb = sb("stp_sb", [128, 4, NSEQ, 15])
        stc_sb = sb("stc_sb", [128, 4, NSEQ, 2])
        stf_sb = sb("stf_sb", [128, 44, NSEQ, 2])
        Um = [sb(f"Um{i}", [128, 31]) for i in range(2)]
        Us = [sb(f"Us{i}", [128, NSEQ, 19]) for i in range(2)]
        Am = [sb(f"Am{i}", [128, 31]) for i in range(2)]
        As = [sb(f"As{i}", [128, NSEQ, 19]) for i in range(2)]
        GCs = sb("GCs", [128, NSM])
        Xcv = [sb(f"Xcv{i}", [128, NQ, 6]) for i in range(2)]
        Tcs = [sb(f"Tcs{i}", [128, NQ, 4]) for i in range(2)]
        X3p = [sb(f"X3p{i}", [128, 2, NQ, 6]) for i in range(2)]
        T3p = [sb(f"T3p{i}", [128, 2, NQ, 4]) for i in range(2)]
        stfV = stf_sb[:].rearrange("p (h c) b t -> p h c b t", h=2)
        ARENA = 2 * 527 + 2 * 527 + 512 + 2 * 514 + 2 * 512 + 4 * 256
        arena = sb("arena", [128, ARENA])
        off = 0

        def carve(n):
            nonlocal off
            v = arena[:, off:off + n]
            off += n
            return v
        U = [carve(527) for _ in range(2)]
        AB = [carve(527) for _ in range(2)]
        GC = [carve(512)]
        Dbuf = [carve(256).bitcast(BF16) for _ in range(4)]
        CV = [carve(514) for _ in range(2)]
        TT = [carve(512) for _ in range(2)]
        assert off == ARENA
        NT1P = 5
        T1p = [arena[:, i * 1028:(i + 1) * 1028].rearrange("p (h n) -> p h n", h=2) for i in range(NT1P)]
        assert NT1P * 1028 <= ARENA
        EsV = Esave[:].rearrange("p (h c) t -> p h c t", h=2)

        G = [psum(f"G{i}", [128, TW]) for i in range(6)]
        TR = [psum(f"TR{i}", [128, 8, 128], BF16) for i in range(2)]
        NG = len(G)
        gctr = [0]
        trctr = [0]
        nctr = [0]

        gfree = list(range(NG))
        gref = {}

        def galloc(nref=1):
            assert gfree, "out of PSUM banks (program-order allocation)"
            i = gfree.pop(0)
            gref[i] = nref
            return i

        def gdone(i):
            gref[i] -= 1
            assert gref[i] >= 0
            if gref[i] == 0:
                gfree.append(i)

        def gk(g):
            return ("G", g)

        def gt(g):
            return ("Gt", g)

        def sub(g, k):
            return G[g][:, k * NSM:(k + 1) * NSM]

        S1T = ["arS1_dve", "arS1_pe", "arS1_pool"]
        S3T = ["arS3_act", "arS3_pool"]

        def cc(col):
            return cst_sb[:, col:col + 1]

        P.op("sp", lambda e: e.dma_start(out=cst_sb[:], in_=cst[:, :]), writes=["cst"], dsem="D_cst")
        P.op("sp", lambda e: e.dma_start(out=xs_sb[0:NSM, :], in_=xsm[:, :]), writes=["xs"], dsem="D_xs")

        P.op("pool", lambda e: e.memset(x1[:, 0, 0:128], 0.0), writes=[("x1", 0)])
        P.op("pool", lambda e: e.affine_select(out=x1[:, 0, 0:128], in_=x1[:, 0, 0:128], pattern=[[-1, 128]],
                                               compare_op=ALU.not_equal, fill=1.0, base=0, channel_multiplier=1),
             reads=[("x1", 0)], writes=[("x1", 0)])
        P.op("pool", lambda e: e.tensor_copy(out=ident[:], in_=x1[:, 0, 0:128]), reads=[("x1", 0)], writes=["ident"])

        def mk_consts(e):
            e.memset(epst[:], EPS)
            for g, w in enumerate(WINDOWS):
                e.memset(invcnt[:, g, w - 1:16], 1.0 / w)
                for t in range(w - 1):
                    e.memset(invcnt[:, g, t:t + 1], 1.0 / (t + 1))
            for i in range(2):
                e.memset(Um[i][:, 0:15], 0.0)
                e.memset(Xcv[i][:, 0, 0:2], 0.0)
            for i in range(4):
                e.memset(Dsm[i][:, NMETA:SOFF], 0.0)
            for i in range(2):
                ins = e.memset(X3p[i][:, :, 0, 0:2], 0.0)
            return ins
        P.op("pool", mk_consts, writes=["epst", "invcnt", "zeros"])
        P.op("sp", lambda e: e.dma_start(out=stp_sb[:].rearrange("p a b c -> p (a b c)"), in_=st_pool[:, :]),
             writes=[("stp", j) for j in range(4)], dsem="D_stp")
        P.op("sp", lambda e: e.dma_start(out=stc_sb[:].rearrange("p a b c -> p (a b c)"), in_=st_conv[:, :]),
             writes=[("stc", j) for j in range(4)], dsem="D_stc")
        P.op("sp", lambda e: e.dma_start(out=stf_sb[:].rearrange("p a b c -> p (a b c)"), in_=st_ffn[:, :]),
             writes=[("stf", c) for c in range(44)], dsem="D_stf")
        P.op("sp", lambda e: e.dma_start(out=gfin_sb[:], in_=gfin[:, :]), writes=["gfin"], dsem="D_gfin")

        S1_COL = {"u": 0, "gb": 512, "gc": 1024, "v": 1536}
        S1_PAIRS_G = ((("gc", 0), ("v", 0)), (("gb", 0), ("gc", 1)), (("u", 3), ("u", 2)), (("v", 1), ("gb", 1)),
                      (("gc", 2), ("v", 2)), (("u", 1), ("u", 0)), (("gb", 2), ("gc", 3)), (("v", 3), ("gb", 3)))
        wplan = []
        for s in range(NST):
            for pair in S1_PAIRS_G:
                (a, b) = [S1_COL[kind] + 128 * j for (kind, j) in pair]
                wplan.append([(0, w_in[:, a:a + 128]), (128, w_in[:, b:b + 128])])
            for j in range(NFF):
                wplan.append([(0, w_up[:, j * 128:(j + 1) * 128]), (128, w_up[:, DFF + j * 128:DFF + (j + 1) * 128])])
        wissued = [0]
        resident = []

        def res_dma(dst, src, key, name):
            resident.append((dst, src, key, name))
        res_dma(w_pool_sb[:], w_pool.rearrange("(g c) d -> c g d", c=128), "wpool", "D_wpool")
        for q in range(4):
            res_dma(w_out_sb[:, :, q * 256:(q + 1) * 256],
                    w_out[:, q * 256:(q + 1) * 256].rearrange("(k p) n -> p k n", p=128), ("wout", q), f"D_wout{q}")
        for q in range(11):
            res_dma(w_down_sb[:, 2 * q:2 * q + 2, :],
                    w_down[q * 256:(q + 1) * 256, :].rearrange("(k p) n -> p k n", p=128), ("wdown", q), f"D_wdown{q}")
        res_i = [0]

        def issue_resident(n=1):
            for _ in range(n):
                if res_i[0] < len(resident):
                    dst, src, key, name = resident[res_i[0]]
                    res_i[0] += 1
                    P.op("pool", lambda e, dst=dst, src=src: e.dma_start(out=dst, in_=src), writes=[key], dsem=name)

        def prefetch(upto):
            while wissued[0] <= min(upto, len(wplan) - 1):
                wi = wissued[0]
                slot = wi % NW
                for h, (c0, src) in enumerate(wplan[wi]):
                    P.op("pool", lambda e, slot=slot, c0=c0, src=src: e.dma_start(
                        out=wslot[slot][:, :, c0:c0 + 128], in_=src.rearrange("(k p) n -> p k n", p=128)),
                        writes=[("W", slot, h)], dsem=f"D_w{slot}_{h}")
                wissued[0] += 1
                if 1 <= wi <= 5:
                    issue_resident(1)

        wuse = [0]

        def next_weights(ahead=0):
            wi = wuse[0]
            wuse[0] += 1
            prefetch(wi + NW - 1 - ahead)
            return wi % NW

        def norm_stats(src, nrows, junk, junkkey, srckeys):
            n = nctr[0]
            nctr[0] += 1
            P.op("act", lambda e: e.activation(out=junk, in_=src, func=AF.Square, accum_out=ss_all[0:nrows, n:n + 1]),
                 reads=srckeys, writes=[("ss", n), junkkey])
            P.op("act", lambda e: e.activation(out=rs_all[0:nrows, n:n + 1], in_=ss_all[0:nrows, n:n + 1], func=AF.Sqrt,
                                               scale=1.0 / D, bias=epst[0:nrows, :]),
                 reads=[("ss", n), "epst"], writes=[("rs", n)])
            P.op("dve", lambda e: e.reciprocal(out=rs_all[0:nrows, n:n + 1], in_=rs_all[0:nrows, n:n + 1]),
                 reads=[("rs", n)], writes=[("rs", n)])
            return n

        def norm_transpose(src, nrows, srckeys, hb, hk, dstT, col0, gcol, dstkeys):
            hbv = hb[0:nrows, :]
            n = norm_stats(src, nrows, hbv, hk, srckeys)
            P.op("dve", lambda e: e.tensor_scalar(out=hbv, in0=src, scalar1=rs_all[0:nrows, n:n + 1], scalar2=None,
                                                  op0=ALU.mult),
                 reads=list(srckeys) + [("rs", n)], writes=[hk])

            def part_b():
                tr = TR[trctr[0] % 2]
                trk = ("TR", trctr[0] % 2)
                trctr[0] += 1

                def do_tr(e):
                    for k in range(8):
                        ins = e.transpose(out=tr[:, k, 0:nrows], in_=hb[0:nrows, k * 128:(k + 1) * 128],
                                          identity=ident[0:nrows, 0:nrows])
                    return ins
                P.op("pe", do_tr, reads=[hk, "ident"], writes=[trk])
                P.op("dve", lambda e: e.tensor_tensor(
                    out=dstT[:, :, col0:col0 + nrows], in0=tr[:, :, 0:nrows],
                    in1=cst_sb[:, gcol:gcol + 8].unsqueeze(2).to_broadcast([128, 8, nrows]), op=ALU.mult),
                    reads=[trk, "cst"], writes=dstkeys)
            return part_b

        def v3(ap, b=4):
            return ap.rearrange("p (a b) -> p a b", b=b)

        def stage0_first():
            hbs = [(hb0, "hb0"), (hbuf[0], ("hb", 0)), (hbuf[1], ("hb", 1))]
            pbs = []
            for i in range(4):
                hb, hk = hbs[i % 3]
                pbs.append(norm_transpose(x1[:, i, :], 128, [("x1", i)], hb, hk, hT, i * 128, C_G1, [("hT", i)]))
                if i >= 1:
                    pbs.pop(0)()
            for pb in pbs:
                pb()

        def stage0_small():
            norm_transpose(xs_sb[0:NSM, :], NSM, ["xs"], hs, "hs", hTs, 0, C_G1, ["hTs"])()

        def x1_reload(s):
            for i in range(4):
                r0 = (s * 4 + i) * 128
                P.op("sp", lambda e, i=i, r0=r0: e.dma_start(out=x1[:, i, :], in_=xp[r0:r0 + 128, :]),
                     writes=[("x1", i)], dsem=f"D_x1_{i}")

        def mm_feat(slot, half, rhsT, rhs_keys, ncols, out_ap, g, nk=8):
            def f(e):
                for k in range(nk):
                    ins = e.matmul(out_ap, lhsT=wslot[slot][:, k, half * 128:(half + 1) * 128],
                                   rhs=rhsT[:, k, 0:ncols], start=(k == 0), stop=(k == nk - 1))
                return ins
            P.op("pe", f, reads=[("W", slot, half)] + rhs_keys, writes=[gk(g), gt(g)])

        S1_PAIRS = S1_PAIRS_G

        def stage1(s):
            small = (s == 0)
            hkeys = [("hT", i) for i in range(4)]
            deferred = []
            slots = {}
            for pi, pair in enumerate(S1_PAIRS):
                slot = next_weights()
                gsm = galloc(2) if small else None
                nd = []
                for (rdy, fn) in deferred:
                    if pi >= rdy:
                        fn()
                    else:
                        nd.append((rdy, fn))
                deferred[:] = nd
                for half, (kind, j) in enumerate(pair):
                    g = galloc()
                    mm_feat(slot, half, hT, hkeys, TW, G[g][:, :], g)
                    slots[(kind, j)] = (g, gsm, half)
                if small:
                    for half in range(2):
                        mm_feat(slot, half, hTs, ["hTs"], NSM, sub(gsm, half), gsm)
                for half, (kind, j) in enumerate(pair):
                    g = slots[(kind, j)][0]
                    if kind == "u":
                        rest = small_u(j, gsm, half) if small else None
                        rdy = 99 if small else pi + 3
                        deferred.append((rdy, main_u(s, j, g)))
                        if rest is not None:
                            deferred.append((rdy, rest()))
                    elif kind == "v":
                        g_gc, gsm_gc, half_gc = slots[("gc", j)]
                        rest = small_cv(j, gsm_gc, half_gc, gsm, half) if small else None
                        main_cv(s, j, g_gc, g)
                        if rest is not None:
                            rest()
                    elif kind == "gb":
                        bj = j % 2
                        P.op("dve", lambda e, g=g, j=j, bj=bj: e.tensor_tensor(
                            out=aT[:, 4 + j, :], in0=G[g][:, :], in1=TT[bj], op=ALU.mult),
                            reads=[gk(g), ("TT", bj)], writes=[("aT", 4 + j, i) for i in range(4)] + [gt(g), "arS1_dve"])
                        gdone(g)
                        if small:
                            P.op("dve", lambda e, gsm=gsm, half=half, j=j, bj=bj: e.tensor_tensor(
                                out=v3(aTs[:, 4 + j, :]), in0=v3(sub(gsm, half)), in1=Tcs[bj][:], op=ALU.mult),
                                reads=[gk(gsm), ("Tcs", bj)], writes=[("aTs", 4 + j), gt(gsm)])
                            gdone(gsm)
            for (rdy, fn) in deferred:
                fn()

        def small_u(j, gsm, half):
            W = WINDOWS[j]
            b = j % 2
            um, us = Um[b], Us[b]
            sp_ap = sub(gsm, half)
            P.op("act", lambda e: e.activation(out=um[:, 15:31], in_=sp_ap[:, 0:16], func=AF.Copy),
                 reads=[gk(gsm), "zeros"], writes=[("Um", b), gt(gsm)])
            P.op("act", lambda e: e.activation(out=us[:, :, 15:19], in_=v3(sp_ap[:, SOFF:NSM]), func=AF.Copy),
                 reads=[gk(gsm)], writes=[("UsN", b), gt(gsm)])
            gdone(gsm)
            P.op("act", lambda e: e.activation(out=Usave[:, j, :], in_=um[:, 16:31], func=AF.Copy),
                 reads=[("Um", b)], writes=[("Usave", j)])

            def rest():
                P.op("pool", lambda e: e.tensor_copy(out=us[:, :, 0:15], in_=stp_sb[:, j, :, :]),
                     reads=[("stp", j)], writes=[("UsH", b)])
                P.op("pool", lambda e: e.tensor_copy(out=stp_sb[:, j, :, :], in_=us[:, :, 4:19]),
                     reads=[("UsH", b), ("UsN", b)], writes=[("stp", j)])
                cur_m, cur_s = um, us
                m = 1
                si = 0
                while m < W:
                    lo = 15 - (W - 2 * m)
                    dst_m, dst_s = Am[si % 2], As[si % 2]

                    def f(e, cur_m=cur_m, cur_s=cur_s, dst_m=dst_m, dst_s=dst_s, m=m, lo=lo):
                        e.tensor_tensor(out=dst_m[:, lo:31], in0=cur_m[:, lo:31], in1=cur_m[:, lo - m:31 - m], op=ALU.add)
                        return e.tensor_tensor(out=dst_s[:, :, lo:19], in0=cur_s[:, :, lo:19],
                                               in1=cur_s[:, :, lo - m:19 - m], op=ALU.add)
                    P.op("dve", f, reads=[("Um", b), ("UsH", b), ("UsN", b), "smscr"], writes=["smscr"])
                    cur_m, cur_s = dst_m, dst_s
                    m *= 2
                    si += 1
                P.op("dve", lambda e, cur_m=cur_m: e.tensor_tensor(out=cur_m[:, 15:31], in0=cur_m[:, 15:31],
                                                                 in1=invcnt[:, j, :], op=ALU.mult),
                     reads=["smscr", "invcnt"], writes=["smscr"])

                def comb(e, cur_m=cur_m, cur_s=cur_s):
                    e.tensor_tensor(out=Dsm[j][:, 0:16], in0=cur_m[:, 15:31], in1=um[:, 15:31], op=ALU.subtract)
                    return e.scalar_tensor_tensor(out=v3(Dsm[j][:, SOFF:NSM]), in0=cur_s[:, :, 15:19], scalar=1.0 / W,
                                                  in1=us[:, :, 15:19], op0=ALU.mult, op1=ALU.subtract)
                P.op("dve", comb, reads=["smscr", ("Um", b), ("UsN", b), "invcnt"], writes=["smscr", ("Dsm", j)])

                def later():
                    g2 = galloc()
                    so = sub(g2, 0)
                    P.op("pe", lambda e: e.matmul(so, lhsT=w_pool_sb[:, j, :], rhs=Dsm[j][:, :], start=True, stop=True),
                         reads=[("Dsm", j), "wpool"], writes=[gk(g2), gt(g2)])
                    P.op("act", lambda e: e.activation(out=aTs[:, j, :], in_=so, func=AF.Copy, scale=cc(C_PS + j)),
                         reads=[gk(g2), "cst"], writes=[("aTs", j), gt(g2)])
                    gdone(g2)
                return later
            return rest

        def main_u(s, j, g):
            W = WINDOWS[j]
            b = j % 2
            u = U[b]
            P.op("act", lambda e: e.activation(out=u[:, 0:15], in_=Usave[:, j, :], func=AF.Copy),
                 reads=[("Usave", j)] + S3T, writes=[("UH", b)])
            P.op("act", lambda e: e.activation(out=u[:, 15:527], in_=G[g][:, :], func=AF.Copy),
                 reads=[gk(g)] + S3T, writes=[("UN", b), gt(g)])
            gdone(g)
            P.op("act", lambda e: e.activation(out=Usave[:, j, :], in_=u[:, 512:527], func=AF.Copy),
                 reads=[("UN", b)], writes=[("Usave", j), "arS1_pool"])
            cur = u
            m = 1
            si = 0
            while m < W:
                lo = 15 - (W - 2 * m)
                dst = AB[si % 2]
                P.op("dve", lambda e, cur=cur, dst=dst, m=m, lo=lo: e.tensor_tensor(
                    out=dst[:, lo:527], in0=cur[:, lo:527], in1=cur[:, lo - m:527 - m], op=ALU.add),
                    reads=[("UH", b), ("UN", b), "AB"] + S3T, writes=["AB", "arS1_dve"])
                cur = dst
                m *= 2
                si += 1
            db = Dbuf[j]
            P.op("dve", lambda e, cur=cur: e.scalar_tensor_tensor(
                out=db[:, :], in0=cur[:, 15:527], scalar=1.0 / W, in1=u[:, 15:527], op0=ALU.mult, op1=ALU.subtract),
                reads=["AB", ("UN", b)] + S3T, writes=[("D", j), "arS1_dve"])

            def later():
                g2 = galloc()
                P.op("pe", lambda e: e.matmul(G[g2][:, :], lhsT=w_pool_sb[:, j, :], rhs=db[:, :], start=True, stop=True),
                     reads=[("D", j), "wpool"], writes=[gk(g2), gt(g2), "arS1_pe"])
                P.op("act", lambda e: e.activation(out=aT[:, j, :], in_=G[g2][:, :], func=AF.Copy, scale=cc(C_PS + j)),
                     reads=[gk(g2), "cst"], writes=[("aT", j, i) for i in range(4)] + [gt(g2)])
                gdone(g2)
            return later

        def small_cv(j, g_gc, h_gc, g_v, h_v):
            b = j % 2
            x = Xcv[b]
            t = Tcs[b]
            gc_ap = sub(g_gc, h_gc)
            v_ap = sub(g_v, h_v)
            P.op("act", lambda e: e.activation(out=GCs[:, :], in_=gc_ap, func=AF.Copy),
                 reads=[gk(g_gc)], writes=["GCs", gt(g_gc)])
            gdone(g_gc)
            P.op("dve", lambda e: e.tensor_tensor(out=x[:, :, 2:6], in0=v3(v_ap), in1=v3(GCs[:, :]), op=ALU.mult),
                 reads=[gk(g_v), "GCs"], writes=[("XcvN", b), gt(g_v)])
            gdone(g_v)
            P.op("act", lambda e: e.activation(out=CVsave[:, j, :], in_=x[:, 3, 4:6], func=AF.Copy),
                 reads=[("XcvN", b)], writes=[("CVsave", j)])

            def rest():
                def hist(e):
                    e.tensor_copy(out=x[:, 1:5, 0:2], in_=x[:, 0:4, 4:6])
                    return e.tensor_copy(out=x[:, 5:NQ, 0:2], in_=stc_sb[:, j, :, :])
                P.op("pool", hist, reads=[("XcvN", b), ("stc", j), "zeros"], writes=[("XcvH", b)])
                P.op("pool", lambda e: e.tensor_copy(out=stc_sb[:, j, :, :], in_=x[:, 5:NQ, 4:6]),
                     reads=[("XcvN", b), ("XcvH", b)], writes=[("stc", j)])
                P.op("act", lambda e: e.activation(out=t[:], in_=x[:, :, 0:4], func=AF.Copy, scale=cc(C_CW + 3 * j)),
                     reads=[("XcvN", b), ("XcvH", b), "cst"], writes=[("Tcs", b)])
                P.op("dve", lambda e: e.scalar_tensor_tensor(out=t[:], in0=x[:, :, 1:5], scalar=cc(C_CW + 3 * j + 1),
                                                              in1=t[:], op0=ALU.mult, op1=ALU.add),
                     reads=[("XcvN", b), ("XcvH", b), ("Tcs", b)], writes=[("Tcs", b)])
                P.op("dve", lambda e: e.scalar_tensor_tensor(out=t[:], in0=x[:, :, 2:6], scalar=cc(C_CW + 3 * j + 2),
                                                              in1=t[:], op0=ALU.mult, op1=ALU.add),
                     reads=[("XcvN", b), ("Tcs", b)], writes=[("Tcs", b)])
            return rest

        def main_cv(s, j, g_gc, g_v):
            b = j % 2
            cv = CV[b]
            t = TT[b]
            gcb = GC[0]
            P.op("act", lambda e: e.activation(out=gcb, in_=G[g_gc][:, :], func=AF.Copy),
                 reads=[gk(g_gc)] + S3T, writes=["GC", gt(g_gc)])
            gdone(g_gc)
            P.op("act", lambda e: e.activation(out=cv[:, 0:2], in_=CVsave[:, j, :], func=AF.Copy),
                 reads=[("CVsave", j)] + S3T, writes=[("CVH", b)])
            P.op("dve", lambda e: e.tensor_tensor(out=cv[:, 2:514], in0=G[g_v][:, :], in1=gcb, op=ALU.mult),
                 reads=[gk(g_v), "GC"] + S3T, writes=[("CVN", b), gt(g_v), "arS1_dve"])
            gdone(g_v)
            P.op("act", lambda e: e.activation(out=CVsave[:, j, :], in_=cv[:, 512:514], func=AF.Copy),
                 reads=[("CVN", b)], writes=[("CVsave", j), "arS1_pool"])
            P.op("act", lambda e: e.activation(out=t, in_=cv[:, 0:512], func=AF.Copy, scale=cc(C_CW + 3 * j)),
                 reads=[("CVN", b), ("CVH", b), "cst"] + S3T, writes=[("TT", b)])
            P.op("dve", lambda e: e.scalar_tensor_tensor(out=t, in0=cv[:, 1:513], scalar=cc(C_CW + 3 * j + 1), in1=t,
                                                          op0=ALU.mult, op1=ALU.add),
                 reads=[("CVN", b), ("CVH", b), ("TT", b)], writes=[("TT", b)])
            P.op("dve", lambda e: e.scalar_tensor_tensor(out=t, in0=cv[:, 2:514], scalar=cc(C_CW + 3 * j + 2), in1=t,
                                                          op0=ALU.mult, op1=ALU.add),
                 reads=[("CVN", b), ("TT", b)], writes=[("TT", b), "arS1_dve"])

        def mm_tok(lhsT_of_k, lkey_of_k, nrows, wsb, wkeys, korder, ngrp, ga, gb):
            nk = len(korder)
            per = (nk + ngrp - 1) // ngrp
            for gi in range(0, nk, per):
                ks = korder[gi:gi + per]

                def f(e, ks=ks, gi=gi):
                    for n, k in enumerate(ks):
                        first = (gi + n == 0)
                        lastk = (gi + n == nk - 1)
                        e.matmul(G[ga][0:nrows, :], lhsT=lhsT_of_k(k), rhs=wsb[:, k, 0:512], start=first, stop=lastk)
                        ins = e.matmul(G[gb][0:nrows, :], lhsT=lhsT_of_k(k), rhs=wsb[:, k, 512:1024],
                                       start=first, stop=lastk)
                    return ins
                P.op("pe", f, reads=[lkey_of_k(k) for k in ks] + list(wkeys), writes=[gk(ga), gk(gb), gt(ga), gt(gb)])

        def resid(dst, nrows, dkey, ga, gb):
            def f(e):
                e.tensor_tensor(out=dst[0:nrows, 0:512], in0=G[ga][0:nrows, :], in1=dst[0:nrows, 0:512], op=ALU.add)
                return e.tensor_tensor(out=dst[0:nrows, 512:1024], in0=G[gb][0:nrows, :], in1=dst[0:nrows, 512:1024],
                                       op=ALU.add)
            P.op("dve", f, reads=[gk(ga), gk(gb), dkey], writes=[dkey, gt(ga), gt(gb)])
            gdone(ga)
            gdone(gb)

        S2_KORDER = [4, 5, 6, 0, 1, 2, 3, 7]
        WOUT_KEYS = [("wout", q) for q in range(4)]
        WDOWN_KEYS = [("wdown", q) for q in range(11)]

        def warm(n):
            if n <= 0:
                return
            g = galloc()

            def f(e):
                for _ in range(n):
                    ins = e.matmul(G[g][:, :], lhsT=w_out_sb[:, 0, 0:128], rhs=w_out_sb[:, 0, 0:512],
                                   start=True, stop=True)
                return ins
            P.op("pe", f, reads=[("wout", 0), ("wout", 1)], writes=[gk(g), gt(g)])
            gdone(g)

        def stage2(s):
            small = (s == 0)
            pend = []
            warm(WARM_S2)

            def do_small():
                ga, gb = galloc(), galloc()
                mm_tok(lambda k: aTs[:, k, 0:NSM], lambda k: ("aTs", k), NSM, w_out_sb, WOUT_KEYS, S2_KORDER, 4, ga, gb)
                resid(xs_sb, NSM, "xs", ga, gb)
                pbs = norm_transpose(xs_sb[0:NSM, :], NSM, ["xs"], hs, "hs", aTs, 0, C_G2,
                                     [("aTs", k) for k in range(8)])

                def pbs2(pbs=pbs):
                    pbs()
                    P.op("pool", lambda e: e.memset(aTs[:, :, NMETA:SOFF], 0.0), writes=[("aTs", k) for k in range(8)])
                return pbs2
            for i in range(4):
                ga, gb = galloc(), galloc()
                mm_tok(lambda k, i=i: aT[:, k, i * 128:(i + 1) * 128], lambda k, i=i: ("aT", k, i), 128,
                       w_out_sb, WOUT_KEYS, S2_KORDER, 4 if i == 0 else 1, ga, gb)
                resid(x1[:, i, :], 128, ("x1", i), ga, gb)
                while pend:
                    pend.pop(0)()
                pend.append(norm_transpose(x1[:, i, :], 128, [("x1", i)], hbuf[i % 2], ("hb", i % 2), aT, i * 128, C_G2,
                                           [("aT", k, i) for k in range(8)]))
                if small and i == 1:
                    pend.append(do_small())
            return pend

        t1ctr = [0]
        x3ctr = [0]

        def stage3(s):
            small = (s == 0)
            last = (s == NST - 1)
            akeys = [("aT", k, i) for k in range(8) for i in range(4)]
            askeys = [("aTs", k) for k in range(8)]
            prev_tail = None
            slots3 = {}
            LA = 2

            def get_slot(j, ahead=0):
                if j not in slots3:
                    slots3[j] = next_weights(ahead)
                return slots3[j]

            def small_mm(j):
                slot = get_slot(j, LA - 1 if j >= LA else j)
                gsm = galloc()
                for half in range(2):
                    mm_feat(slot, half, aTs, askeys, NSM, sub(gsm, half), gsm)
                return gsm
            if small:
                for jj in range(LA):
                    small_A(jj, small_mm(jj))
                small_B(0)
            for j in range(NFF):
                slot = get_slot(j)
                gs = []
                for half in range(2):
                    g = galloc()
                    gs.append(g)
                    mm_feat(slot, half, aT, akeys, TW, G[g][:, :], g)
                gsm_next = small_mm(j + LA) if (small and j + LA < NFF) else None
                if small and j + 1 < NFF:
                    small_B(j + 1)
                ti = t1ctr[0] % NT1P
                t1ctr[0] += 1
                tp = T1p[ti]
                kh = [("T1h", ti, h) for h in range(2)]
                kb = [("T1b", ti, h) for h in range(2)]
                kt = [("T1t", ti, h) for h in range(2)]
                cs = (j, NFF + j)
                ek = [("Esave", c) for c in cs]
                P.op("act", lambda e, tp=tp, j=j: e.activation(out=tp[:, :, 0:2], in_=EsV[:, :, j, :], func=AF.Copy),
                     reads=ek + S1T, writes=kh)
                for h in range(2):
                    c, g = cs[h], gs[h]
                    P.op("act", lambda e, tp=tp, g=g, c=c, h=h: e.activation(
                        out=tp[:, h, 2:514], in_=G[g][:, :], func=AF.Identity, scale=cc(C_FW + 3 * c), bias=cc(C_FB + c)),
                        reads=[gk(g), "cst"] + S1T, writes=[kb[h], kt[h], gt(g)])
                if prev_tail is not None:
                    prev_tail()
                if gsm_next is not None:
                    small_A(j + LA, gsm_next)
                for h in range(2):
                    c, g = cs[h], gs[h]
                    P.op("dve", lambda e, tp=tp, g=g, c=c, h=h: e.scalar_tensor_tensor(
                        out=tp[:, h, 1:513], in0=G[g][:, :], scalar=cc(C_FW + 3 * c + 1), in1=tp[:, h, 1:513],
                        op0=ALU.mult, op1=ALU.add),
                        reads=[gk(g), kh[h], kb[h], kt[h], "cst"], writes=[kh[h], kb[h], kt[h], gt(g)])
                for h in range(2):
                    c, g = cs[h], gs[h]
                    if last:
                        P.op("dve", lambda e, g=g, c=c: e.tensor_copy(out=OFp[:, c, :], in_=G[g][:, 510:512]),
                             reads=[gk(g)], writes=[("OFp", c), gt(g)])
                    P.op("dve", lambda e, tp=tp, g=g, c=c, h=h: e.scalar_tensor_tensor(
                        out=tp[:, h, 0:512], in0=G[g][:, :], scalar=cc(C_FW + 3 * c + 2), in1=tp[:, h, 0:512],
                        op0=ALU.mult, op1=ALU.add),
                        reads=[gk(g), kh[h], kb[h], "cst"], writes=[kh[h], kb[h], gt(g)])
                    gdone(g)

                def tail(tp=tp, j=j, kh=kh, kb=kb, kt=kt, ek=ek):
                    P.op("act", lambda e: e.activation(out=EsV[:, :, j, :], in_=tp[:, :, 512:514], func=AF.Copy),
                         reads=kt, writes=ek + ["arS3_act"])
                    P.op("act", lambda e: e.activation(out=tp[:, 1, 0:512], in_=tp[:, 1, 0:512], func=AF.Silu),
                         reads=[kh[1], kb[1]], writes=[kh[1], kb[1]])
                    P.op("pool", lambda e: e.tensor_tensor(out=actT[:, j, :], in0=tp[:, 1, 0:512], in1=tp[:, 0, 0:512],
                                                           op=ALU.mult),
                         reads=kh + kb, writes=[("actT", j, i) for i in range(4)] + ["arS3_pool"])
                prev_tail = tail
                if small:
                    small_C(j)
            prev_tail()

        def small_bufs(j):
            xi = j % 2
            return X3p[xi], T3p[xi], ("X3N", xi), ("X3H", xi), ("T3", xi), (j, NFF + j)

        def small_A(j, gsm):
            x, t, kx, kxh, ktt, cs = small_bufs(j)
            P.op("act", lambda e: e.activation(out=x[:, :, :, 2:6],
                                               in_=G[gsm][:, 0:2 * NSM].rearrange("p (h q t) -> p h q t", h=2, t=4),
                                               func=AF.Copy),
                 reads=[gk(gsm)], writes=[kx, gt(gsm)])
            gdone(gsm)

            def hist(e):
                e.tensor_copy(out=x[:, :, 1:5, 0:2], in_=x[:, :, 0:4, 4:6])
                return e.tensor_copy(out=x[:, :, 5:NQ, 0:2], in_=stfV[:, :, j, :, :])
            P.op("pool", hist, reads=[kx, ("stf", cs[0]), ("stf", cs[1]), "zeros"], writes=[kxh])
            P.op("pool", lambda e: e.tensor_copy(out=stfV[:, :, j, :, :], in_=x[:, :, 5:NQ, 4:6]),
                 reads=[kx, kxh], writes=[("stf", cs[0]), ("stf", cs[1])])

        def small_B(j):
            x, t, kx, kxh, ktt, cs = small_bufs(j)
            for h in range(2):
                c = cs[h]
                P.op("act", lambda e, h=h, c=c: e.activation(out=t[:, h], in_=x[:, h, :, 0:4], func=AF.Identity,
                                                             scale=cc(C_FW + 3 * c), bias=cc(C_FB + c)),
                     reads=[kx, kxh, "cst"], writes=[(ktt, h)])
            for tap in (1, 2):
                for h in range(2):
                    c = cs[h]
                    P.op("dve", lambda e, h=h, c=c, tap=tap: e.scalar_tensor_tensor(
                        out=t[:, h], in0=x[:, h, :, tap:tap + 4], scalar=cc(C_FW + 3 * c + tap), in1=t[:, h],
                        op0=ALU.mult, op1=ALU.add),
                        reads=[kx, kxh, (ktt, h), "cst"], writes=[(ktt, h)])
            P.op("pool", lambda e: e.tensor_copy(out=EsV[:, :, j, :], in_=t[:, :, NMETA // 4, 0:2]),
                 reads=[(ktt, 0), (ktt, 1)], writes=[("Esave", cs[0]), ("Esave", cs[1])])

        def small_C(j):
            x, t, kx, kxh, ktt, cs = small_bufs(j)
            P.op("act", lambda e: e.activation(out=t[:, 1], in_=t[:, 1], func=AF.Silu),
                 reads=[(ktt, 1)], writes=[(ktt, 1)])
            P.op("dve", lambda e: e.tensor_tensor(out=v3(actTs[:, j, :]), in0=t[:, 1], in1=t[:, 0], op=ALU.mult),
                 reads=[(ktt, 0), (ktt, 1)], writes=[("actTs", j)])

        def final_norm_store(dst, nrows, dkey, junk, junkkey, out_ap, src_rows, dsem):
            n = norm_stats(dst[0:nrows, :], nrows, junk, junkkey, [dkey])
            P.op("dve", lambda e: e.scalar_tensor_tensor(
                out=dst[0:nrows, :], in0=dst[0:nrows, :], scalar=rs_all[0:nrows, n:n + 1], in1=gfin_sb[0:nrows, :],
                op0=ALU.mult, op1=ALU.mult), reads=[dkey, ("rs", n), "gfin"], writes=[dkey])
            lo, hi = src_rows
            finals.append(P.op("sp", lambda e: e.dma_start(out=out_ap, in_=dst[lo:hi, :]), reads=[dkey], dsem=dsem))

        S4_KORDER = list(range(NFF))

        def stage0_tile(sn, i):
            gti = sn * 4 + i
            xr = xrot[gti % 2]
            xk = ("xrot", gti % 2)
            r0 = gti * 128
            P.op("sp", lambda e: e.dma_start(out=xr[:], in_=xp[r0:r0 + 128, :]), writes=[xk], dsem=f"D_xrot{gti % 2}")
            return norm_transpose(xr[:], 128, [xk], hb0, "hb0", hT, i * 128, C_G1, [("hT", i)])

        def stage4(s):
            small = (s == 0)
            nxt = s + 1 if s + 1 < NST else None
            def do_small4():
                ga, gb = galloc(), galloc()
                mm_tok(lambda k: actTs[:, k, 0:NSM], lambda k: ("actTs", k), NSM, w_down_sb, WDOWN_KEYS,
                       S4_KORDER, 3, ga, gb)
                resid(xs_sb, NSM, "xs", ga, gb)
                final_norm_store(xs_sb, NSM, "xs", hs[0:NSM, :], "hs", ys[:, :], (SOFF, NSM), "D_ys")
            pb = stage0_tile(nxt, 0) if nxt is not None else None
            for i in range(4):
                ga, gb = galloc(), galloc()
                mm_tok(lambda k, i=i: actT[:, k, i * 128:(i + 1) * 128], lambda k, i=i: ("actT", k, i), 128,
                       w_down_sb, WDOWN_KEYS, S4_KORDER, 4 if i == 0 else 1, ga, gb)
                if pb is not None:
                    pb()
                    pb = stage0_tile(nxt, i + 1) if i < 3 else None
                    if i == 3:
                        warm(WARM_S1)
                resid(x1[:, i, :], 128, ("x1", i), ga, gb)
                r0 = (s * 4 + i) * 128
                final_norm_store(x1[:, i, :], 128, ("x1", i), hbuf[i % 2][:, :], ("hb", i % 2), yp[r0:r0 + 128, :],
                                 (0, 128), f"D_x1_{i}")
                if small and i == 1:
                    do_small4()

        prefetch(NW - 2)
        x1_reload(0)
        stage0_small()
        stage0_first()
        for s in range(NST):
            if s > 0:
                x1_reload(s)
            if DEBUG and s == 0:
                finals.append(P.op("sp", lambda e: e.dma_start(out=dbg_hT[:, :], in_=hT[:].rearrange("p a b -> p (a b)")),
                                   reads=[("hT", i) for i in range(4)], dsem="D_dbg0"))
            stage1(s)
            if DEBUG and s == 0:
                finals.append(P.op("sp", lambda e: e.dma_start(out=dbg_aT[:, :], in_=aT[:].rearrange("p a b -> p (a b)")),
                                   reads=[("aT", k, i) for k in range(8) for i in range(4)], dsem="D_dbg1"))
            if s == 0:
                issue_resident(len(resident))
            pend = stage2(s)
            warm(WARM_S3A)
            for fn in pend:
                fn()
            warm(WARM_S3B)
            if DEBUG and s == 0:
                finals.append(P.op("sp", lambda e: e.dma_start(out=dbg_h2T[:, :], in_=aT[:].rearrange("p a b -> p (a b)")),
                                   reads=[("aT", k, i) for k in range(8) for i in range(4)], dsem="D_dbg2"))
                finals.append(P.op("sp", lambda e: e.dma_start(out=dbg_x1[:, :], in_=x1[:].rearrange("p a b -> p (a b)")),
                                   reads=[("x1", i) for i in range(4)], dsem="D_dbg3"))
            stage3(s)
            if DEBUG and s == 0:
                finals.append(P.op("sp", lambda e: e.dma_start(out=dbg_actT[:, :], in_=actT[:].rearrange("p a b -> p (a b)")),
                                   reads=[("actT", j, i) for j in range(NFF) for i in range(4)], dsem="D_dbg4"))
            stage4(s)
        while res_i[0] < len(resident):
            issue_resident(1)
        finals.append(P.op("sp", lambda e: e.dma_start(out=o_pool_p[:, :], in_=Usave[:].rearrange("p a b -> p (a b)")),
                           reads=[("Usave", j) for j in range(4)], dsem="D_o1"))
        finals.append(P.op("sp", lambda e: e.dma_start(out=o_conv_p[:, :], in_=CVsave[:].rearrange("p a b -> p (a b)")),
                           reads=[("CVsave", j) for j in range(4)], dsem="D_o2"))
        finals.append(P.op("sp", lambda e: e.dma_start(out=o_ffn_p[:, :], in_=OFp[:].rearrange("p a b -> p (a b)")),
                           reads=[("OFp", c) for c in range(44)], dsem="D_o3"))
        finals.append(P.op("sp", lambda e: e.dma_start(out=o_pool_s[:, :], in_=stp_sb[:].rearrange("p a b c -> p (a b c)")),
                           reads=[("stp", j) for j in range(4)], dsem="D_o4"))
        finals.append(P.op("sp", lambda e: e.dma_start(out=o_conv_s[:, :], in_=stc_sb[:].rearrange("p a b c -> p (a b c)")),
                           reads=[("stc", j) for j in range(4)], dsem="D_o5"))
        finals.append(P.op("sp", lambda e: e.dma_start(out=o_ffn_s[:, :], in_=stf_sb[:].rearrange("p a b c -> p (a b c)")),
                           reads=[("stf", c) for c in range(44)], dsem="D_o6"))
        P.emit(nc, final_wait_ops=finals)
    return nc


def _col(v, n):
    return np.ascontiguousarray(np.asarray(v, np.float32).reshape(n, 128).T)


_NC_CACHE = {}


def kernel(x_prompt, x_sample, state_pool, state_conv, state_ffn, meta_tokens,
           norm_mix, w_in, w_pool, pool_scale, conv_w, w_out,
           norm_ffn, w_up, ffn_conv_w, ffn_conv_b, w_down, norm_final):
    f = lambda a: np.ascontiguousarray(np.asarray(a, dtype=np.float32))
    x_prompt, x_sample = f(x_prompt), f(x_sample)
    state_pool, state_conv, state_ffn = f(state_pool), f(state_conv), f(state_ffn)
    meta_tokens = f(meta_tokens)
    cst = np.zeros((128, NCST), np.float32)
    cst[:, C_G1:C_G1 + 8] = _col(norm_mix[0], 8)
    cst[:, C_G2:C_G2 + 8] = _col(norm_ffn[0], 8)
    cst[:, C_PS:C_PS + 4] = _col(pool_scale[0], 4)
    cw = f(conv_w)[0]
    cst[:, C_CW:C_CW + 12] = cw.reshape(3, 4, 128).transpose(2, 1, 0).reshape(128, 12)
    fw = f(ffn_conv_w)[0]
    cst[:, C_FW:C_FW + 132] = fw.reshape(3, 44, 128).transpose(2, 1, 0).reshape(128, 132)
    cst[:, C_FB:C_FB + 44] = _col(f(ffn_conv_b)[0], 44)
    gfin = np.ascontiguousarray(np.broadcast_to(f(norm_final)[None, :], (128, D)))
    shared = {
        "w_in": f(w_in)[0], "w_up": f(w_up)[0], "w_out": f(w_out)[0], "w_down": f(w_down)[0],
        "w_pool": f(w_pool)[0].reshape(512, 128), "cst": cst, "gfin": gfin,
    }
    in_maps = []
    for c in range(NCORES):
        sl = slice(c * NSEQ, (c + 1) * NSEQ)
        m = dict(shared)
        m["xp"] = x_prompt[c]
        m["xsm"] = np.ascontiguousarray(np.concatenate(
            [meta_tokens, np.zeros((NPAD, D), np.float32), x_sample[sl].reshape(NSEQ * DSEQ, D)], axis=0))
        m["st_pool"] = np.ascontiguousarray(
            state_pool[0, sl].reshape(NSEQ, 15, 4, 128).transpose(3, 2, 0, 1).reshape(128, -1))
        m["st_conv"] = np.ascontiguousarray(
            state_conv[0, sl].reshape(NSEQ, 2, 4, 128).transpose(3, 2, 0, 1).reshape(128, -1))
        m["st_ffn"] = np.ascontiguousarray(
            state_ffn[0, sl].reshape(NSEQ, 2, 44, 128).transpose(3, 2, 0, 1).reshape(128, -1))
        in_maps.append(m)
    if "nc" not in _NC_CACHE:
        _NC_CACHE["nc"] = build_nc()
    nc = _NC_CACHE["nc"]
    res = run_bass_kernel_spmd(nc, in_maps, core_ids=list(range(NCORES)))
    R = res.results
    if DEBUG:
        _NC_CACHE["dbg"] = R[0]
    y_prompt = np.stack([R[c]["yp"] for c in range(NCORES)], axis=0)
    y_sample = np.concatenate([R[c]["ys"].reshape(NSEQ, DSEQ, D) for c in range(NCORES)], axis=0)

    def unp(name, nch, rows):
        return np.stack([R[c][name].reshape(128, nch, rows).transpose(2, 1, 0).reshape(rows, nch * 128)
                         for c in range(NCORES)], axis=0)[None]

    def uns(name, nch, rows):
        return np.concatenate([R[c][name].reshape(128, nch, NSEQ, rows).transpose(2, 3, 1, 0).reshape(NSEQ, rows, nch * 128)
                               for c in range(NCORES)], axis=0)[None]
    outs = (y_prompt, y_sample,
            unp("o_pool_p", 4, 15), unp("o_conv_p", 4, 2), unp("o_ffn_p", 44, 2),
            uns("o_pool_s", 4, 15), uns("o_conv_s", 4, 2), uns("o_ffn_s", 44, 2))
    return tuple(np.ascontiguousarray(o.astype(np.float32)) for o in outs)
```

```python
from contextlib import ExitStack

import numpy as np
import concourse.bass as bass
import concourse.mybir as mybir
from concourse.bass_utils import run_bass_kernel_spmd

F32 = mybir.dt.float32
BF16 = mybir.dt.bfloat16
AF = mybir.ActivationFunctionType
ALU = mybir.AluOpType

D = 1024
NCORES = 8
SEQ = 2048
NMETA = 16
NSEQ = 16
DSEQ = 4
NPAD = 4
SOFF = NMETA + NPAD
NSM = SOFF + NSEQ * DSEQ
NQ = NSM // 4
DFF = 2816
NFF = 22
EPS = 1e-6
NST = 4
TW = 512
WINDOWS = (2, 4, 8, 16)
NW = 4
DEBUG = False
WARM_S2, WARM_S3A, WARM_S3B, WARM_S1 = 6, 12, 6, 6

C_G1, C_G2, C_PS, C_CW, C_FW, C_FB, NCST = 0, 8, 16, 20, 32, 164, 208

ENGS = ("pe", "act", "dve", "pool", "sp")
SAME_ENG_DIST = 1 << 30


class Op:
    __slots__ = ("eng", "fn", "deps", "dsem", "idx", "pos", "sig", "signal")

    def __init__(self, eng, fn, deps, dsem, idx):
        self.eng = eng
        self.fn = fn
        self.deps = deps
        self.dsem = dsem
        self.idx = idx
        self.pos = None
        self.sig = None
        self.signal = False


class Prog:
    def __init__(self):
        self.ops = []
        self.last_writer = {}
        self.readers = {}

    def op(self, eng, fn, reads=(), writes=(), dsem=None):
        deps = set()
        for k in reads:
            w = self.last_writer.get(k)
            if w is not None:
                deps.add(w)
        for k in writes:
            w = self.last_writer.get(k)
            if w is not None:
                deps.add(w)
            deps.update(self.readers.get(k, ()))
        idx = len(self.ops)
        self.ops.append(Op(eng, fn, deps, dsem, idx))
        for k in reads:
            self.readers.setdefault(k, []).append(idx)
        for k in writes:
            self.last_writer[k] = idx
            self.readers[k] = []
        return idx

    def emit(self, nc, final_wait_ops=()):
        ops = self.ops
        streams = {e: [] for e in ENGS}
        for o in ops:
            o.pos = len(streams[o.eng])
            streams[o.eng].append(o)
        for o in ops:
            for d in o.deps:
                p = ops[d]
                if p.dsem is not None:
                    p.signal = True
                elif p.eng != o.eng:
                    p.signal = True
                elif o.eng != "pe" and o.pos - p.pos <= SAME_ENG_DIST:
                    p.signal = True
        for d in final_wait_ops:
            ops[d].signal = True
        cnt = {}
        dsems = []
        for o in ops:
            if o.dsem is not None:
                if o.dsem not in cnt:
                    dsems.append(o.dsem)
                cnt[o.dsem] = cnt.get(o.dsem, 0) + 16
                o.sig = (o.dsem, cnt[o.dsem])
            elif o.signal:
                k = "E_" + o.eng
                cnt[k] = cnt.get(k, 0) + 1
                o.sig = (k, cnt[k])
        with ExitStack() as es:
            sems = {}
            for k in ["E_" + e for e in ENGS] + dsems:
                sems[k] = es.enter_context(nc.semaphore(k))
            block = es.enter_context(nc.Block())

            def run_stream(ename, eng):
                waited = {}
                for o in streams[ename]:
                    need = {}
                    for d in o.deps:
                        p = ops[d]
                        if p.sig is None:
                            continue
                        if p.dsem is None and p.eng == o.eng and (o.eng == "pe" or o.pos - p.pos > SAME_ENG_DIST):
                            continue
                        s, v = p.sig
                        if need.get(s, 0) < v:
                            need[s] = v
                    for s, v in need.items():
                        if waited.get(s, 0) < v:
                            eng.wait_ge(sems[s], v)
                            waited[s] = v
                    ins = o.fn(eng)
                    if o.sig is not None:
                        ins.then_inc(sems[o.sig[0]], 16 if o.dsem is not None else 1)
                if ename == "sp":
                    for d in final_wait_ops:
                        s, v = ops[d].sig
                        if waited.get(s, 0) < v:
                            eng.wait_ge(sems[s], v)
                            waited[s] = v

            @block.tensor
            def _(e):
                run_stream("pe", e)

            @block.scalar
            def _(e):
                run_stream("act", e)

            @block.vector
            def _(e):
                run_stream("dve", e)

            @block.gpsimd
            def _(e):
                run_stream("pool", e)

            @block.sync
            def _(e):
                run_stream("sp", e)


def build_nc():
    nc = bass.Bass("TRN2", target_bir_lowering=False)

    def din(name, shape):
        return nc.dram_tensor(name, shape, F32, kind="ExternalInput").ap()

    def dout(name, shape):
        return nc.dram_tensor(name, shape, F32, kind="ExternalOutput").ap()

    xp = din("xp", [SEQ, D])
    xsm = din("xsm", [NSM, D])
    w_in = din("w_in", [D, 2048])
    w_up = din("w_up", [D, 2 * DFF])
    w_out = din("w_out", [D, D])
    w_down = din("w_down", [DFF, D])
    w_pool = din("w_pool", [512, 128])
    cst = din("cst", [128, NCST])
    gfin = din("gfin", [128, D])
    st_pool = din("st_pool", [128, 4 * NSEQ * 15])
    st_conv = din("st_conv", [128, 4 * NSEQ * 2])
    st_ffn = din("st_ffn", [128, 44 * NSEQ * 2])

    yp = dout("yp", [SEQ, D])
    ys = dout("ys", [NSEQ * DSEQ, D])
    o_pool_p = dout("o_pool_p", [128, 4 * 15])
    o_conv_p = dout("o_conv_p", [128, 4 * 2])
    o_ffn_p = dout("o_ffn_p", [128, 44 * 2])
    o_pool_s = dout("o_pool_s", [128, 4 * NSEQ * 15])
    o_conv_s = dout("o_conv_s", [128, 4 * NSEQ * 2])
    o_ffn_s = dout("o_ffn_s", [128, 44 * NSEQ * 2])

    P = Prog()
    finals = []
    if DEBUG:
        dbg_aT = nc.dram_tensor("dbg_aT", [128, 8 * TW], BF16, kind="ExternalOutput").ap()
        dbg_h2T = nc.dram_tensor("dbg_h2T", [128, 8 * TW], BF16, kind="ExternalOutput").ap()
        dbg_hT = nc.dram_tensor("dbg_hT", [128, 8 * TW], BF16, kind="ExternalOutput").ap()
        dbg_x1 = nc.dram_tensor("dbg_x1", [128, 4 * D], F32, kind="ExternalOutput").ap()
        dbg_actT = nc.dram_tensor("dbg_actT", [128, NFF * TW], BF16, kind="ExternalOutput").ap()

    with ExitStack() as es:
        def sb(name, shape, dt=F32):
            return es.enter_context(nc.sbuf_tensor(name, shape, dt))

        def psum(name, shape, dt=F32):
            return es.enter_context(nc.psum_tensor(name, shape, dt))

        wslot = [sb(f"wslot{i}", [128, 8, 256], BF16) for i in range(NW)]
        w_out_sb = sb("w_out_sb", [128, 8, D], BF16)
        w_down_sb = sb("w_down_sb", [128, NFF, D], BF16)
        w_pool_sb = sb("w_pool_sb", [128, 4, 128], BF16)
        cst_sb = sb("cst_sb", [128, NCST])
        gfin_sb = sb("gfin_sb", [128, D])
        ident = sb("ident", [128, 128], BF16)
        epst = sb("epst", [128, 1])
        invcnt = sb("invcnt", [128, 4, 16])
        x1 = sb("x1", [128, 4, D])
        xrot = [sb(f"xrot{i}", [128, D]) for i in range(2)]
        hbuf = [sb(f"hbuf{i}", [128, D], BF16) for i in range(2)]
        hb0 = sb("hb0", [128, D], BF16)
        hT = sb("hT", [128, 8, TW], BF16)
        aT = sb("aT", [128, 8, TW], BF16)
        actT = sb("actT", [128, NFF, TW], BF16)
        xs_sb = sb("xs_sb", [128, D])
        hs = sb("hs", [128, D], BF16)
        hTs = sb("hTs", [128, 8, NSM], BF16)
        aTs = sb("aTs", [128, 8, NSM], BF16)
        actTs = sb("actTs", [128, NFF, NSM], BF16)
        ss_all = sb("ss_all", [128, 64])
        rs_all = sb("rs_all", [128, 64])
        Dsm = [sb(f"Dsm{i}", [128, NSM], BF16) for i in range(4)]
        Usave = sb("Usave", [128, 4, 15])
        CVsave = sb("CVsave", [128, 4, 2])
        Esave = sb("Esave", [128, 44, 2])
        OFp = sb("OFp", [128, 44, 2])
        stp_sb = sb("stp_sb", [128, 4, NSEQ, 15])
        stc_sb = sb("stc_sb", [128, 4, NSEQ, 2])
        stf_sb = sb("stf_sb", [128, 44, NSEQ, 2])
        Um = [sb(f"Um{i}", [128, 31]) for i in range(2)]
        Us = [sb(f"Us{i}", [128, NSEQ, 19]) for i in range(2)]
        Am = [sb(f"Am{i}", [128, 31]) for i in range(2)]
        As = [sb(f"As{i}", [128, NSEQ, 19]) for i in range(2)]
        GCs = sb("GCs", [128, NSM])
        Xcv = [sb(f"Xcv{i}", [128, NQ, 6]) for i in range(2)]
        Tcs = [sb(f"Tcs{i}", [128, NQ, 4]) for i in range(2)]
        X3p = [sb(f"X3p{i}", [128, 2, NQ, 6]) for i in range(2)]
        T3p = [sb(f"T3p{i}", [128, 2, NQ, 4]) for i in range(2)]
        stfV = stf_sb[:].rearrange("p (h c) b t -> p h c b t", h=2)
        ARENA = 2 * 527 + 2 * 527 + 512 + 2 * 514 + 2 * 512 + 4 * 256
        arena = sb("arena", [128, ARENA])
        off = 0

        def carve(n):
            nonlocal off
            v = arena[:, off:off + n]
            off += n
            return v
        U = [carve(527) for _ in range(2)]
        AB = [carve(527) for _ in range(2)]
        GC = [carve(512)]
        Dbuf = [carve(256).bitcast(BF16) for _ in range(4)]
        CV = [carve(514) for _ in range(2)]
        TT = [carve(512) for _ in range(2)]
        assert off == ARENA
        NT1P = 5
        T1p = [arena[:, i * 1028:(i + 1) * 1028].rearrange("p (h n) -> p h n", h=2) for i in range(NT1P)]
        assert NT1P * 1028 <= ARENA
        EsV = Esave[:].rearrange("p (h c) t -> p h c t", h=2)

        G = [psum(f"G{i}", [128, TW]) for i in range(6)]
        TR = [psum(f"TR{i}", [128, 8, 128], BF16) for i in range(2)]
        NG = len(G)
        gctr = [0]
        trctr = [0]
        nctr = [0]

        gfree = list(range(NG))
        gref = {}

        def galloc(nref=1):
            assert gfree, "out of PSUM banks (program-order allocation)"
            i = gfree.pop(0)
            gref[i] = nref
            return i

        def gdone(i):
            gref[i] -= 1
            assert gref[i] >= 0
            if gref[i] == 0:
                gfree.append(i)

        def gk(g):
            return ("G", g)

        def gt(g):
            return ("Gt", g)

        def sub(g, k):
            return G[g][:, k * NSM:(k + 1) * NSM]

        S1T = ["arS1_dve", "arS1_pe", "arS1_pool"]
        S3T = ["arS3_act", "arS3_pool"]

        def cc(col):
            return cst_sb[:, col:col + 1]

        P.op("sp", lambda e: e.dma_start(out=cst_sb[:], in_=cst[:, :]), writes=["cst"], dsem="D_cst")
        P.op("sp", lambda e: e.dma_start(out=xs_sb[0:NSM, :], in_=xsm[:, :]), writes=["xs"], dsem="D_xs")

        P.op("pool", lambda e: e.memset(x1[:, 0, 0:128], 0.0), writes=[("x1", 0)])
        P.op("pool", lambda e: e.affine_select(out=x1[:, 0, 0:128], in_=x1[:, 0, 0:128], pattern=[[-1, 128]],
                                               compare_op=ALU.not_equal, fill=1.0, base=0, channel_multiplier=1),
             reads=[("x1", 0)], writes=[("x1", 0)])
        P.op("pool", lambda e: e.tensor_copy(out=ident[:], in_=x1[:, 0, 0:128]), reads=[("x1", 0)], writes=["ident"])

        def mk_consts(e):
            e.memset(epst[:], EPS)
            for g, w in enumerate(WINDOWS):
                e.memset(invcnt[:, g, w - 1:16], 1.0 / w)
                for t in range(w - 1):
                    e.memset(invcnt[:, g, t:t + 1], 1.0 / (t + 1))
            for i in range(2):
                e.memset(Um[i][:, 0:15], 0.0)
                e.memset(Xcv[i][:, 0, 0:2], 0.0)
            for i in range(4):
                e.memset(Dsm[i][:, NMETA:SOFF], 0.0)
            for i in range(2):
                ins = e.memset(X3p[i][:, :, 0, 0:2], 0.0)
            return ins
        P.op("pool", mk_consts, writes=["epst", "invcnt", "zeros"])
        P.op("sp", lambda e: e.dma_start(out=stp_sb[:].rearrange("p a b c -> p (a b c)"), in_=st_pool[:, :]),
             writes=[("stp", j) for j in range(4)], dsem="D_stp")
        P.op("sp", lambda e: e.dma_start(out=stc_sb[:].rearrange("p a b c -> p (a b c)"), in_=st_conv[:, :]),
             writes=[("stc", j) for j in range(4)], dsem="D_stc")
        P.op("sp", lambda e: e.dma_start(out=stf_sb[:].rearrange("p a b c -> p (a b c)"), in_=st_ffn[:, :]),
             writes=[("stf", c) for c in range(44)], dsem="D_stf")
        P.op("sp", lambda e: e.dma_start(out=gfin_sb[:], in_=gfin[:, :]), writes=["gfin"], dsem="D_gfin")

        S1_COL = {"u": 0, "gb": 512, "gc": 1024, "v": 1536}
        S1_PAIRS_G = ((("gc", 0), ("v", 0)), (("gb", 0), ("gc", 1)), (("u", 3), ("u", 2)), (("v", 1), ("gb", 1)),
                      (("gc", 2), ("v", 2)), (("u", 1), ("u", 0)), (("gb", 2), ("gc", 3)), (("v", 3), ("gb", 3)))
        wplan = []
        for s in range(NST):
            for pair in S1_PAIRS_G:
                (a, b) = [S1_COL[kind] + 128 * j for (kind, j) in pair]
                wplan.append([(0, w_in[:, a:a + 128]), (128, w_in[:, b:b + 128])])
            for j in range(NFF):
                wplan.append([(0, w_up[:, j * 128:(j + 1) * 128]), (128, w_up[:, DFF + j * 128:DFF + (j + 1) * 128])])
        wissued = [0]
        resident = []

        def res_dma(dst, src, key, name):
            resident.append((dst, src, key, name))
        res_dma(w_pool_sb[:], w_pool.rearrange("(g c) d -> c g d", c=128), "wpool", "D_wpool")
        for q in range(4):
            res_dma(w_out_sb[:, :, q * 256:(q + 1) * 256],
                    w_out[:, q * 256:(q + 1) * 256].rearrange("(k p) n -> p k n", p=128), ("wout", q), f"D_wout{q}")
        for q in range(11):
            res_dma(w_down_sb[:, 2 * q:2 * q + 2, :],
                    w_down[q * 256:(q + 1) * 256, :].rearrange("(k p) n -> p k n", p=128), ("wdown", q), f"D_wdown{q}")
        res_i = [0]

        def issue_resident(n=1):
            for _ in range(n):
                if res_i[0] < len(resident):
                    dst, src, key, name = resident[res_i[0]]
                    res_i[0] += 1
                    P.op("pool", lambda e, dst=dst, src=src: e.dma_start(out=dst, in_=src), writes=[key], dsem=name)

        def prefetch(upto):
            while wissued[0] <= min(upto, len(wplan) - 1):
                wi = wissued[0]
                slot = wi % NW
                for h, (c0, src) in enumerate(wplan[wi]):
                    P.op("pool", lambda e, slot=slot, c0=c0, src=src: e.dma_start(
                        out=wslot[slot][:, :, c0:c0 + 128], in_=src.rearrange("(k p) n -> p k n", p=128)),
                        writes=[("W", slot, h)], dsem=f"D_w{slot}_{h}")
                wissued[0] += 1
                if wi >= 1:
                    issue_resident(1)

        wuse = [0]

        def next_weights(ahead=0):
            wi = wuse[0]
            wuse[0] += 1
            prefetch(wi + NW - 1 - ahead)
            return wi % NW

        def norm_stats(src, nrows, junk, junkkey, srckeys):
            n = nctr[0]
            nctr[0] += 1
            P.op("act", lambda e: e.activation(out=junk, in_=src, func=AF.Square, accum_out=ss_all[0:nrows, n:n + 1]),
                 reads=srckeys, writes=[("ss", n), junkkey])
            P.op("act", lambda e: e.activation(out=rs_all[0:nrows, n:n + 1], in_=ss_all[0:nrows, n:n + 1], func=AF.Sqrt,
                                               scale=1.0 / D, bias=epst[0:nrows, :]),
                 reads=[("ss", n), "epst"], writes=[("rs", n)])
            P.op("dve", lambda e: e.reciprocal(out=rs_all[0:nrows, n:n + 1], in_=rs_all[0:nrows, n:n + 1]),
                 reads=[("rs", n)], writes=[("rs", n)])
            return n

        def norm_transpose(src, nrows, srckeys, hb, hk, dstT, col0, gcol, dstkeys):
            hbv = hb[0:nrows, :]
            n = norm_stats(src, nrows, hbv, hk, srckeys)
            P.op("dve", lambda e: e.tensor_scalar(out=hbv, in0=src, scalar1=rs_all[0:nrows, n:n + 1], scalar2=None,
                                                  op0=ALU.mult),
                 reads=list(srckeys) + [("rs", n)], writes=[hk])

            def part_b():
                tr = TR[trctr[0] % 2]
                trk = ("TR", trctr[0] % 2)
                trctr[0] += 1

                def do_tr(e):
                    for k in range(8):
                        ins = e.transpose(out=tr[:, k, 0:nrows], in_=hb[0:nrows, k * 128:(k + 1) * 128],
                                          identity=ident[0:nrows, 0:nrows])
                    return ins
                P.op("pe", do_tr, reads=[hk, "ident"], writes=[trk])
                P.op("dve", lambda e: e.tensor_tensor(
                    out=dstT[:, :, col0:col0 + nrows], in0=tr[:, :, 0:nrows],
                    in1=cst_sb[:, gcol:gcol + 8].unsqueeze(2).to_broadcast([128, 8, nrows]), op=ALU.mult),
                    reads=[trk, "cst"], writes=dstkeys)
            return part_b

        def v3(ap, b=4):
            return ap.rearrange("p (a b) -> p a b", b=b)

        def stage0_first():
            hbs = [(hb0, "hb0"), (hbuf[0], ("hb", 0)), (hbuf[1], ("hb", 1))]
            pbs = []
            for i in range(4):
                hb, hk = hbs[i % 3]
                pbs.append(norm_transpose(x1[:, i, :], 128, [("x1", i)], hb, hk, hT, i * 128, C_G1, [("hT", i)]))
                if i >= 1:
                    pbs.pop(0)()
            for pb in pbs:
                pb()

        def stage0_small():
            norm_transpose(xs_sb[0:NSM, :], NSM, ["xs"], hs, "hs", hTs, 0, C_G1, ["hTs"])()

        def x1_reload(s):
            for i in range(4):
                r0 = (s * 4 + i) * 128
                P.op("sp", lambda e, i=i, r0=r0: e.dma_start(out=x1[:, i, :], in_=xp[r0:r0 + 128, :]),
                     writes=[("x1", i)], dsem=f"D_x1_{i}")

        def mm_feat(slot, half, rhsT, rhs_keys, ncols, out_ap, g, nk=8):
            def f(e):
                for k in range(nk):
                    ins = e.matmul(out_ap, lhsT=wslot[slot][:, k, half * 128:(half + 1) * 128],
                                   rhs=rhsT[:, k, 0:ncols], start=(k == 0), stop=(k == nk - 1))
                return ins
            P.op("pe", f, reads=[("W", slot, half)] + rhs_keys, writes=[gk(g), gt(g)])

        S1_PAIRS = S1_PAIRS_G

        def stage1(s):
            small = (s == 0)
            hkeys = [("hT", i) for i in range(4)]
            deferred = []
            slots = {}
            for pi, pair in enumerate(S1_PAIRS):
                slot = next_weights()
                gsm = galloc(2) if small else None
                nd = []
                for (rdy, fn) in deferred:
                    if pi >= rdy:
                        fn()
                    else:
                        nd.append((rdy, fn))
                deferred[:] = nd
                for half, (kind, j) in enumerate(pair):
                    g = galloc()
                    mm_feat(slot, half, hT, hkeys, TW, G[g][:, :], g)
                    slots[(kind, j)] = (g, gsm, half)
                if small:
                    for half in range(2):
                        mm_feat(slot, half, hTs, ["hTs"], NSM, sub(gsm, half), gsm)
                for half, (kind, j) in enumerate(pair):
                    g = slots[(kind, j)][0]
                    if kind == "u":
                        rest = small_u(j, gsm, half) if small else None
                        rdy = 99 if small else pi + 3
                        deferred.append((rdy, main_u(s, j, g)))
                        if rest is not None:
                            deferred.append((rdy, rest()))
                    elif kind == "v":
                        g_gc, gsm_gc, half_gc = slots[("gc", j)]
                        rest = small_cv(j, gsm_gc, half_gc, gsm, half) if small else None
                        main_cv(s, j, g_gc, g)
                        if rest is not None:
                            rest()
                    elif kind == "gb":
                        bj = j % 2
                        P.op("dve", lambda e, g=g, j=j, bj=bj: e.tensor_tensor(
                            out=aT[:, 4 + j, :], in0=G[g][:, :], in1=TT[bj], op=ALU.mult),
                            reads=[gk(g), ("TT", bj)], writes=[("aT", 4 + j, i) for i in range(4)] + [gt(g), "arS1_dve"])
                        gdone(g)
                        if small:
                            P.op("dve", lambda e, gsm=gsm, half=half, j=j, bj=bj: e.tensor_tensor(
                                out=v3(aTs[:, 4 + j, :]), in0=v3(sub(gsm, half)), in1=Tcs[bj][:], op=ALU.mult),
                                reads=[gk(gsm), ("Tcs", bj)], writes=[("aTs", 4 + j), gt(gsm)])
                            gdone(gsm)
            for (rdy, fn) in deferred:
                fn()

        def small_u(j, gsm, half):
            W = WINDOWS[j]
            b = j % 2
            um, us = Um[b], Us[b]
            sp_ap = sub(gsm, half)
            P.op("act", lambda e: e.activation(out=um[:, 15:31], in_=sp_ap[:, 0:16], func=AF.Copy),
                 reads=[gk(gsm), "zeros"], writes=[("Um", b), gt(gsm)])
            P.op("act", lambda e: e.activation(out=us[:, :, 15:19], in_=v3(sp_ap[:, SOFF:NSM]), func=AF.Copy),
                 reads=[gk(gsm)], writes=[("UsN", b), gt(gsm)])
            gdone(gsm)
            P.op("act", lambda e: e.activation(out=Usave[:, j, :], in_=um[:, 16:31], func=AF.Copy),
                 reads=[("Um", b)], writes=[("Usave", j)])

            def rest():
                P.op("pool", lambda e: e.tensor_copy(out=us[:, :, 0:15], in_=stp_sb[:, j, :, :]),
                     reads=[("stp", j)], writes=[("UsH", b)])
                P.op("pool", lambda e: e.tensor_copy(out=stp_sb[:, j, :, :], in_=us[:, :, 4:19]),
                     reads=[("UsH", b), ("UsN", b)], writes=[("stp", j)])
                cur_m, cur_s = um, us
                m = 1
                si = 0
                while m < W:
                    lo = 15 - (W - 2 * m)
                    dst_m, dst_s = Am[si % 2], As[si % 2]

                    def f(e, cur_m=cur_m, cur_s=cur_s, dst_m=dst_m, dst_s=dst_s, m=m, lo=lo):
                        e.tensor_tensor(out=dst_m[:, lo:31], in0=cur_m[:, lo:31], in1=cur_m[:, lo - m:31 - m], op=ALU.add)
                        return e.tensor_tensor(out=dst_s[:, :, lo:19], in0=cur_s[:, :, lo:19],
                                               in1=cur_s[:, :, lo - m:19 - m], op=ALU.add)
                    P.op("dve", f, reads=[("Um", b), ("UsH", b), ("UsN", b), "smscr"], writes=["smscr"])
                    cur_m, cur_s = dst_m, dst_s
                    m *= 2
                    si += 1
                P.op("dve", lambda e, cur_m=cur_m: e.tensor_tensor(out=cur_m[:, 15:31], in0=cur_m[:, 15:31],
                                                                 in1=invcnt[:, j, :], op=ALU.mult),
                     reads=["smscr", "invcnt"], writes=["smscr"])

                def comb(e, cur_m=cur_m, cur_s=cur_s):
                    e.tensor_tensor(out=Dsm[j][:, 0:16], in0=cur_m[:, 15:31], in1=um[:, 15:31], op=ALU.subtract)
                    return e.scalar_tensor_tensor(out=v3(Dsm[j][:, SOFF:NSM]), in0=cur_s[:, :, 15:19], scalar=1.0 / W,
                                                  in1=us[:, :, 15:19], op0=ALU.mult, op1=ALU.subtract)
                P.op("dve", comb, reads=["smscr", ("Um", b), ("UsN", b), "invcnt"], writes=["smscr", ("Dsm", j)])

                def later():
                    g2 = galloc()
                    so = sub(g2, 0)
                    P.op("pe", lambda e: e.matmul(so, lhsT=w_pool_sb[:, j, :], rhs=Dsm[j][:, :], start=True, stop=True),
                         reads=[("Dsm", j), "wpool"], writes=[gk(g2), gt(g2)])
                    P.op("act", lambda e: e.activation(out=aTs[:, j, :], in_=so, func=AF.Copy, scale=cc(C_PS + j)),
                         reads=[gk(g2), "cst"], writes=[("aTs", j), gt(g2)])
                    gdone(g2)
                return later
            return rest

        def main_u(s, j, g):
            W = WINDOWS[j]
            b = j % 2
            u = U[b]
            P.op("act", lambda e: e.activation(out=u[:, 0:15], in_=Usave[:, j, :], func=AF.Copy),
                 reads=[("Usave", j)] + S3T, writes=[("UH", b)])
            P.op("act", lambda e: e.activation(out=u[:, 15:527], in_=G[g][:, :], func=AF.Copy),
                 reads=[gk(g)] + S3T, writes=[("UN", b), gt(g)])
            gdone(g)
            P.op("act", lambda e: e.activation(out=Usave[:, j, :], in_=u[:, 512:527], func=AF.Copy),
                 reads=[("UN", b)], writes=[("Usave", j), "arS1_pool"])
            cur = u
            m = 1
            si = 0
            while m < W:
                lo = 15 - (W - 2 * m)
                dst = AB[si % 2]
                P.op("dve", lambda e, cur=cur, dst=dst, m=m, lo=lo: e.tensor_tensor(
                    out=dst[:, lo:527], in0=cur[:, lo:527], in1=cur[:, lo - m:527 - m], op=ALU.add),
                    reads=[("UH", b), ("UN", b), "AB"] + S3T, writes=["AB", "arS1_dve"])
                cur = dst
                m *= 2
                si += 1
            db = Dbuf[j]
            P.op("dve", lambda e, cur=cur: e.scalar_tensor_tensor(
                out=db[:, :], in0=cur[:, 15:527], scalar=1.0 / W, in1=u[:, 15:527], op0=ALU.mult, op1=ALU.subtract),
                reads=["AB", ("UN", b)] + S3T, writes=[("D", j), "arS1_dve"])

            def later():
                g2 = galloc()
                P.op("pe", lambda e: e.matmul(G[g2][:, :], lhsT=w_pool_sb[:, j, :], rhs=db[:, :], start=True, stop=True),
                     reads=[("D", j), "wpool"], writes=[gk(g2), gt(g2), "arS1_pe"])
                P.op("act", lambda e: e.activation(out=aT[:, j, :], in_=G[g2][:, :], func=AF.Copy, scale=cc(C_PS + j)),
                     reads=[gk(g2), "cst"], writes=[("aT", j, i) for i in range(4)] + [gt(g2)])
                gdone(g2)
            return later

        def small_cv(j, g_gc, h_gc, g_v, h_v):
            b = j % 2
            x = Xcv[b]
            t = Tcs[b]
            gc_ap = sub(g_gc, h_gc)
            v_ap = sub(g_v, h_v)
            P.op("act", lambda e: e.activation(out=GCs[:, :], in_=gc_ap, func=AF.Copy),
                 reads=[gk(g_gc)], writes=["GCs", gt(g_gc)])
            gdone(g_gc)
            P.op("dve", lambda e: e.tensor_tensor(out=x[:, :, 2:6], in0=v3(v_ap), in1=v3(GCs[:, :]), op=ALU.mult),
                 reads=[gk(g_v), "GCs"], writes=[("XcvN", b), gt(g_v)])
            gdone(g_v)
            P.op("act", lambda e: e.activation(out=CVsave[:, j, :], in_=x[:, 3, 4:6], func=AF.Copy),
                 reads=[("XcvN", b)], writes=[("CVsave", j)])

            def rest():
                def hist(e):
                    e.tensor_copy(out=x[:, 1:5, 0:2], in_=x[:, 0:4, 4:6])
                    return e.tensor_copy(out=x[:, 5:NQ, 0:2], in_=stc_sb[:, j, :, :])
                P.op("pool", hist, reads=[("XcvN", b), ("stc", j), "zeros"], writes=[("XcvH", b)])
                P.op("pool", lambda e: e.tensor_copy(out=stc_sb[:, j, :, :], in_=x[:, 5:NQ, 4:6]),
                     reads=[("XcvN", b), ("XcvH", b)], writes=[("stc", j)])
                P.op("act", lambda e: e.activation(out=t[:], in_=x[:, :, 0:4], func=AF.Copy, scale=cc(C_CW + 3 * j)),
                     reads=[("XcvN", b), ("XcvH", b), "cst"], writes=[("Tcs", b)])
                P.op("dve", lambda e: e.scalar_tensor_tensor(out=t[:], in0=x[:, :, 1:5], scalar=cc(C_CW + 3 * j + 1),
                                                              in1=t[:], op0=ALU.mult, op1=ALU.add),
                     reads=[("XcvN", b), ("XcvH", b), ("Tcs", b)], writes=[("Tcs", b)])
                P.op("dve", lambda e: e.scalar_tensor_tensor(out=t[:], in0=x[:, :, 2:6], scalar=cc(C_CW + 3 * j + 2),
                                                              in1=t[:], op0=ALU.mult, op1=ALU.add),
                     reads=[("XcvN", b), ("Tcs", b)], writes=[("Tcs", b)])
            return rest

        def main_cv(s, j, g_gc, g_v):
            b = j % 2
            cv = CV[b]
            t = TT[b]
            gcb = GC[0]
            P.op("act", lambda e: e.activation(out=gcb, in_=G[g_gc][:, :], func=AF.Copy),
                 reads=[gk(g_gc)] + S3T, writes=["GC", gt(g_gc)])
            gdone(g_gc)
            P.op("act", lambda e: e.activation(out=cv[:, 0:2], in_=CVsave[:, j, :], func=AF.Copy),
                 reads=[("CVsave", j)] + S3T, writes=[("CVH", b)])
            P.op("dve", lambda e: e.tensor_tensor(out=cv[:, 2:514], in0=G[g_v][:, :], in1=gcb, op=ALU.mult),
                 reads=[gk(g_v), "GC"] + S3T, writes=[("CVN", b), gt(g_v), "arS1_dve"])
            gdone(g_v)
            P.op("act", lambda e: e.activation(out=CVsave[:, j, :], in_=cv[:, 512:514], func=AF.Copy),
                 reads=[("CVN", b)], writes=[("CVsave", j), "arS1_pool"])
            P.op("act", lambda e: e.activation(out=t, in_=cv[:, 0:512], func=AF.Copy, scale=cc(C_CW + 3 * j)),
                 reads=[("CVN", b), ("CVH", b), "cst"] + S3T, writes=[("TT", b)])
            P.op("dve", lambda e: e.scalar_tensor_tensor(out=t, in0=cv[:, 1:513], scalar=cc(C_CW + 3 * j + 1), in1=t,
                                                          op0=ALU.mult, op1=ALU.add),
                 reads=[("CVN", b), ("CVH", b), ("TT", b)], writes=[("TT", b)])
            P.op("dve", lambda e: e.scalar_tensor_tensor(out=t, in0=cv[:, 2:514], scalar=cc(C_CW + 3 * j + 2), in1=t,
                                                          op0=ALU.mult, op1=ALU.add),
                 reads=[("CVN", b), ("TT", b)], writes=[("TT", b), "arS1_dve"])

        def mm_tok(lhsT_of_k, lkey_of_k, nrows, wsb, wkeys, korder, ngrp, ga, gb):
            nk = len(korder)
            per = (nk + ngrp - 1) // ngrp
            for gi in range(0, nk, per):
                ks = korder[gi:gi + per]

                def f(e, ks=ks, gi=gi):
                    for n, k in enumerate(ks):
                        first = (gi + n == 0)
                        lastk = (gi + n == nk - 1)
                        e.matmul(G[ga][0:nrows, :], lhsT=lhsT_of_k(k), rhs=wsb[:, k, 0:512], start=first, stop=lastk)
                        ins = e.matmul(G[gb][0:nrows, :], lhsT=lhsT_of_k(k), rhs=wsb[:, k, 512:1024],
                                       start=first, stop=lastk)
                    return ins
                P.op("pe", f, reads=[lkey_of_k(k) for k in ks] + list(wkeys), writes=[gk(ga), gk(gb), gt(ga), gt(gb)])

        def resid(dst, nrows, dkey, ga, gb):
            def f(e):
                e.tensor_tensor(out=dst[0:nrows, 0:512], in0=G[ga][0:nrows, :], in1=dst[0:nrows, 0:512], op=ALU.add)
                return e.tensor_tensor(out=dst[0:nrows, 512:1024], in0=G[gb][0:nrows, :], in1=dst[0:nrows, 512:1024],
                                       op=ALU.add)
            P.op("dve", f, reads=[gk(ga), gk(gb), dkey], writes=[dkey, gt(ga), gt(gb)])
            gdone(ga)
            gdone(gb)

        S2_KORDER = [4, 5, 6, 0, 1, 2, 3, 7]
        WOUT_KEYS = [("wout", q) for q in range(4)]
        WDOWN_KEYS = [("wdown", q) for q in range(11)]

        def warm(n):
            if n <= 0:
                return
            g = galloc()

            def f(e):
                for _ in range(n):
                    ins = e.matmul(G[g][:, :], lhsT=w_out_sb[:, 0, 0:128], rhs=w_out_sb[:, 0, 0:512],
                                   start=True, stop=True)
                return ins
            P.op("pe", f, reads=[("wout", 0), ("wout", 1)], writes=[gk(g), gt(g)])
            gdone(g)

        def stage2(s):
            small = (s == 0)
            pend = []
            warm(WARM_S2)

            def do_small():
                ga, gb = galloc(), galloc()
                mm_tok(lambda k: aTs[:, k, 0:NSM], lambda k: ("aTs", k), NSM, w_out_sb, WOUT_KEYS, S2_KORDER, 4, ga, gb)
                resid(xs_sb, NSM, "xs", ga, gb)
                pbs = norm_transpose(xs_sb[0:NSM, :], NSM, ["xs"], hs, "hs", aTs, 0, C_G2,
                                     [("aTs", k) for k in range(8)])

                def pbs2(pbs=pbs):
                    pbs()
                    P.op("pool", lambda e: e.memset(aTs[:, :, NMETA:SOFF], 0.0), writes=[("aTs", k) for k in range(8)])
                return pbs2
            for i in range(4):
                ga, gb = galloc(), galloc()
                mm_tok(lambda k, i=i: aT[:, k, i * 128:(i + 1) * 128], lambda k, i=i: ("aT", k, i), 128,
                       w_out_sb, WOUT_KEYS, S2_KORDER, 4 if i == 0 else 1, ga, gb)
                resid(x1[:, i, :], 128, ("x1", i), ga, gb)
                while pend:
                    pend.pop(0)()
                pend.append(norm_transpose(x1[:, i, :], 128, [("x1", i)], hbuf[i % 2], ("hb", i % 2), aT, i * 128, C_G2,
                                           [("aT", k, i) for k in range(8)]))
                if small and i == 1:
                    pend.append(do_small())
            return pend

        t1ctr = [0]
        x3ctr = [0]

        def stage3(s):
            small = (s == 0)
            last = (s == NST - 1)
            akeys = [("aT", k, i) for k in range(8) for i in range(4)]
            askeys = [("aTs", k) for k in range(8)]
            prev_tail = None
            slots3 = {}
            LA = 2

            def get_slot(j, ahead=0):
                if j not in slots3:
                    slots3[j] = next_weights(ahead)
                return slots3[j]

            def small_mm(j):
                slot = get_slot(j, LA - 1 if j >= LA else j)
                gsm = galloc()
                for half in range(2):
                    mm_feat(slot, half, aTs, askeys, NSM, sub(gsm, half), gsm)
                return gsm
            if small:
                for jj in range(LA):
                    small_A(jj, small_mm(jj))
                small_B(0)
            for j in range(NFF):
                slot = get_slot(j)
                gs = []
                for half in range(2):
                    g = galloc()
                    gs.append(g)
                    mm_feat(slot, half, aT, akeys, TW, G[g][:, :], g)
                gsm_next = small_mm(j + LA) if (small and j + LA < NFF) else None
                if small and j + 1 < NFF:
                    small_B(j + 1)
                ti = t1ctr[0] % NT1P
                t1ctr[0] += 1
                tp = T1p[ti]
                kh = [("T1h", ti, h) for h in range(2)]
                kb = [("T1b", ti, h) for h in range(2)]
                kt = [("T1t", ti, h) for h in range(2)]
                cs = (j, NFF + j)
                ek = [("Esave", c) for c in cs]
                P.op("act", lambda e, tp=tp, j=j: e.activation(out=tp[:, :, 0:2], in_=EsV[:, :, j, :], func=AF.Copy),
                     reads=ek + S1T, writes=kh)
                for h in range(2):
                    c, g = cs[h], gs[h]
                    P.op("act", lambda e, tp=tp, g=g, c=c, h=h: e.activation(
                        out=tp[:, h, 2:514], in_=G[g][:, :], func=AF.Identity, scale=cc(C_FW + 3 * c), bias=cc(C_FB + c)),
                        reads=[gk(g), "cst"] + S1T, writes=[kb[h], kt[h], gt(g)])
                if prev_tail is not None:
                    prev_tail()
                if gsm_next is not None:
                    small_A(j + LA, gsm_next)
                for h in range(2):
                    c, g = cs[h], gs[h]
                    P.op("dve", lambda e, tp=tp, g=g, c=c, h=h: e.scalar_tensor_tensor(
                        out=tp[:, h, 1:513], in0=G[g][:, :], scalar=cc(C_FW + 3 * c + 1), in1=tp[:, h, 1:513],
                        op0=ALU.mult, op1=ALU.add),
                        reads=[gk(g), kh[h], kb[h], kt[h], "cst"], writes=[kh[h], kb[h], kt[h], gt(g)])
                for h in range(2):
                    c, g = cs[h], gs[h]
                    if last:
                        P.op("dve", lambda e, g=g, c=c: e.tensor_copy(out=OFp[:, c, :], in_=G[g][:, 510:512]),
                             reads=[gk(g)], writes=[("OFp", c), gt(g)])
                    P.op("dve", lambda e, tp=tp, g=g, c=c, h=h: e.scalar_tensor_tensor(
                        out=tp[:, h, 0:512], in0=G[g][:, :], scalar=cc(C_FW + 3 * c + 2), in1=tp[:, h, 0:512],
                        op0=ALU.mult, op1=ALU.add),
                        reads=[gk(g), kh[h], kb[h], "cst"], writes=[kh[h], kb[h], gt(g)])
                    gdone(g)

                def tail(tp=tp, j=j, kh=kh, kb=kb, kt=kt, ek=ek):
                    P.op("act", lambda e: e.activation(out=EsV[:, :, j, :], in_=tp[:, :, 512:514], func=AF.Copy),
                         reads=kt, writes=ek + ["arS3_act"])
                    P.op("act", lambda e: e.activation(out=tp[:, 1, 0:512], in_=tp[:, 1, 0:512], func=AF.Silu),
                         reads=[kh[1], kb[1]], writes=[kh[1], kb[1]])
                    P.op("pool", lambda e: e.tensor_tensor(out=actT[:, j, :], in0=tp[:, 1, 0:512], in1=tp[:, 0, 0:512],
                                                           op=ALU.mult),
                         reads=kh + kb, writes=[("actT", j, i) for i in range(4)] + ["arS3_pool"])
                prev_tail = tail
                if small:
                    small_C(j)
            prev_tail()

        def small_bufs(j):
            xi = j % 2
            return X3p[xi], T3p[xi], ("X3N", xi), ("X3H", xi), ("T3", xi), (j, NFF + j)

        def small_A(j, gsm):
            x, t, kx, kxh, ktt, cs = small_bufs(j)
            P.op("act", lambda e: e.activation(out=x[:, :, :, 2:6],
                                               in_=G[gsm][:, 0:2 * NSM].rearrange("p (h q t) -> p h q t", h=2, t=4),
                                               func=AF.Copy),
                 reads=[gk(gsm)], writes=[kx, gt(gsm)])
            gdone(gsm)

            def hist(e):
                e.tensor_copy(out=x[:, :, 1:5, 0:2], in_=x[:, :, 0:4, 4:6])
                return e.tensor_copy(out=x[:, :, 5:NQ, 0:2], in_=stfV[:, :, j, :, :])
            P.op("pool", hist, reads=[kx, ("stf", cs[0]), ("stf", cs[1]), "zeros"], writes=[kxh])
            P.op("pool", lambda e: e.tensor_copy(out=stfV[:, :, j, :, :], in_=x[:, :, 5:NQ, 4:6]),
                 reads=[kx, kxh], writes=[("stf", cs[0]), ("stf", cs[1])])

        def small_B(j):
            x, t, kx, kxh, ktt, cs = small_bufs(j)
            for h in range(2):
                c = cs[h]
                P.op("act", lambda e, h=h, c=c: e.activation(out=t[:, h], in_=x[:, h, :, 0:4], func=AF.Identity,
                                                             scale=cc(C_FW + 3 * c), bias=cc(C_FB + c)),
                     reads=[kx, kxh, "cst"], writes=[(ktt, h)])
            for tap in (1, 2):
                for h in range(2):
                    c = cs[h]
                    P.op("dve", lambda e, h=h, c=c, tap=tap: e.scalar_tensor_tensor(
                        out=t[:, h], in0=x[:, h, :, tap:tap + 4], scalar=cc(C_FW + 3 * c + tap), in1=t[:, h],
                        op0=ALU.mult, op1=ALU.add),
                        reads=[kx, kxh, (ktt, h), "cst"], writes=[(ktt, h)])
            P.op("pool", lambda e: e.tensor_copy(out=EsV[:, :, j, :], in_=t[:, :, NMETA // 4, 0:2]),
                 reads=[(ktt, 0), (ktt, 1)], writes=[("Esave", cs[0]), ("Esave", cs[1])])

        def small_C(j):
            x, t, kx, kxh, ktt, cs = small_bufs(j)
            P.op("act", lambda e: e.activation(out=t[:, 1], in_=t[:, 1], func=AF.Silu),
                 reads=[(ktt, 1)], writes=[(ktt, 1)])
            P.op("dve", lambda e: e.tensor_tensor(out=v3(actTs[:, j, :]), in0=t[:, 1], in1=t[:, 0], op=ALU.mult),
                 reads=[(ktt, 0), (ktt, 1)], writes=[("actTs", j)])

        def final_norm_store(dst, nrows, dkey, junk, junkkey, out_ap, src_rows, dsem):
            n = norm_stats(dst[0:nrows, :], nrows, junk, junkkey, [dkey])
            P.op("dve", lambda e: e.scalar_tensor_tensor(
                out=dst[0:nrows, :], in0=dst[0:nrows, :], scalar=rs_all[0:nrows, n:n + 1], in1=gfin_sb[0:nrows, :],
                op0=ALU.mult, op1=ALU.mult), reads=[dkey, ("rs", n), "gfin"], writes=[dkey])
            lo, hi = src_rows
            finals.append(P.op("sp", lambda e: e.dma_start(out=out_ap, in_=dst[lo:hi, :]), reads=[dkey], dsem=dsem))

        S4_KORDER = list(range(NFF))

        def stage0_tile(sn, i):
            gti = sn * 4 + i
            xr = xrot[gti % 2]
            xk = ("xrot", gti % 2)
            r0 = gti * 128
            P.op("sp", lambda e: e.dma_start(out=xr[:], in_=xp[r0:r0 + 128, :]), writes=[xk], dsem=f"D_xrot{gti % 2}")
            return norm_transpose(xr[:], 128, [xk], hb0, "hb0", hT, i * 128, C_G1, [("hT", i)])

        def stage4(s):
            small = (s == 0)
            nxt = s + 1 if s + 1 < NST else None
            def do_small4():
                ga, gb = galloc(), galloc()
                mm_tok(lambda k: actTs[:, k, 0:NSM], lambda k: ("actTs", k), NSM, w_down_sb, WDOWN_KEYS,
                       S4_KORDER, 3, ga, gb)
                resid(xs_sb, NSM, "xs", ga, gb)
                final_norm_store(xs_sb, NSM, "xs", hs[0:NSM, :], "hs", ys[:, :], (SOFF, NSM), "D_ys")
            pb = stage0_tile(nxt, 0) if nxt is not None else None
            for i in range(4):
                ga, gb = galloc(), galloc()
                mm_tok(lambda k, i=i: actT[:, k, i * 128:(i + 1) * 128], lambda k, i=i: ("actT", k, i), 128,
                       w_down_sb, WDOWN_KEYS, S4_KORDER, 4 if i == 0 else 1, ga, gb)
                if pb is not None:
                    pb()
                    pb = stage0_tile(nxt, i + 1) if i < 3 else None
                    if i == 3:
                        warm(WARM_S1)
                resid(x1[:, i, :], 128, ("x1", i), ga, gb)
                r0 = (s * 4 + i) * 128
                final_norm_store(x1[:, i, :], 128, ("x1", i), hbuf[i % 2][:, :], ("hb", i % 2), yp[r0:r0 + 128, :],
                                 (0, 128), f"D_x1_{i}")
                if small and i == 1:
                    do_small4()

        def state_outputs():
            finals.append(P.op("sp", lambda e: e.dma_start(out=o_pool_p[:, :], in_=Usave[:].rearrange("p a b -> p (a b)")),
                               reads=[("Usave", j) for j in range(4)], dsem="D_o1"))
            finals.append(P.op("sp", lambda e: e.dma_start(out=o_conv_p[:, :], in_=CVsave[:].rearrange("p a b -> p (a b)")),
                               reads=[("CVsave", j) for j in range(4)], dsem="D_o2"))
            finals.append(P.op("sp", lambda e: e.dma_start(out=o_ffn_p[:, :], in_=OFp[:].rearrange("p a b -> p (a b)")),
                               reads=[("OFp", c) for c in range(44)], dsem="D_o3"))
            finals.append(P.op("sp", lambda e: e.dma_start(out=o_pool_s[:, :], in_=stp_sb[:].rearrange("p a b c -> p (a b c)")),
                               reads=[("stp", j) for j in range(4)], dsem="D_o4"))
            finals.append(P.op("sp", lambda e: e.dma_start(out=o_conv_s[:, :], in_=stc_sb[:].rearrange("p a b c -> p (a b c)")),
                               reads=[("stc", j) for j in range(4)], dsem="D_o5"))
            finals.append(P.op("sp", lambda e: e.dma_start(out=o_ffn_s[:, :], in_=stf_sb[:].rearrange("p a b c -> p (a b c)")),
                               reads=[("stf", c) for c in range(44)], dsem="D_o6"))

        prefetch(NW - 2)
        x1_reload(0)
        stage0_small()
        stage0_first()
        for s in range(NST):
            if s > 0:
                x1_reload(s)
            if DEBUG and s == 0:
                finals.append(P.op("sp", lambda e: e.dma_start(out=dbg_hT[:, :], in_=hT[:].rearrange("p a b -> p (a b)")),
                                   reads=[("hT", i) for i in range(4)], dsem="D_dbg0"))
            stage1(s)
            if DEBUG and s == 0:
                finals.append(P.op("sp", lambda e: e.dma_start(out=dbg_aT[:, :], in_=aT[:].rearrange("p a b -> p (a b)")),
                                   reads=[("aT", k, i) for k in range(8) for i in range(4)], dsem="D_dbg1"))
            pend = stage2(s)
            warm(WARM_S3A)
            for fn in pend:
                fn()
            warm(WARM_S3B)
            if DEBUG and s == 0:
                finals.append(P.op("sp", lambda e: e.dma_start(out=dbg_h2T[:, :], in_=aT[:].rearrange("p a b -> p (a b)")),
                                   reads=[("aT", k, i) for k in range(8) for i in range(4)], dsem="D_dbg2"))
                finals.append(P.op("sp", lambda e: e.dma_start(out=dbg_x1[:, :], in_=x1[:].rearrange("p a b -> p (a b)")),
                                   reads=[("x1", i) for i in range(4)], dsem="D_dbg3"))
            stage3(s)
            if DEBUG and s == 0:
                finals.append(P.op("sp", lambda e: e.dma_start(out=dbg_actT[:, :], in_=actT[:].rearrange("p a b -> p (a b)")),
                                   reads=[("actT", j, i) for j in range(NFF) for i in range(4)], dsem="D_dbg4"))
            if s == NST - 1:
                state_outputs()
            stage4(s)
        while res_i[0] < len(resident):
            issue_resident(1)
        P.emit(nc, final_wait_ops=finals)
    return nc


def _col(v, n):
    return np.ascontiguousarray(np.asarray(v, np.float32).reshape(n, 128).T)


_NC_CACHE = {}


def kernel(x_prompt, x_sample, state_pool, state_conv, state_ffn, meta_tokens,
           norm_mix, w_in, w_pool, pool_scale, conv_w, w_out,
           norm_ffn, w_up, ffn_conv_w, ffn_conv_b, w_down, norm_final):
    f = lambda a: np.ascontiguousarray(np.asarray(a, dtype=np.float32))
    x_prompt, x_sample = f(x_prompt), f(x_sample)
    state_pool, state_conv, state_ffn = f(state_pool), f(state_conv), f(state_ffn)
    meta_tokens = f(meta_tokens)
    cst = np.zeros((128, NCST), np.float32)
    cst[:, C_G1:C_G1 + 8] = _col(norm_mix[0], 8)
    cst[:, C_G2:C_G2 + 8] = _col(norm_ffn[0], 8)
    cst[:, C_PS:C_PS + 4] = _col(pool_scale[0], 4)
    cw = f(conv_w)[0]
    cst[:, C_CW:C_CW + 12] = cw.reshape(3, 4, 128).transpose(2, 1, 0).reshape(128, 12)
    fw = f(ffn_conv_w)[0]
    cst[:, C_FW:C_FW + 132] = fw.reshape(3, 44, 128).transpose(2, 1, 0).reshape(128, 132)
    cst[:, C_FB:C_FB + 44] = _col(f(ffn_conv_b)[0], 44)
    gfin = np.ascontiguousarray(np.broadcast_to(f(norm_final)[None, :], (128, D)))
    shared = {
        "w_in": f(w_in)[0], "w_up": f(w_up)[0], "w_out": f(w_out)[0], "w_down": f(w_down)[0],
        "w_pool": f(w_pool)[0].reshape(512, 128), "cst": cst, "gfin": gfin,
    }
    in_maps = []
    for c in range(NCORES):
        sl = slice(c * NSEQ, (c + 1) * NSEQ)
        m = dict(shared)
        m["xp"] = x_prompt[c]
        m["xsm"] = np.ascontiguousarray(np.concatenate(
            [meta_tokens, np.zeros((NPAD, D), np.float32), x_sample[sl].reshape(NSEQ * DSEQ, D)], axis=0))
        m["st_pool"] = np.ascontiguousarray(
            state_pool[0, sl].reshape(NSEQ, 15, 4, 128).transpose(3, 2, 0, 1).reshape(128, -1))
        m["st_conv"] = np.ascontiguousarray(
            state_conv[0, sl].reshape(NSEQ, 2, 4, 128).transpose(3, 2, 0, 1).reshape(128, -1))
        m["st_ffn"] = np.ascontiguousarray(
            state_ffn[0, sl].reshape(NSEQ, 2, 44, 128).transpose(3, 2, 0, 1).reshape(128, -1))
        in_maps.append(m)
    if "nc" not in _NC_CACHE:
        _NC_CACHE["nc"] = build_nc()
    nc = _NC_CACHE["nc"]
    res = run_bass_kernel_spmd(nc, in_maps, core_ids=list(range(NCORES)))
    R = res.results
    if DEBUG:
        _NC_CACHE["dbg"] = R[0]
    y_prompt = np.stack([R[c]["yp"] for c in range(NCORES)], axis=0)
    y_sample = np.concatenate([R[c]["ys"].reshape(NSEQ, DSEQ, D) for c in range(NCORES)], axis=0)

    def unp(name, nch, rows):
        return np.stack([R[c][name].reshape(128, nch, rows).transpose(2, 1, 0).reshape(rows, nch * 128)
                         for c in range(NCORES)], axis=0)[None]

    def uns(name, nch, rows):
        return np.concatenate([R[c][name].reshape(128, nch, NSEQ, rows).transpose(2, 3, 1, 0).reshape(NSEQ, rows, nch * 128)
                               for c in range(NCORES)], axis=0)[None]
    outs = (y_prompt, y_sample,
            unp("o_pool_p", 4, 15), unp("o_conv_p", 4, 2), unp("o_ffn_p", 44, 2),
            uns("o_pool_s", 4, 15), uns("o_conv_s", 4, 2), uns("o_ffn_s", 44, 2))
    return tuple(np.ascontiguousarray(o.astype(np.float32)) for o in outs)
```
